# Optimizing a Trainium2 kernel written in Bass

```python
import math
import jax, jax.numpy as jnp
from jax import lax
import numpy as np

D_MODEL = 1024
BATCH = 4
SEQ = 8192
DEPTH = 4
DEC_BATCH = 32
DEC_SEQ = 16
PAST_LEN = 1024

CHUNK = 64
N_MIXERS = 2
N_DN_LAYERS = (DEPTH + N_MIXERS - 1) // N_MIXERS
N_MLA_LAYERS = DEPTH // N_MIXERS
PLE_DIM = 256
D_FF = 4 * D_MODEL
EPS = 1e-6
ALPHA = (2 * DEPTH) ** 0.25
BETA_INIT = (8 * DEPTH) ** -0.25
DN_HEADS = 8
DN_DK = 128
DN_DV = 128
DN_QK_DIM = DN_HEADS * DN_DK
DN_V_DIM = DN_HEADS * DN_DV
DN_CONV_DIM = 2 * DN_QK_DIM + DN_V_DIM
DN_IN_DIM = DN_CONV_DIM + DN_V_DIM + 2 * DN_HEADS
CONV_W = 4
MLA_HEADS = 8
Q_LORA = 512
KV_LORA = 256
NOPE_DIM = 128
ROPE_DIM = 64
V_DIM = 128
MLA_IN_DIM = Q_LORA + KV_LORA + ROPE_DIM
MLA_SCALE = (NOPE_DIM + ROPE_DIM) ** -0.5
ROPE_THETA = 10000.0
Q_BLOCK = 128

kernel_name = 'hybrid_gdn_mla_stream_step'


def rms_norm(x, g):
    xf = x.astype(jnp.float32)
    y = xf * lax.rsqrt(jnp.mean(xf * xf, axis=-1, keepdims=True) + EPS)
    return (y * g.astype(jnp.float32)).astype(x.dtype)


def layer_norm(x, g, b):
    xf = x.astype(jnp.float32)
    xc = xf - jnp.mean(xf, axis=-1, keepdims=True)
    var = jnp.mean(xc * xc, axis=-1, keepdims=True)
    return (xc * lax.rsqrt(var + EPS) * g.astype(jnp.float32) + b.astype(jnp.float32)).astype(x.dtype)


def rope(x, pos):
    half = ROPE_DIM // 2
    inv_freq = ROPE_THETA ** (-jnp.arange(half, dtype=jnp.float32) / half)
    ang = pos.astype(jnp.float32)[:, None] * inv_freq[None, :]
    cos = jnp.cos(ang)[None, :, None, :]
    sin = jnp.sin(ang)[None, :, None, :]
    xf = x.astype(jnp.float32)
    x1, x2 = xf[..., :half], xf[..., half:]
    return jnp.concatenate([x1 * cos - x2 * sin, x2 * cos + x1 * sin], axis=-1).astype(x.dtype)


def causal_dwconv(x_full, w):
    return lax.conv_general_dilated(
        x_full, w[:, None, :].astype(x_full.dtype), window_strides=(1,), padding='VALID',
        dimension_numbers=('NWC', 'WIO', 'NWC'), feature_group_count=x_full.shape[-1])


def chunk_gated_delta(q, k, v, g, beta, s0):
    bsz, seq, nh, _ = q.shape
    c = min(CHUNK, seq)
    n = seq // c

    def blocks(t):
        return jnp.moveaxis(t.reshape((bsz, n, c, nh) + t.shape[3:]), 3, 2)

    q, k, v, g, beta = blocks(q), blocks(k), blocks(v), blocks(g), blocks(beta)
    gc = jnp.cumsum(g, axis=-1)
    idx = jnp.arange(c)
    causal = idx[:, None] >= idx[None, :]
    strict = idx[:, None] > idx[None, :]
    decay = jnp.exp(jnp.where(causal, gc[..., :, None] - gc[..., None, :], -jnp.inf))
    kb = k * beta[..., None]
    a_low = jnp.where(strict, jnp.einsum('bnhid,bnhjd->bnhij', kb, k) * decay, 0.0)
    rhs = jnp.concatenate([v * beta[..., None], kb * jnp.exp(gc)[..., None]], axis=-1)
    uw = lax.linalg.triangular_solve(a_low + jnp.eye(c, dtype=jnp.float32), rhs,
                                     left_side=True, lower=True, unit_diagonal=True)
    u, w = uw[..., :DN_DV], uw[..., DN_DV:]
    attn = jnp.einsum('bnhid,bnhjd->bnhij', q, k) * decay
    qg = q * jnp.exp(gc)[..., None]
    g_last = gc[..., -1]
    kd = k * jnp.exp(g_last[..., None] - gc)[..., None]

    def step(s, xs):
        u_c, w_c, qg_c, kd_c, attn_c, gl_c = xs
        v_new = u_c - jnp.einsum('bhcd,bhde->bhce', w_c, s)
        o = jnp.einsum('bhcd,bhde->bhce', qg_c, s) + jnp.einsum('bhij,bhje->bhie', attn_c, v_new)
        s = s * jnp.exp(gl_c)[..., None, None] + jnp.einsum('bhcd,bhce->bhde', kd_c, v_new)
        return s, o

    xs = tuple(jnp.moveaxis(t, 1, 0) for t in (u, w, qg, kd, attn, g_last))
    s_fin, o = lax.scan(step, s0, xs)
    o = jnp.transpose(o, (1, 0, 3, 2, 4)).reshape(bsz, seq, nh, DN_DV)
    return o, s_fin


def gated_deltanet(x, conv_prev, s0, w_in, conv_w, a_log, dt_bias, o_norm, w_o):
    bsz, seq, _ = x.shape
    proj = x @ w_in
    qkv = proj[..., :DN_CONV_DIM]
    z = proj[..., DN_CONV_DIM:DN_CONV_DIM + DN_V_DIM]
    b = proj[..., DN_CONV_DIM + DN_V_DIM:DN_CONV_DIM + DN_V_DIM + DN_HEADS]
    a = proj[..., DN_CONV_DIM + DN_V_DIM + DN_HEADS:]
    qkv_full = jnp.concatenate([conv_prev.astype(x.dtype), qkv], axis=1)
    new_conv = qkv_full[:, -(CONV_W - 1):]
    qkv = jax.nn.silu(causal_dwconv(qkv_full, conv_w)).astype(jnp.float32)
    q = qkv[..., :DN_QK_DIM].reshape(bsz, seq, DN_HEADS, DN_DK)
    k = qkv[..., DN_QK_DIM:2 * DN_QK_DIM].reshape(bsz, seq, DN_HEADS, DN_DK)
    v = qkv[..., 2 * DN_QK_DIM:].reshape(bsz, seq, DN_HEADS, DN_DV)
    q = q * lax.rsqrt(jnp.sum(q * q, axis=-1, keepdims=True) + EPS) * (DN_DK ** -0.5)
    k = k * lax.rsqrt(jnp.sum(k * k, axis=-1, keepdims=True) + EPS)
    beta = jax.nn.sigmoid(b.astype(jnp.float32))
    g = -jnp.exp(a_log.astype(jnp.float32)) * jax.nn.softplus(a.astype(jnp.float32) + dt_bias.astype(jnp.float32))
    o, s_fin = chunk_gated_delta(q, k, v, g, beta, s0.astype(jnp.float32))
    o = o * lax.rsqrt(jnp.mean(o * o, axis=-1, keepdims=True) + EPS) * o_norm.astype(jnp.float32)
    o = o * jax.nn.silu(z.astype(jnp.float32)).reshape(bsz, seq, DN_HEADS, DN_DV)
    out = o.reshape(bsz, seq, DN_V_DIM).astype(x.dtype) @ w_o
    return out, new_conv, s_fin.astype(x.dtype)


def mla_project(x, pos, w_in, q_norm, w_uq, kv_norm):
    bsz, seq, _ = x.shape
    proj = x @ w_in
    c_q = rms_norm(proj[..., :Q_LORA], q_norm)
    c_kv = rms_norm(proj[..., Q_LORA:Q_LORA + KV_LORA], kv_norm)
    k_rope = rope(proj[..., Q_LORA + KV_LORA:][:, :, None, :], pos)[:, :, 0, :]
    q = (c_q @ w_uq).reshape(bsz, seq, MLA_HEADS, NOPE_DIM + ROPE_DIM)
    return q[..., :NOPE_DIM], rope(q[..., NOPE_DIM:], pos), c_kv, k_rope


def chunk_mask(q_pos, k_pos):
    return (k_pos[None, :] // CHUNK) <= (q_pos[:, None] // CHUNK)


def mla_prompt(x, w_in, q_norm, w_uq, kv_norm, w_uk, w_uv, w_o):
    bsz, seq, _ = x.shape
    pos = jnp.arange(seq)
    q_nope, q_rope, c_kv, k_rope = mla_project(x, pos, w_in, q_norm, w_uq, kv_norm)
    k_nope = jnp.einsum('bsc,chd->bshd', c_kv, w_uk)
    v = jnp.einsum('bsc,chd->bshd', c_kv, w_uv)
    qb = min(Q_BLOCK, seq)

    def attend_block(start):
        qn = lax.dynamic_slice_in_dim(q_nope, start, qb, axis=1)
        qr = lax.dynamic_slice_in_dim(q_rope, start, qb, axis=1)
        s = jnp.einsum('bqhd,bkhd->bhqk', qn, k_nope) + jnp.einsum('bqhd,bkd->bhqk', qr, k_rope)
        s = jnp.where(chunk_mask(start + jnp.arange(qb), pos), s.astype(jnp.float32) * MLA_SCALE, -jnp.inf)
        p = jax.nn.softmax(s, axis=-1).astype(v.dtype)
        return jnp.einsum('bhqk,bkhd->bqhd', p, v)

    o = lax.map(attend_block, jnp.arange(seq // qb) * qb)
    o = jnp.moveaxis(o, 0, 1).reshape(bsz, seq, MLA_HEADS * V_DIM)
    return o @ w_o, c_kv, k_rope


def mla_sample(x, ckv_cache, kr_cache, w_in, q_norm, w_uq, kv_norm, w_uk, w_uv, w_o):
    bsz, seq, _ = x.shape
    past = ckv_cache.shape[1]
    q_pos = past + jnp.arange(seq)
    q_nope, q_rope, c_kv, k_rope = mla_project(x, q_pos, w_in, q_norm, w_uq, kv_norm)
    ckv_all = jnp.concatenate([ckv_cache.astype(x.dtype), c_kv], axis=1)
    kr_all = jnp.concatenate([kr_cache.astype(x.dtype), k_rope], axis=1)
    q_lat = jnp.einsum('bqhd,chd->bqhc', q_nope, w_uk)
    s = jnp.einsum('bqhc,bkc->bhqk', q_lat, ckv_all) + jnp.einsum('bqhd,bkd->bhqk', q_rope, kr_all)
    s = jnp.where(chunk_mask(q_pos, jnp.arange(past + seq)), s.astype(jnp.float32) * MLA_SCALE, -jnp.inf)
    p = jax.nn.softmax(s, axis=-1).astype(x.dtype)
    o_lat = jnp.einsum('bhqk,bkc->bqhc', p, ckv_all)
    o = jnp.einsum('bqhc,chd->bqhd', o_lat, w_uv).reshape(bsz, seq, MLA_HEADS * V_DIM)
    return o @ w_o, c_kv, k_rope


def finish_layer(y, m, p, ln1_g, ln1_b, w_up, w_down, ln2_g, ln2_b, ple_proj, ple_norm, ple_gate):
    y = layer_norm(ALPHA * y + m, ln1_g, ln1_b)
    h = jnp.square(jax.nn.relu(y @ w_up))
    y = layer_norm(ALPHA * y + h @ w_down, ln2_g, ln2_b)
    e = rms_norm(p @ ple_proj, ple_norm)
    return y + jax.nn.sigmoid(y @ ple_gate) * e


def setup_inputs(seed: int = 0) -> dict:
    key = jax.random.key(seed)
    ks = iter(jax.random.split(key, 40))
    f32 = jnp.float32

    def nrm(shape, scale=1.0):
        return jax.random.normal(next(ks), shape, f32) * scale

    def gain(shape):
        return 1.0 + nrm(shape, 0.02)

    def unif(shape, lo, hi):
        return jax.random.uniform(next(ks), shape, f32, lo, hi)

    dt = jnp.exp(unif((N_DN_LAYERS, DN_HEADS), math.log(1e-3), math.log(1e-1)))
    return {
        'x_prompt': nrm((BATCH, SEQ, D_MODEL)),
        'x_sample': nrm((DEC_BATCH, DEC_SEQ, D_MODEL)),
        'state_dn_conv': nrm((N_DN_LAYERS, DEC_BATCH, CONV_W - 1, DN_CONV_DIM)),
        'state_dn_recurrent': nrm((N_DN_LAYERS, DEC_BATCH, DN_HEADS, DN_DK, DN_DV), 0.1),
        'cache_mla_ckv': nrm((N_MLA_LAYERS, DEC_BATCH, PAST_LEN, KV_LORA)),
        'cache_mla_krope': nrm((N_MLA_LAYERS, DEC_BATCH, PAST_LEN, ROPE_DIM)),
        'p_prompt': nrm((DEPTH, BATCH, SEQ, PLE_DIM)),
        'p_sample': nrm((DEPTH, DEC_BATCH, DEC_SEQ, PLE_DIM)),
        'ln1_g': gain((DEPTH, D_MODEL)),
        'ln1_b': nrm((DEPTH, D_MODEL), 0.02),
        'ln2_g': gain((DEPTH, D_MODEL)),
        'ln2_b': nrm((DEPTH, D_MODEL), 0.02),
        'mlp_w_up': nrm((DEPTH, D_MODEL, D_FF), D_MODEL ** -0.5),
        'mlp_w_down': nrm((DEPTH, D_FF, D_MODEL), D_FF ** -0.5 * BETA_INIT),
        'ple_w_proj': nrm((DEPTH, PLE_DIM, D_MODEL), PLE_DIM ** -0.5),
        'ple_norm': gain((DEPTH, D_MODEL)),
        'ple_w_gate': nrm((DEPTH, D_MODEL, D_MODEL), D_MODEL ** -0.5),
        'dn_w_in': nrm((N_DN_LAYERS, D_MODEL, DN_IN_DIM), D_MODEL ** -0.5),
        'dn_conv_w': nrm((N_DN_LAYERS, CONV_W, DN_CONV_DIM), CONV_W ** -0.5),
        'dn_a_log': jnp.log(unif((N_DN_LAYERS, DN_HEADS), 1.0, 16.0)),
        'dn_dt_bias': dt + jnp.log(-jnp.expm1(-dt)),
        'dn_o_norm': gain((N_DN_LAYERS, DN_DV)),
        'dn_w_o': nrm((N_DN_LAYERS, DN_V_DIM, D_MODEL), DN_V_DIM ** -0.5 * BETA_INIT),
        'mla_w_in': nrm((N_MLA_LAYERS, D_MODEL, MLA_IN_DIM), D_MODEL ** -0.5),
        'mla_q_norm': gain((N_MLA_LAYERS, Q_LORA)),
        'mla_w_uq': nrm((N_MLA_LAYERS, Q_LORA, MLA_HEADS * (NOPE_DIM + ROPE_DIM)), Q_LORA ** -0.5),
        'mla_kv_norm': gain((N_MLA_LAYERS, KV_LORA)),
        'mla_w_uk': nrm((N_MLA_LAYERS, KV_LORA, MLA_HEADS, NOPE_DIM), KV_LORA ** -0.5),
        'mla_w_uv': nrm((N_MLA_LAYERS, KV_LORA, MLA_HEADS, V_DIM), KV_LORA ** -0.5),
        'mla_w_o': nrm((N_MLA_LAYERS, MLA_HEADS * V_DIM, D_MODEL), (MLA_HEADS * V_DIM) ** -0.5 * BETA_INIT),
    }


def reference(x_prompt, x_sample, state_dn_conv, state_dn_recurrent, cache_mla_ckv, cache_mla_krope,
              p_prompt, p_sample, ln1_g, ln1_b, ln2_g, ln2_b, mlp_w_up, mlp_w_down,
              ple_w_proj, ple_norm, ple_w_gate, dn_w_in, dn_conv_w, dn_a_log, dn_dt_bias, dn_o_norm, dn_w_o,
              mla_w_in, mla_q_norm, mla_w_uq, mla_kv_norm, mla_w_uk, mla_w_uv, mla_w_o):
    yp, ys = x_prompt, x_sample
    bp = x_prompt.shape[0]
    p_conv, p_rec, p_ckv, p_kr = [], [], [], []
    s_conv, s_rec, s_ckv, s_kr = [], [], [], []
    for i in range(DEPTH):
        j = i // N_MIXERS
        if i % N_MIXERS == 0:
            zero_conv = jnp.zeros((bp, CONV_W - 1, DN_CONV_DIM), x_prompt.dtype)
            zero_state = jnp.zeros((bp, DN_HEADS, DN_DK, DN_DV), x_prompt.dtype)
            mp, c, s = gated_deltanet(yp, zero_conv, zero_state, dn_w_in[j], dn_conv_w[j], dn_a_log[j],
                                      dn_dt_bias[j], dn_o_norm[j], dn_w_o[j])
            p_conv.append(c)
            p_rec.append(s)
            ms, c, s = gated_deltanet(ys, state_dn_conv[j], state_dn_recurrent[j], dn_w_in[j], dn_conv_w[j],
                                      dn_a_log[j], dn_dt_bias[j], dn_o_norm[j], dn_w_o[j])
            s_conv.append(c)
            s_rec.append(s)
        else:
            mp, c, kr = mla_prompt(yp, mla_w_in[j], mla_q_norm[j], mla_w_uq[j], mla_kv_norm[j],
                                   mla_w_uk[j], mla_w_uv[j], mla_w_o[j])
            p_ckv.append(c)
            p_kr.append(kr)
            ms, c, kr = mla_sample(ys, cache_mla_ckv[j], cache_mla_krope[j], mla_w_in[j], mla_q_norm[j],
                                   mla_w_uq[j], mla_kv_norm[j], mla_w_uk[j], mla_w_uv[j], mla_w_o[j])
            s_ckv.append(c)
            s_kr.append(kr)
        yp = finish_layer(yp, mp, p_prompt[i], ln1_g[i], ln1_b[i], mlp_w_up[i], mlp_w_down[i],
                          ln2_g[i], ln2_b[i], ple_w_proj[i], ple_norm[i], ple_w_gate[i])
        ys = finish_layer(ys, ms, p_sample[i], ln1_g[i], ln1_b[i], mlp_w_up[i], mlp_w_down[i],
                          ln2_g[i], ln2_b[i], ple_w_proj[i], ple_norm[i], ple_w_gate[i])
    prompt_dn_conv = jnp.stack(p_conv)
    prompt_dn_recurrent = jnp.stack(p_rec)
    prompt_mla_ckv = jnp.stack(p_ckv)
    prompt_mla_krope = jnp.stack(p_kr)
    sample_dn_conv = jnp.stack(s_conv)
    sample_dn_recurrent = jnp.stack(s_rec)
    sample_mla_ckv = jnp.stack(s_ckv)
    sample_mla_krope = jnp.stack(s_kr)
    return (yp, ys, prompt_dn_conv, prompt_dn_recurrent, prompt_mla_ckv, prompt_mla_krope,
            sample_dn_conv, sample_dn_recurrent, sample_mla_ckv, sample_mla_krope)
```

```python
import numpy as np
from contextlib import ExitStack
import concourse.bass as bass
import concourse.mybir as mybir
from concourse.bass_utils import run_bass_kernel_spmd

F32 = mybir.dt.float32
BF16 = mybir.dt.bfloat16
AF = mybir.ActivationFunctionType
ALU = mybir.AluOpType

D = 1024
DFF = 4096
NL = 4
ALPHA = 8.0 ** 0.25
EPS = 1e-6
MLA_SCALE = 192.0 ** -0.5
NEG = -30000.0
SEQ_FULL = 8192
TN = 512
SN = 128
PAST = 1024


class Buf:
    __slots__ = ("lw", "rd")

    def __init__(self):
        self.lw = None
        self.rd = {}


class Prod:
    def __init__(self, name, sem, step):
        self.name = name
        self.sem = sem
        self.step = step
        self.n = 0


class Sched:
    def __init__(self, nc, stack):
        self.nc = nc
        self.stack = stack
        self.h = {"pe": nc.tensor, "act": nc.scalar, "dve": nc.vector, "pool": nc.gpsimd, "sp": nc.sync}
        self.eng = {}
        for k in self.h:
            self.eng[k] = Prod(k, stack.enter_context(nc.semaphore("s_" + k)), 1)
        self.waited = {k: {} for k in self.h}
        self.chan = {}
        self.ninstr = 0

    def channel(self, name):
        if name not in self.chan:
            self.chan[name] = Prod(name, self.stack.enter_context(self.nc.semaphore("c_" + name)), 16)
        return self.chan[name]

    def _need(self, e, reads, writes):
        need = {}
        me = self.eng.get(e)
        for b in reads:
            if b.lw is not None:
                p, i = b.lw
                if need.get(p, 0) < i:
                    need[p] = i
        for b in writes:
            if b.lw is not None:
                p, i = b.lw
                if p is not me and need.get(p, 0) < i:
                    need[p] = i
            for p, i in b.rd.items():
                if p is not me and need.get(p, 0) < i:
                    need[p] = i
        return need

    def _waits(self, e, need):
        w = self.waited[e]
        h = self.h[e]
        for p, i in need.items():
            if w.get(p, 0) < i:
                h.wait_ge(p.sem, i)
                w[p] = i
                self.ninstr += 1

    def op(self, e, fn, reads=(), writes=()):
        need = self._need(e, reads, writes)
        if e == "pe":
            need.pop(self.eng["pe"], None)
        self._waits(e, need)
        prod = self.eng[e]
        inst = fn(self.h[e])
        prod.n += 1
        inst.then_inc(prod.sem, 1)
        self.ninstr += 1
        for b in reads:
            b.rd[prod] = prod.n
        for b in writes:
            b.lw = (prod, prod.n)
            b.rd = {}

    def dma(self, q, chan, out, in_, reads=(), writes=(), **kw):
        c = self.channel(chan)
        need = self._need(q, reads, writes)
        if c.n > 0 and need.get(c, 0) < c.n:
            need[c] = c.n
        self._waits(q, need)
        inst = self.h[q].dma_start(out=out, in_=in_, **kw)
        c.n += 16
        inst.then_inc(c.sem, 16)
        self.ninstr += 1
        for b in reads:
            b.rd[c] = c.n
        for b in writes:
            b.lw = (c, c.n)
            b.rd = {}


class Tl:
    def __init__(self, ap, buf=None, excl=False):
        self.ap = ap
        self.b = buf if buf is not None else Buf()
        self.excl = excl

    def __getitem__(self, idx):
        return Vw(self, self.ap[idx])

    def v(self, ap):
        return Vw(self, ap)


class Vw:
    def __init__(self, tl, ap):
        self.tl = tl
        self.ap = ap

    def __getitem__(self, idx):
        return Vw(self.tl, self.ap[idx])

    def re(self, s, **kw):
        return Vw(self.tl, self.ap.rearrange(s, **kw))

    def bc(self, shape):
        return Vw(self.tl, self.ap.to_broadcast(shape))


def _rw(outs, ins):
    r, w = [], []
    for v in ins:
        if isinstance(v, Vw):
            (w if v.tl.excl else r).append(v.tl.b)
    for v in outs:
        w.append(v.tl.b)
    return r, w


def _a(x):
    return x.ap if isinstance(x, Vw) else x


class TileCtx:
    pass


class MK:
    def __init__(self, NT=16, NLAY=4, sample=True, solve_dt=F32):
        self.NT = NT
        self.NLAY = NLAY
        self.sample = sample
        self.SEQ = NT * TN
        self.SD = solve_dt
        self.nc = bass.Bass("TRN2", target_bir_lowering=False)
        self.st = ExitStack()
        self.S = Sched(self.nc, self.st)
        self.rr = 0
        self.ev = 0
        self.castn = 0
        self._decl()
        self._alloc()

    def din(self, name, shape, dt=F32):
        return self.nc.dram_tensor(name, list(shape), dt, kind="ExternalInput").ap()

    def dout(self, name, shape, dt=F32):
        return self.nc.dram_tensor(name, list(shape), dt, kind="ExternalOutput").ap()

    def dint(self, name, shape, dt=BF16):
        return self.nc.dram_tensor(name, list(shape), dt, kind="Internal").ap()

    def _decl(self):
        SEQ = self.SEQ
        i = self.din
        self.xT = i("xT", [D, SEQ])
        self.pT = i("pT", [NL, 256, SEQ])
        self.xsT = i("xsT", [D, SN])
        self.psT = i("psT", [NL, 256, SN])
        self.sconvT = i("sconvT", [2, 4, 3072, 3])
        self.srec = i("srec", [2, 4, 8, 128, 128])
        self.ckvT = i("ckvT", [2, 4, 256, PAST])
        self.krT = i("krT", [2, 4, 64, PAST])
        self.w = {
            "up": i("mlp_w_up", [NL, D, DFF]), "down": i("mlp_w_down", [NL, DFF, D]),
            "pproj": i("ple_w_proj", [NL, 256, D]), "gate": i("ple_w_gate", [NL, D, D]),
            "dn_in": i("dn_w_in", [2, D, 4112]), "dn_o": i("dn_w_o", [2, D, D]),
            "mla_in": i("mla_w_in", [2, D, 832]), "mla_uq": i("mla_w_uq", [2, 512, 1536]),
            "mla_uk": i("mla_w_uk", [2, 256, 1024]), "mla_uv": i("mla_w_uv", [2, 256, 1024]),
            "mla_o": i("mla_w_o", [2, D, D]),
        }
        self.wb = {k: self.dint("wb_" + k, v.shape) for k, v in self.w.items()}
        self.wbuf = {k: [] for k in self.w}
        self.vec_d = i("vec1024", [128, NL * 5 * 8])
        self.convw_d = i("convw", [128, 2 * 24 * 4])
        self.qn_d = i("qnorm", [128, 2 * 4])
        self.kvn_d = i("kvnorm", [128, 2 * 2])
        self.on_d = i("onorm", [128, 2])
        self.alog_d = i("alog", [8, 2])
        self.dtb_d = i("dtb", [8, 2])
        self.ident_d = i("ident", [128, 128])
        self.sel_d = i("sel", [8, 8 * 128])
        self.mb_d = i("maskbias", [128, 4 * 128])
        self.gm_d = i("gatemask", [8, TN + 2 * SN])
        self.tokm_d = i("tokmask", [128, 8])
        self.am_d = i("attnmask", [128, 4 * TN + 5 * 128])
        self.prot_d = i("prot", [64, 64])
        self.ropeC = i("ropeC", [64, SEQ + SN])
        self.ropeS = i("ropeS", [64, SEQ + SN])
        o = self.dout
        self.o_yT = o("o_yT", [D, SEQ])
        self.o_ysT = o("o_ysT", [D, SN])
        self.o_pconv = o("o_pconv", [2, 3072, 3])
        self.o_prec = o("o_prec", [2, 8, 128, 128])
        self.o_pckv = o("o_pckv", [2, 256, SEQ])
        self.o_pkr = o("o_pkr", [2, 64, SEQ])
        self.o_sconv = o("o_sconv", [2, 4, 3072, 3])
        self.o_srec = o("o_srec", [2, 4, 8, 128, 128])
        self.o_sckv = o("o_sckv", [2, 256, SN])
        self.o_skr = o("o_skr", [2, 64, SN])
        self.obufs = []
        KS = SEQ + 4 * PAST
        self.KS = KS
        self.Kscr = self.dint("Kscr", [2, 8, 128, KS])
        self.Vscr = self.dint("Vscr", [2, 128, KS // 128, 1024])
        self.KRscr = self.dint("KRscr", [2, 64, KS])
        self.kvbuf = {}

    def sb(self, name, shape, dt):
        return self.st.enter_context(self.nc.sbuf_tensor(name, list(shape), dt))

    def _alloc(self):
        nc = self.nc
        sb = self.sb
        yt = sb("Y", [128, 8, TN], F32)
        self._ytensor = yt
        self.Y = [Tl(yt[:, c]) for c in range(8)]
        ybt = sb("YB", [128, 8, TN], BF16)
        self.YB = [Tl(ybt[:, c]) for c in range(8)]
        NA = 22
        at = sb("A", [128, NA, TN], F32)
        self.A = [Tl(at[:, c]) for c in range(NA)]
        self.Awide = at
        ht = sb("H", [128, 32, TN], BF16)
        self.H = [Tl(ht[:, c]) for c in range(32)]
        self.Ht = ht
        NBX = 12
        bt = sb("BX", [128, NBX, TN], BF16)
        self.BX = [Tl(bt[:, c]) for c in range(NBX)]
        self.BXt = bt
        self.VT = [Tl(bt[:, 2 * kb:2 * kb + 2, :].rearrange("p a n -> p (a n)")) for kb in range(4)]
        NM = 24
        mt = sb("M", [128, NM, 128], self.SD)
        self.M = [Tl(mt[:, c]) for c in range(NM)]
        self.mi = 0
        xs = sb("XS", [128, 3, TN + 8], F32)
        self.XS = [Tl(xs[:, c]) for c in range(3)]
        self.NSLOT = 4
        rt = sb("RING", [128, self.NSLOT, 4096], BF16)
        self.ring = [Tl(rt[:, c]) for c in range(self.NSLOT)]
        hk = sb("HK", [128, 2, 1024], BF16)
        hv = sb("HV", [128, 2, 8, 128], BF16)
        hr = sb("HKR", [64, 2, 1024], BF16)
        self.HK = [Tl(hk[:, c]) for c in range(2)]
        self.HV = [Tl(hv[:, c]) for c in range(2)]
        self.HKR = [Tl(hr[:, c]) for c in range(2)]
        self.hslot = 0
        self.P = [Tl(self.st.enter_context(nc.psum_tensor(f"P{i}", [128, TN], F32))[:], excl=True) for i in range(8)]
        self.IDENT = Tl(sb("IDENT", [128, 128], F32)[:])
        self.IDENTS = Tl(sb("IDENTS", [128, 128], self.SD)[:])
        self.ONESB = Tl(sb("ONESB", [128, 128], BF16)[:])
        self.SEL = Tl(sb("SEL", [8, 8 * 128], F32)[:])
        self.MB = Tl(sb("MB", [128, 4 * 128], F32)[:])
        self.GM = Tl(sb("GM", [8, TN + 2 * SN], F32)[:])
        self.TOKM = Tl(sb("TOKM", [128, 8], F32)[:])
        self.AM = Tl(sb("AM", [128, 4 * TN + 5 * 128], BF16)[:])
        self.PROT = Tl(sb("PROT", [64, 64], F32)[:])
        self.COS = Tl(sb("COS", [64, TN], F32)[:])
        self.SIN = Tl(sb("SIN", [64, TN], F32)[:])
        self.VEC = Tl(sb("VEC", [128, NL * 5 * 8], F32)[:])
        self.CONVW = Tl(sb("CONVW", [128, 2 * 24 * 4], F32)[:])
        self.QN = Tl(sb("QN", [128, 8], F32)[:])
        self.KVN = Tl(sb("KVN", [128, 4], F32)[:])
        self.ON = Tl(sb("ON", [128, 2], F32)[:])
        self.ALOG = Tl(sb("ALOG", [8, 2], F32)[:])
        self.NEGA = Tl(sb("NEGA", [8, 2], F32)[:])
        self.DTB = Tl(sb("DTB", [8, 2], F32)[:])
        sst = sb("SST", [128, 2, 8, 128], F32)
        self.SST = [[Tl(sst[:, j, h]) for h in range(8)] for j in range(2)]
        self.SSTt = sst
        halo = sb("HALO", [128, 2, 24, 3], F32)
        self.HALO = [[Tl(halo[:, j, c]) for c in range(24)] for j in range(2)]
        self.HALOt = halo
        ts = sb("TS", [128, 4, 32], F32)
        self.TS = [Tl(ts[:, b]) for b in range(4)]
        self.DKS = Tl(sb("DKS", [128, 32], F32)[:])
        self.EGLB = Tl(sb("EGLB", [128, 16], F32)[:])
        s0 = sb("S0", [128, 4, 128], F32)
        self.S0 = [Tl(s0[:, c]) for c in range(4)]
        self.EGLT = Tl(sb("EGLT", [8, 16], F32)[:])
        s1 = sb("S1", [128, 2, 128], F32)
        self.S1 = [Tl(s1[:, c]) for c in range(2)]
        self.s0i = 0

    def mm(self, out, lhsT, rhs, start=True, stop=True, **kw):
        r, w = _rw([out], [lhsT, rhs])
        self.S.op("pe", lambda t: t.matmul(out.ap, lhsT.ap, rhs.ap, start=start, stop=stop, **kw), r, w)

    def tr(self, out, in_, ident):
        r, w = _rw([out], [in_, ident])
        self.S.op("pe", lambda t: t.transpose(out.ap, in_.ap, ident.ap), r, w)

    def act(self, out, in_, func, scale=1.0, bias=None):
        r, w = _rw([out], [in_, scale, bias])
        kw = {}
        if bias is not None:
            kw["bias"] = _a(bias)
        self.S.op("act", lambda a: a.activation(out=out.ap, in_=in_.ap, func=func, scale=_a(scale), **kw), r, w)

    def tt(self, e, out, in0, in1, op):
        r, w = _rw([out], [in0, in1])
        self.S.op(e, lambda v: v.tensor_tensor(out.ap, in0.ap, in1.ap, op), r, w)

    def ts(self, e, out, in0, s1, op0, s2=None, op1=None):
        r, w = _rw([out], [in0, s1, s2])
        if op1 is None:
            self.S.op(e, lambda v: v.tensor_scalar(out.ap, in0.ap, _a(s1), None, op0), r, w)
        else:
            self.S.op(e, lambda v: v.tensor_scalar(out.ap, in0.ap, _a(s1), _a(s2), op0, op1), r, w)

    def stt(self, out, in0, scalar, in1, op0, op1):
        r, w = _rw([out], [in0, scalar, in1])
        self.S.op("dve", lambda v: v.scalar_tensor_tensor(out.ap, in0.ap, _a(scalar), in1.ap, op0, op1), r, w)

    def cp(self, e, out, in_):
        r, w = _rw([out], [in_])
        if e == "act":
            self.S.op("act", lambda a: a.activation(out=out.ap, in_=in_.ap, func=AF.Copy), r, w)
        else:
            self.S.op(e, lambda v: v.tensor_copy(out.ap, in_.ap), r, w)

    def evac(self, out, in_):
        self.ev += 1
        self.cp("act" if self.ev % 2 else "dve", out, in_)

    def recip(self, out, in_):
        r, w = _rw([out], [in_])
        self.S.op("dve", lambda v: v.reciprocal(out.ap, in_.ap), r, w)

    def memset(self, e, out, val):
        r, w = _rw([out], [])
        self.S.op(e, lambda v: v.memset(out.ap, val), r, w)

    def dma(self, q, chan, out, in_, **kw):
        reads, writes = [], []
        if isinstance(in_, Vw):
            reads.append(in_.tl.b)
            ia = in_.ap
        else:
            ia, bl = in_
            reads += bl
        if isinstance(out, Vw):
            writes.append(out.tl.b)
            oa = out.ap
        else:
            oa, bl = out
            writes += bl
        self.S.dma(q, chan, oa, ia, reads=reads, writes=writes, **kw)

    def newM(self):
        m = self.M[self.mi % len(self.M)]
        self.mi += 1
        return m

    def R(self):
        i = self.rr
        self.rr += 1
        bank = self.P[6 + (i % 2)]
        c = ((i // 2) % 4) * 128
        return bank[:, c:c + 128]

    def cast_weights(self):
        order = ["dn_in", "dn_o", "up", "down", "pproj", "gate", "mla_in", "mla_uk", "mla_uv", "mla_uq", "mla_o"]
        for k in order:
            src = self.w[k]
            dst = self.wb[k]
            L = src.shape[0]
            R_ = src.shape[1]
            if k == "dn_in":
                for l in range(L):
                    for h in range(8):
                        b = Buf()
                        self.wbuf[k].append((l, b))
                        sv = src[l, :, 0:4096].rearrange("r (g h n) -> r g h n", g=4, h=8)[:, :, h, :]
                        dv = dst[l, :, h * 512:(h + 1) * 512].rearrange("r (g n) -> r g n", g=4)
                        self.S.dma("pool", f"cast{self.castn % 4}", dv, sv, writes=[b])
                        self.castn += 1
                    b = Buf()
                    self.wbuf[k].append((l, b))
                    self.S.dma("pool", f"cast{self.castn % 4}", dst[l, :, 4096:4112], src[l, :, 4096:4112], writes=[b])
                    self.castn += 1
                continue
            for l in range(L):
                nsplit = max(1, (R_ * src.shape[2]) // (1 << 20))
                rs = R_ // nsplit
                for s in range(nsplit):
                    b = Buf()
                    self.wbuf[k].append((l, b))
                    self.S.dma("pool", f"cast{self.castn % 4}", dst[l, s * rs:(s + 1) * rs, :], src[l, s * rs:(s + 1) * rs, :], writes=[b])
                    self.castn += 1

    def piece_src(self, key):
        k = key[0]
        l = key[1]
        wbv = self.wb[k][l]
        if k in ("up", "gate", "dn_o", "mla_o"):
            j = key[2]
            v = wbv.rearrange("(c p) n -> p c n", p=128)[:, :, j * 512:(j + 1) * 512]
            return v, [8, 512]
        if k == "down":
            m = key[2]
            v = wbv.rearrange("(c p) n -> p c n", p=128)[:, :, m * 128:(m + 1) * 128]
            return v, [32, 128]
        if k == "pproj":
            return wbv.rearrange("(c p) n -> p c n", p=128), [2, 1024]
        if k == "dn_in":
            h = key[2]
            if h == "ba":
                v = wbv.rearrange("(c p) n -> p c n", p=128)[:, :, 4096:4112]
                return v, [8, 16]
            v = wbv.rearrange("(c p) n -> p c n", p=128)[:, :, h * 512:(h + 1) * 512]
            return v, [8, 4, 128]
        if k == "mla_in":
            if key[2] == 0:
                return wbv.rearrange("(c p) n -> p c n", p=128)[:, :, 0:512], [8, 512]
            return wbv.rearrange("(c p) n -> p c n", p=128)[:, :, 512:832], [8, 320]
        if k == "mla_uq":
            h = key[2]
            return wbv.rearrange("(c p) n -> p c n", p=128)[:, :, h * 192:(h + 1) * 192], [4, 192]
        if k in ("mla_uk", "mla_uv"):
            return wbv.rearrange("(c p) n -> p c n", p=128), [2, 1024]
        raise KeyError(key)

    def plan_layer(self, l):
        j = l // 2
        pl = []
        if l % 2 == 0:
            pl.append(("dn_in", j, "ba"))
            pl += [("dn_in", j, h) for h in range(8)]
            pl += [("dn_o", j, 0), ("dn_o", j, 1)]
        else:
            pl += [("mla_in", j, 0), ("mla_in", j, 1), ("mla_uk", j), ("mla_uv", j)]
            pl += [("mla_uq", j, h) for h in range(8)]
            pl += [("mla_o", j, 0), ("mla_o", j, 1)]
        pl += [("up", l, jj) for jj in range(8)]
        pl += [("down", l, m) for m in range(8)]
        pl += [("pproj", l), ("gate", l, 0), ("gate", l, 1)]
        return pl

    def set_plan(self, plan):
        self.plan = plan
        self.pi = 0
        self.pl = 0

    def _load_piece(self, idx):
        key = self.plan[idx]
        src, shp = self.piece_src(key)
        slot = self.ring[idx % self.NSLOT]
        n = int(np.prod(shp))
        dst = slot[:, 0:n].re("p (c n) -> p c n", c=shp[0])
        bl = [b for (l, b) in self.wbuf[key[0]] if l == key[1]]
        self.dma("sp", f"w{idx % self.NSLOT}", dst, (src, bl))

    def wnext(self, key):
        assert self.plan[self.pi] == key, (self.plan[self.pi], key)
        while self.pl < len(self.plan) and self.pl < self.pi + self.NSLOT:
            self._load_piece(self.pl)
            self.pl += 1
        idx = self.pi
        self.pi += 1
        _, shp = self.piece_src(key)
        slot = self.ring[idx % self.NSLOT]
        n = int(np.prod(shp))
        v = slot[:, 0:n]
        if len(shp) == 2:
            return v.re("p (c n) -> p c n", c=shp[0])
        return v.re("p (c g n) -> p c g n", c=shp[0], g=shp[1])

    def setup(self):
        ld = lambda tl, d: self.dma("sp", "const", tl[:], (d, []))
        ld(self.IDENT, self.ident_d[:, :])
        ld(self.SEL, self.sel_d[:, :])
        ld(self.MB, self.mb_d[:, :])
        ld(self.GM, self.gm_d[:, :])
        ld(self.TOKM, self.tokm_d[:, :])
        AMW = 4 * TN + 5 * 128
        amf = self.Awide[:, 0:6, :].rearrange("p a n -> p (a n)")[:, 0:AMW]
        self.S.dma("sp", "const", amf, self.am_d[:, :], reads=[], writes=[self.A[k].b for k in range(6)])
        ld(self.PROT, self.prot_d[:, :])
        ld(self.VEC, self.vec_d[:, :])
        ld(self.CONVW, self.convw_d[:, :])
        ld(self.QN, self.qn_d[:, :])
        ld(self.KVN, self.kvn_d[:, :])
        ld(self.ON, self.on_d[:, :])
        ld(self.ALOG, self.alog_d[:, :])
        ld(self.DTB, self.dtb_d[:, :])
        self.S.op("dve", lambda v: v.tensor_copy(self.AM.ap[:, :], amf), reads=[self.A[k].b for k in range(6)], writes=[self.AM.b])
        self.cp("dve", self.IDENTS[:], self.IDENT[:])
        self.memset("dve", self.ONESB[:], 1.0)
        for g in range(3):
            self.memset("pool", self.XS[g][:, :], 0.0)
        self.act(self.NEGA[:], self.ALOG[:], AF.Exp)
        self.ts("dve", self.NEGA[:], self.NEGA[:], -1.0, ALU.mult)
        for j in range(2):
            for h in range(8):
                self.memset("dve", self.SST[j][h][:], 0.0)
            for cc in range(24):
                self.memset("pool", self.HALO[j][cc][:], 0.0)

    def vec(self, l, kind, c):
        i = (l * 5 + kind) * 8 + c
        return self.VEC[:, i:i + 1]

    def prompt_ctx(self, t):
        c = TileCtx()
        c.sample = False
        c.t = t
        c.N = TN
        c.pos0 = t * TN
        c.nblk = 4
        c.U = 128
        c.nunits = 4
        c.last = (t == self.NT - 1)
        c.L = 6
        return c

    def sample_ctx(self):
        c = TileCtx()
        c.sample = True
        c.t = self.NT
        c.N = SN
        c.pos0 = self.SEQ
        c.nblk = 1
        c.U = 32
        c.nunits = 4
        c.last = True
        c.L = 4
        return c

    def load_tile(self, c):
        N = c.N
        src = (self.xsT if c.sample else self.xT).rearrange("(c p) n -> p c n", p=128)
        src = src[:, :, 0:N] if c.sample else src[:, :, c.pos0:c.pos0 + N]
        self.S.dma("sp", "ldx", self._ytensor[:, :, 0:N], src, reads=[], writes=[y.b for y in self.Y])
        for k in range(8):
            self.cp("pool", self.YB[k][:, :N], self.Y[k][:, :N])
        self.dma("sp", "ldrope", self.COS[:, :N], (self.ropeC[:, c.pos0:c.pos0 + N], []))
        self.dma("sp", "ldrope", self.SIN[:, :N], (self.ropeS[:, c.pos0:c.pos0 + N], []))

    def bstats(self, c, srcs, want_mean, nfeat, PSs, PSq):
        N = c.N
        n = len(srcs)
        for k, s in enumerate(srcs):
            sq = self.H[k % 4][:, :N]
            self.act(sq, s, AF.Square)
            self.mm(PSq[:, :N], self.ONESB[:], sq, start=(k == 0), stop=(k == n - 1))
            if want_mean:
                rb = self.H[4 + k % 4][:, :N]
                self.cp("pool", rb, s)
                self.mm(PSs[:, :N], self.ONESB[:], rb, start=(k == 0), stop=(k == n - 1))
        A = self.A
        inv = 1.0 / nfeat
        if want_mean:
            self.act(A[0][:, :N], PSs[:, :N], AF.Copy, scale=inv)
            self.tt("dve", A[1][:, :N], A[0][:, :N], A[0][:, :N], ALU.mult)
            self.stt(A[2][:, :N], PSq[:, :N], inv, A[1][:, :N], ALU.mult, ALU.subtract)
            self.act(A[2][:, :N], A[2][:, :N], AF.Sqrt, bias=EPS)
        else:
            self.act(A[2][:, :N], PSq[:, :N], AF.Sqrt, scale=inv, bias=EPS)
        self.recip(A[3][:, :N], A[2][:, :N])
        return A[3]

    def layer_norm(self, c, l, kg, kb):
        N = c.N
        A = self.A
        rstd = self.bstats(c, [self.Y[k][:, :N] for k in range(8)], True, float(D), self.P[2], self.P[3])
        for k in range(8):
            t = A[4 + k % 2][:, :N]
            self.tt("dve", t, self.Y[k][:, :N], A[0][:, :N], ALU.subtract)
            self.tt("pool", t, t, rstd[:, :N], ALU.mult)
            self.act(self.Y[k][:, :N], t, AF.Identity, scale=self.vec(l, kg, k), bias=self.vec(l, kb, k))
            self.cp("pool", self.YB[k][:, :N], self.Y[k][:, :N])

    def residual(self, c, k, ps):
        N = c.N
        self.stt(self.Y[k][:, :N], self.Y[k][:, :N], ALPHA, ps, ALU.mult, ALU.add)

    def finish(self, c, l):
        N = c.N
        A = self.A
        self.layer_norm(c, l, 0, 1)
        pb = 0
        for jj in range(8):
            wu = self.wnext(("up", l, jj))
            for m in range(4):
                ff = jj * 4 + m
                ps = self.P[pb % 2]
                pb += 1
                for kc in range(8):
                    self.mm(ps[:, :N], wu[:, kc, m * 128:(m + 1) * 128], self.YB[kc][:, :N], start=(kc == 0), stop=(kc == 7))
                r = A[6 + ff % 2][:, :N]
                self.act(r, ps[:, :N], AF.Relu)
                self.tt("dve" if ff % 2 else "pool", self.H[ff][:, :N], r, r, ALU.mult)
        for m in range(8):
            wd = self.wnext(("down", l, m))
            ps = self.P[pb % 2]
            pb += 1
            for kc in range(32):
                self.mm(ps[:, :N], wd[:, kc, :], self.H[kc][:, :N], start=(kc == 0), stop=(kc == 31))
            self.residual(c, m, ps[:, :N])
        self.layer_norm(c, l, 2, 3)
        psrc = (self.psT if c.sample else self.pT)[l].rearrange("(c p) n -> p c n", p=128)
        psrc = psrc[:, :, 0:N] if c.sample else psrc[:, :, c.pos0:c.pos0 + N]
        PT = [A[18], A[19]]
        self.S.dma("sp", "ldp", self.Awide[:, 18:20, 0:N], psrc, reads=[], writes=[PT[0].b, PT[1].b])
        PB = [self.BX[8], self.BX[9]]
        for kc in range(2):
            self.cp("pool", PB[kc][:, :N], PT[kc][:, :N])
        wpj = self.wnext(("pproj", l))
        E = [A[8 + m] for m in range(8)]
        for m in range(8):
            ps = self.P[pb % 2]
            pb += 1
            for kc in range(2):
                self.mm(ps[:, :N], wpj[:, kc, m * 128:(m + 1) * 128], PB[kc][:, :N], start=(kc == 0), stop=(kc == 1))
            self.cp("act", E[m][:, :N], ps[:, :N])
        rstd = self.bstats(c, [E[m][:, :N] for m in range(8)], False, float(D), self.P[2], self.P[3])
        for jj in range(2):
            wg = self.wnext(("gate", l, jj))
            for m4 in range(4):
                m = jj * 4 + m4
                ps = self.P[pb % 2]
                pb += 1
                for kc in range(8):
                    self.mm(ps[:, :N], wg[:, kc, m4 * 128:(m4 + 1) * 128], self.YB[kc][:, :N], start=(kc == 0), stop=(kc == 7))
                gt = A[16 + m % 2][:, :N]
                self.act(gt, ps[:, :N], AF.Sigmoid)
                t = A[4 + m % 2][:, :N]
                self.tt("pool", t, E[m][:, :N], rstd[:, :N], ALU.mult)
                self.tt("dve", t, t, gt, ALU.mult)
                self.stt(self.Y[m][:, :N], t, self.vec(l, 4, m), self.Y[m][:, :N], ALU.mult, ALU.add)
        for m in range(8):
            self.cp("pool", self.YB[m][:, :N], self.Y[m][:, :N])

    def store(self, dram_ap, src, chan="st"):
        b = Buf()
        self.obufs.append((chan, b))
        self.dma("pool", chan, (dram_ap, [b]), src)

    def store_y(self, c):
        N = c.N
        dst = (self.o_ysT if c.sample else self.o_yT).rearrange("(c p) n -> p c n", p=128)
        dst = dst[:, :, 0:N] if c.sample else dst[:, :, c.pos0:c.pos0 + N]
        b = Buf()
        self.S.dma("pool", "sty", dst, self._ytensor[:, :, 0:N], reads=[y.b for y in self.Y], writes=[b])

    def build(self, mixers=True):
        self.mixers = mixers
        self.setup()
        self.cast_weights()
        tiles = [self.prompt_ctx(t) for t in range(self.NT)]
        if self.sample:
            tiles.append(self.sample_ctx())
        plan = []
        for c in tiles:
            for l in range(self.NLAY):
                pl = self.plan_layer(l)
                if not mixers:
                    pl = [k for k in pl if k[0] in ("up", "down", "pproj", "gate")]
                plan += pl
        if self.sample and mixers:
            pre = []
            for j in range(2):
                if 2 * j + 1 < self.NLAY:
                    pre += [("mla_uk", j), ("mla_uv", j)]
            plan = pre + plan
        self.set_plan(plan)
        if self.sample and mixers:
            self.sample_cache_kv()
        for c in tiles:
            self.load_tile(c)
            for l in range(self.NLAY):
                if mixers:
                    if l % 2 == 0:
                        self.dn_layer(c, l)
                    else:
                        self.mla_layer(c, l)
                else:
                    for k in range(8):
                        self.ts("dve", self.Y[k][:, :c.N], self.Y[k][:, :c.N], ALPHA, ALU.mult)
                self.finish(c, l)
            self.store_y(c)
        for name, ch in self.S.chan.items():
            if name.startswith("st"):
                self.nc.gpsimd.wait_ge(ch.sem, ch.n)
        self.st.close()
        return self.nc


def _consts(SEQ):
    c = {}
    c["ident"] = np.eye(128, dtype=np.float32)
    sel = np.zeros((8, 8, 128), np.float32)
    for h in range(8):
        sel[h, h, :] = 1.0
    c["sel"] = sel.reshape(8, 1024)
    j = np.arange(128)[:, None]
    i = np.arange(128)[None, :]
    p_incl = np.where(i >= j, 0.0, NEG)
    p_strict = np.where(i > j, 0.0, NEG)
    same = (i // 32) == (j // 32)
    jreal = (j % 32) >= 16
    s_incl = np.where(same & jreal & (i >= j), 0.0, NEG)
    s_strict = np.where(same & jreal & (i > j), 0.0, NEG)
    c["maskbias"] = np.concatenate([p_incl, p_strict, s_incl, s_strict], 1).astype(np.float32)
    tt = np.arange(TN)
    reset_p = np.where(tt % 128 == 0, 0.0, 1.0)
    reset_s = np.where(tt % 32 == 0, 0.0, 1.0)[:SN]
    real_s = np.where((tt % 32) >= 16, 1.0, 0.0)[:SN]
    gm = np.concatenate([reset_p, reset_s, real_s], 0).reshape(1, TN + 2 * SN)
    c["gatemask"] = np.repeat(gm, 8, 0).astype(np.float32)
    tok = np.arange(128)
    tokm = np.zeros((128, 8), np.float32)
    tokm[:, 0] = (tok % 32) >= 16
    for s in range(4):
        tokm[:, 1 + s] = ((tok // 32) == s) & ((tok % 32) >= 16)
    c["tokmask"] = tokm
    am = np.zeros((128, 4 * TN + 5 * 128), np.float32)
    q = np.arange(TN)[None, :]
    for kb in range(4):
        kp = kb * 128 + np.arange(128)[:, None]
        am[:, kb * TN:(kb + 1) * TN] = (kp // 64) <= (q // 64)
    qs = np.arange(128)[None, :]
    for s in range(4):
        am[:, 4 * TN + s * 128:4 * TN + (s + 1) * 128] = np.broadcast_to((qs // 32) == s, (128, 128))
    ks = np.arange(128)[:, None]
    am[:, 4 * TN + 4 * 128:] = ((ks // 32) == (qs // 32)) & ((ks % 32) >= 16)
    c["attnmask"] = am
    prot = np.zeros((64, 64), np.float32)
    for m in range(32):
        prot[m + 32, m] = -1.0
        prot[m, m + 32] = 1.0
    c["prot"] = prot
    half = 32
    inv_freq = (10000.0 ** (-np.arange(half, dtype=np.float32) / half)).astype(np.float32)
    pos = np.concatenate([np.arange(SEQ), np.tile(np.concatenate([np.zeros(16), PAST + np.arange(16)]), 4)]).astype(np.float32)
    ang = pos[None, :] * inv_freq[:, None]
    c["ropeC"] = np.concatenate([np.cos(ang), np.cos(ang)], 0).astype(np.float32)
    c["ropeS"] = np.concatenate([np.sin(ang), np.sin(ang)], 0).astype(np.float32)
    return c


def _fm(v, nchunk):
    return np.ascontiguousarray(np.moveaxis(v.reshape(v.shape[:-1] + (nchunk, 128)), -1, 0))


def host_inputs(inp, NT, core_seq, core_samp):
    SEQ = NT * TN
    f = lambda a: np.ascontiguousarray(np.asarray(a, dtype=np.float32))
    cst = _consts(SEQ)
    shared = dict(cst)
    for k in ("mlp_w_up", "mlp_w_down", "ple_w_proj", "ple_w_gate", "dn_w_in", "dn_w_o", "mla_w_in", "mla_w_uq", "mla_w_o"):
        shared[k] = f(inp[k])
    shared["mla_w_uk"] = f(inp["mla_w_uk"]).reshape(2, 256, 1024)
    shared["mla_w_uv"] = f(inp["mla_w_uv"]).reshape(2, 256, 1024)
    vec = np.stack([f(inp[k]) for k in ("ln1_g", "ln1_b", "ln2_g", "ln2_b", "ple_norm")], 1)
    shared["vec1024"] = _fm(vec, 8).reshape(128, NL * 5 * 8)
    shared["convw"] = np.ascontiguousarray(np.transpose(f(inp["dn_conv_w"]).reshape(2, 4, 24, 128), (3, 0, 2, 1))).reshape(128, 2 * 24 * 4)
    shared["qnorm"] = _fm(f(inp["mla_q_norm"]), 4).reshape(128, 8)
    shared["kvnorm"] = _fm(f(inp["mla_kv_norm"]), 2).reshape(128, 4)
    shared["onorm"] = np.ascontiguousarray(f(inp["dn_o_norm"]).T)
    shared["alog"] = np.ascontiguousarray(f(inp["dn_a_log"]).T)
    shared["dtb"] = np.ascontiguousarray(f(inp["dn_dt_bias"]).T)
    maps = []
    xp = f(inp["x_prompt"])
    pp = f(inp["p_prompt"])
    xs = f(inp["x_sample"])
    ps = f(inp["p_sample"])
    for c in range(len(core_seq)):
        m = dict(shared)
        b = core_seq[c]
        if b is None:
            m["xT"] = np.zeros((D, SEQ), np.float32)
            m["pT"] = np.zeros((NL, 256, SEQ), np.float32)
        else:
            m["xT"] = np.ascontiguousarray(xp[b, :SEQ].T)
            m["pT"] = np.ascontiguousarray(np.transpose(pp[:, b, :SEQ], (0, 2, 1)))
        ss = core_samp[c]
        xsT = np.zeros((D, 4, 32), np.float32)
        psT = np.zeros((NL, 256, 4, 32), np.float32)
        for i, s in enumerate(ss):
            xsT[:, i, 16:] = xs[s].T
            psT[:, :, i, 16:] = np.transpose(ps[:, s], (0, 2, 1))
        m["xsT"] = xsT.reshape(D, SN)
        m["psT"] = psT.reshape(NL, 256, SN)
        m["sconvT"] = np.ascontiguousarray(np.transpose(f(inp["state_dn_conv"])[:, ss], (0, 1, 3, 2)))
        m["srec"] = np.ascontiguousarray(f(inp["state_dn_recurrent"])[:, ss])
        m["ckvT"] = np.ascontiguousarray(np.transpose(f(inp["cache_mla_ckv"])[:, ss], (0, 1, 3, 2)))
        m["krT"] = np.ascontiguousarray(np.transpose(f(inp["cache_mla_krope"])[:, ss], (0, 1, 3, 2)))
        maps.append(m)
    return maps


def dn_layer(self, c, l):
    j = l // 2
    N = c.N
    A = self.A
    H = self.H
    nblk = c.nblk
    U = c.U
    nun = N // U
    samp = c.sample
    SD = self.SD
    mbI = self.MB[:, (2 if samp else 0) * 128:(3 if samp else 1) * 128]
    mbS = self.MB[:, (3 if samp else 1) * 128:(4 if samp else 2) * 128]
    reset = self.GM[:, TN:TN + SN] if samp else self.GM[:, 0:TN]
    realT = self.GM[:, TN + SN:TN + 2 * SN]
    wba = self.wnext(("dn_in", j, "ba"))
    pb_, pa_ = self.P[2], self.P[3]
    for kc in range(8):
        self.mm(pb_[0:8, :N], wba[:, kc, 0:8], self.YB[kc][:, :N], start=(kc == 0), stop=(kc == 7))
    for kc in range(8):
        self.mm(pa_[0:8, :N], wba[:, kc, 8:16], self.YB[kc][:, :N], start=(kc == 0), stop=(kc == 7))
    BETA, LNB, G, GC, GCB, EGC, DKD, NGC, NS1, TMP = [A[k][0:8, :N] for k in range(10)]
    self.act(BETA, pb_[0:8, :N], AF.Sigmoid)
    self.act(LNB, BETA, AF.Ln)
    self.act(G, pa_[0:8, :N], AF.Exp, bias=self.DTB[:, j:j + 1])
    self.act(G, G, AF.Ln, bias=1.0)
    self.ts("dve", G, G, self.NEGA[:, j:j + 1], ALU.mult)
    if samp:
        self.tt("dve", G, G, realT, ALU.mult)
    r, w = _rw([GC], [reset, G])
    self.S.op("dve", lambda v: v.tensor_tensor_scan(GC.ap, reset.ap, G.ap, 0.0, ALU.mult, ALU.add), r, w)
    self.tt("dve", GCB, GC, LNB, ALU.add)
    self.act(EGC, GC, AF.Exp)
    gc3 = GC.re("p (u k) -> p u k", k=U)
    gl = gc3[:, :, U - 1:U]
    self.tt("dve", TMP.re("p (u k) -> p u k", k=U), gl.bc([8, nun, U]), gc3, ALU.subtract)
    self.act(DKD, TMP, AF.Exp)
    if samp:
        self.tt("dve", DKD, DKD, realT, ALU.mult)
    EGL = self.EGLT[:, 0:nun]
    self.act(EGL.re("p (u k) -> p u k", k=1), gl, AF.Exp)
    self.ts("dve", NGC, GC, -1.0, ALU.mult)
    self.act(NS1, GCB, AF.Exp)
    self.ts("dve", NS1, NS1, -1.0, ALU.mult)
    for b in range(nblk):
        cb = slice(b * 128, (b + 1) * 128)
        ps = self.P[6 + b % 2]
        for q, src in enumerate((NGC, NS1, BETA, DKD)):
            self.tr(ps[:, q * 8:(q + 1) * 8], src[:, cb], self.IDENT[0:8, 0:8])
        self.cp("dve", self.TS[b][:, :], ps[:, 0:32])
    if samp:
        for s in range(4):
            self.ts("dve", self.DKS[:, s * 8:(s + 1) * 8], self.TS[0][:, 24:32], self.TOKM[:, 1 + s:2 + s], ALU.mult)
    pbi = 0
    for h in range(8):
        wp = self.wnext(("dn_in", j, h))
        QKV = [A[10], A[11], A[12]]
        for g in range(3):
            ps = self.P[pbi % 2]
            pbi += 1
            for kc in range(8):
                self.mm(ps[:, :N], wp[:, kc, g, :], self.YB[kc][:, :N], start=(kc == 0), stop=(kc == 7))
            xs = self.XS[g]
            self.cp("act", xs[:, 3:3 + N], ps[:, :N])
            ch = g * 8 + h
            if samp:
                for s in range(4):
                    self.dma("sp", "ldcs", xs[:, 3 + 32 * s + 13:3 + 32 * s + 16],
                             (self.sconvT[j, s, ch * 128:(ch + 1) * 128, :], []))
                    self.store(self.o_sconv[j, s, ch * 128:(ch + 1) * 128, :], xs[:, 3 + 32 * s + 29:3 + 32 * s + 32], "stc")
            else:
                self.cp("dve", xs[:, 0:3], self.HALO[j][ch][:, :])
                self.cp("dve", self.HALO[j][ch][:, :], xs[:, N:N + 3])
                if c.last:
                    self.store(self.o_pconv[j, ch * 128:(ch + 1) * 128, :], self.HALO[j][ch][:, :], "stc")
            cw = lambda i: self.CONVW[:, ((j * 24 + ch) * 4 + i):((j * 24 + ch) * 4 + i + 1)]
            acc = QKV[g][:, :N]
            self.ts("dve", acc, xs[:, 0:N], cw(0), ALU.mult)
            for i in (1, 2, 3):
                self.stt(acc, xs[:, i:i + N], cw(i), acc, ALU.mult, ALU.add)
            self.act(acc, acc, AF.Silu)
        QS, KS, VS = [t[:, :N] for t in QKV]
        QH = A[15][:, :N]
        KH = A[16][:, :N]
        for src, dst, sc, rt in ((QS, QH, 128.0 ** -0.5, A[13]), (KS, KH, 1.0, A[14])):
            sq = H[0][:, :N]
            self.act(sq, src, AF.Square)
            self.mm(self.P[2][:, :N], self.ONESB[:], sq)
            self.act(rt[:, :N], self.P[2][:, :N], AF.Sqrt, bias=EPS)
            self.recip(rt[:, :N], rt[:, :N])
            self.stt(dst, src, sc, rt[:, :N], ALU.mult, ALU.mult)
        sel = self.SEL[:, h * 128:(h + 1) * 128]
        DI = A[18][:, :N]
        DS = A[19][:, :N]
        QG = A[17][:, :N]
        self.mm(self.P[3][:, :N], sel, GC)
        for b in range(nblk):
            cb = slice(b * 128, (b + 1) * 128)
            self.tt("dve", DI[:, cb], self.P[3][:, cb], mbI, ALU.add)
        self.mm(self.P[4][:, :N], sel, GCB)
        for b in range(nblk):
            cb = slice(b * 128, (b + 1) * 128)
            self.tt("dve", DS[:, cb], self.P[4][:, cb], mbS, ALU.add)
        self.mm(self.P[3][:, :N], sel, EGC)
        self.tt("dve", QG, QH, self.P[3][:, :N], ALU.mult)
        self.mm(self.P[4][:, 0:nun], sel, EGL)
        self.cp("dve", self.EGLB[:, 0:nun], self.P[4][:, 0:nun])
        PO = self.P[5]
        for b in range(nblk):
            cb = slice(b * 128, (b + 1) * 128)
            tsb = self.TS[b]
            ngc = tsb[:, h:h + 1]
            ns1 = tsb[:, 8 + h:9 + h]
            bet = tsb[:, 16 + h:17 + h]
            dkd = tsb[:, 24 + h:25 + h]
            DTs = self.newM()
            DTi = self.newM()
            self.act(DTs[:, :], DS[:, cb], AF.Exp, bias=ngc)
            self.act(DTi[:, :], DI[:, cb], AF.Exp, bias=ngc)
            r0 = self.R()
            self.mm(r0, KH[:, cb], KH[:, cb])
            n0t = self.newM()
            self.stt(n0t[:, :], r0, -1.0, DTs[:, :], ALU.mult, ALU.mult)
            r1 = self.R()
            self.tr(r1, n0t[:, :], self.IDENTS[:])
            n0 = self.newM()
            self.cp("act", n0[:, :], r1)
            r2 = self.R()
            self.mm(r2, KH[:, cb], QH[:, cb])
            attnT = self.newM()
            self.tt("dve", attnT[:, :], r2, DTi[:, :], ALU.mult)
            TT = self.newM()
            self.tt("pool", TT[:, :], self.IDENTS[:], n0t[:, :], ALU.add)
            Nk, NkT = n0, n0t
            for k in range(1, c.L + 1):
                rk = self.R()
                self.mm(rk, NkT[:, :], Nk[:, :])
                Nk1 = self.newM()
                self.cp("act", Nk1[:, :], rk)
                if k < c.L:
                    rkt = self.R()
                    self.mm(rkt, Nk[:, :], NkT[:, :])
                    Nk1T = self.newM()
                    self.cp("dve", Nk1T[:, :], rkt)
                else:
                    Nk1T = None
                rp = self.R()
                self.mm(rp, Nk1[:, :], TT[:, :])
                TT2 = self.newM()
                self.tt("dve", TT2[:, :], rp, TT[:, :], ALU.add)
                TT = TT2
                Nk, NkT = Nk1, Nk1T
            r3 = self.R()
            self.tr(r3, VS[:, cb], self.IDENTS[:] if SD == F32 else self.IDENT[:])
            vb = self.newM()
            self.ts("dve", vb[:, :], r3, bet, ALU.mult)
            r4 = self.R()
            self.tr(r4, KH[:, cb], self.IDENTS[:])
            if samp:
                segs = [(s * 32, 32, s) for s in range(4)]
            else:
                segs = [(0, 128, 0)]
            kds = []
            for (p0, pl_, s) in segs:
                kd = self.newM()
                if samp:
                    self.ts("dve", kd[:, :], r4, self.DKS[:, s * 8 + h:s * 8 + h + 1], ALU.mult)
                else:
                    self.ts("dve", kd[:, :], r4, dkd, ALU.mult)
                kds.append(kd)
            Ss = []
            r5 = self.R()
            for (p0, pl_, s) in segs:
                if samp:
                    S_ = self.S0[s]
                    self.dma("sp", "lds0", S_[:, :], (self.srec[j, s, h], []))
                else:
                    S_ = self.SST[j][h]
                Ss.append(S_)
                kw = {} if pl_ == 128 else {"tile_position": (0, p0)}
                self.mm(r5[p0:p0 + pl_, :], KH[:, b * 128 + p0:b * 128 + p0 + pl_], S_[:, :], **kw)
            rr_ = self.newM()
            self.stt(rr_[:, :], r5, ns1, vb[:, :], ALU.mult, ALU.add)
            r6 = self.R()
            self.mm(r6, TT[:, :], rr_[:, :])
            vn = self.newM()
            self.cp("act", vn[:, :], r6)
            for si, (p0, pl_, s) in enumerate(segs):
                cs = slice(b * 128 + p0, b * 128 + p0 + pl_)
                self.mm(PO[:, cs], Ss[si][:, :], QG[:, cs], start=(si == 0), stop=False)
            self.mm(PO[:, cb], vn[:, :], attnT[:, :], start=False, stop=True)
            for si, (p0, pl_, s) in enumerate(segs):
                S_ = Ss[si]
                r7 = self.R()
                self.mm(r7, kds[si][:, :], vn[:, :])
                u = b * (128 // U) + (si if samp else 0)
                if samp:
                    So = self.S1[self.s0i % 2]
                    self.s0i += 1
                    self.stt(So[:, :], S_[:, :], self.EGLB[:, u:u + 1], r7, ALU.mult, ALU.add)
                    self.store(self.o_srec[j, s, h], So[:, :], "sts")
                else:
                    self.stt(S_[:, :], S_[:, :], self.EGLB[:, u:u + 1], r7, ALU.mult, ALU.add)
        sq = H[0][:, :N]
        self.act(sq, PO[:, :N], AF.Square)
        self.mm(self.P[2][:, :N], self.ONESB[:], sq)
        rn = A[21][:, :N]
        self.act(rn, self.P[2][:, :N], AF.Sqrt, scale=1.0 / 128.0, bias=EPS)
        self.recip(rn, rn)
        ps = self.P[pbi % 2]
        pbi += 1
        for kc in range(8):
            self.mm(ps[:, :N], wp[:, kc, 3, :], self.YB[kc][:, :N], start=(kc == 0), stop=(kc == 7))
        zs = A[20][:, :N]
        self.act(zs, ps[:, :N], AF.Silu)
        self.stt(rn, PO[:, :N], self.ON[:, j:j + 1], rn, ALU.mult, ALU.mult)
        self.tt("dve", H[8 + h][:, :N], rn, zs, ALU.mult)
        if (not samp) and c.last:
            self.store(self.o_prec[j, h], self.SST[j][h][:, :], "sts")
    self.out_proj(c, ("dn_o", j))


def out_proj(self, c, key):
    N = c.N
    pb = 0
    for half in range(2):
        wo = self.wnext(key + (half,))
        for m4 in range(4):
            m = half * 4 + m4
            ps = self.P[pb % 2]
            pb += 1
            for hh in range(8):
                self.mm(ps[:, :N], wo[:, hh, m4 * 128:(m4 + 1) * 128], self.H[8 + hh][:, :N], start=(hh == 0), stop=(hh == 7))
            self.residual(c, m, ps[:, :N])


MK.dn_layer = dn_layer
MK.out_proj = out_proj


def rope(self, c, src, dst, tmp):
    N = c.N
    ps = self.P[6]
    self.mm(ps[0:64, :N], self.PROT[:, :], src)
    self.tt("dve", tmp, src, self.COS[:, :N], ALU.mult)
    self.tt("dve", dst, ps[0:64, :N], self.SIN[:, :N], ALU.mult)
    self.tt("pool", dst, dst, tmp, ALU.add)


def make_k(self, wuk, CKB, N):
    for h in range(8):
        ps = self.P[h % 2]
        for kc in range(2):
            self.mm(ps[:, :N], wuk[:, kc, h * 128:(h + 1) * 128], CKB[kc][:, :N], start=(kc == 0), stop=(kc == 1))
        self.evac(self.H[19 + h][:, :N], ps[:, :N])


def make_v(self, wuv, CKB, nblk):
    for kb in range(nblk):
        for half in range(2):
            ps = self.P[half]
            for kc in range(2):
                self.mm(ps[:, :], CKB[kc][:, kb * 128:(kb + 1) * 128], wuv[:, kc, half * 512:(half + 1) * 512], start=(kc == 0), stop=(kc == 1))
            self.evac(self.VT[kb][:, half * 512:(half + 1) * 512], ps[:, :])


def spill_k(self, j, key0, N):
    b = Buf()
    dst = self.Kscr[j].rearrange("h p n -> p h n")[:, :, key0:key0 + N]
    self.S.dma("pool", "spk", dst, self.Ht[:, 19:27, 0:N], reads=[self.H[19 + h].b for h in range(8)], writes=[b])
    return b


def spill_v(self, j, key0, nblk):
    b = Buf()
    dst = self.Vscr[j][:, key0 // 128:key0 // 128 + nblk, :]
    src = self.BXt[:, 0:2 * nblk, :].rearrange("p (b a) n -> p b (a n)", a=2)
    self.S.dma("pool", "spv", dst, src, reads=[self.VT[kb].b for kb in range(nblk)], writes=[b])
    return b


def sample_cache_kv(self):
    A = self.A
    H = self.H
    for j in range(2):
        if 2 * j + 1 >= self.NLAY:
            continue
        bl = []
        grp = [(s, g) for s in range(4) for g in range(2)]
        for gi, (s, g) in enumerate(grp):
            src = self.ckvT[j, s].rearrange("(c p) n -> p c n", p=128)[:, :, g * 512:(g + 1) * 512]
            self.S.dma("sp", "ldck", self.Awide[:, 12:14, :], src, reads=[], writes=[A[12].b, A[13].b])
            for kc in range(2):
                self.cp("pool", H[2 * gi + kc][:, :], A[12 + kc][:, :])
        for s in range(4):
            b = Buf()
            self.S.dma("pool", "spr", self.KRscr[j][:, self.SEQ + s * PAST:self.SEQ + (s + 1) * PAST], self.krT[j, s], reads=[], writes=[b])
            bl.append(b)
        wuk = self.wnext(("mla_uk", j))
        for gi, (s, g) in enumerate(grp):
            self.make_k(wuk, [H[2 * gi], H[2 * gi + 1]], TN)
            bl.append(self.spill_k(j, self.SEQ + s * PAST + g * 512, TN))
        wuv = self.wnext(("mla_uv", j))
        for gi, (s, g) in enumerate(grp):
            self.make_v(wuv, [H[2 * gi], H[2 * gi + 1]], 4)
            bl.append(self.spill_v(j, self.SEQ + s * PAST + g * 512, 4))
        self.kvbuf[(j, "cache")] = bl


def mla_layer(self, c, l):
    j = l // 2
    N = c.N
    A = self.A
    H = self.H
    samp = c.sample
    nblk = c.nblk
    pb = 0
    wi0 = self.wnext(("mla_in", j, 0))
    CQ = [A[8 + m] for m in range(4)]
    for m in range(4):
        ps = self.P[pb % 2]
        pb += 1
        for kc in range(8):
            self.mm(ps[:, :N], wi0[:, kc, m * 128:(m + 1) * 128], self.YB[kc][:, :N], start=(kc == 0), stop=(kc == 7))
        self.evac(CQ[m][:, :N], ps[:, :N])
    rstd = self.bstats(c, [CQ[m][:, :N] for m in range(4)], False, 512.0, self.P[2], self.P[3])
    for m in range(4):
        self.stt(H[4 + m][:, :N], CQ[m][:, :N], self.QN[:, j * 4 + m:j * 4 + m + 1], rstd[:, :N], ALU.mult, ALU.mult)
    wi1 = self.wnext(("mla_in", j, 1))
    CKV = [A[12], A[13]]
    CKVN = [A[14], A[15]]
    for m in range(2):
        ps = self.P[pb % 2]
        pb += 1
        for kc in range(8):
            self.mm(ps[:, :N], wi1[:, kc, m * 128:(m + 1) * 128], self.YB[kc][:, :N], start=(kc == 0), stop=(kc == 7))
        self.evac(CKV[m][:, :N], ps[:, :N])
    ps = self.P[pb % 2]
    pb += 1
    for kc in range(8):
        self.mm(ps[0:64, :N], wi1[:, kc, 256:320], self.YB[kc][:, :N], start=(kc == 0), stop=(kc == 7))
    KR = A[16][0:64, :N]
    self.evac(KR, ps[0:64, :N])
    rstd = self.bstats(c, [CKV[m][:, :N] for m in range(2)], False, 256.0, self.P[2], self.P[3])
    CKB = [H[16], H[17]]
    for m in range(2):
        self.stt(CKVN[m][:, :N], CKV[m][:, :N], self.KVN[:, j * 2 + m:j * 2 + m + 1], rstd[:, :N], ALU.mult, ALU.mult)
        if samp:
            self.store(self.o_sckv[j, m * 128:(m + 1) * 128, 0:N], CKVN[m][:, :N], "stk")
        else:
            self.store(self.o_pckv[j, m * 128:(m + 1) * 128, c.pos0:c.pos0 + N], CKVN[m][:, :N], "stk")
        self.cp("pool", CKB[m][:, :N], CKVN[m][:, :N])
    KRR = A[17][0:64, :N]
    self.rope(c, KR, KRR, A[18][0:64, :N])
    if samp:
        self.store(self.o_skr[j, :, 0:N], KRR, "stk")
    else:
        self.store(self.o_pkr[j, :, c.pos0:c.pos0 + N], KRR, "stk")
    KRB = H[18][0:64, :N]
    self.cp("pool", KRB, KRR)
    wuk = self.wnext(("mla_uk", j))
    self.make_k(wuk, CKB, N)
    wuv = self.wnext(("mla_uv", j))
    self.make_v(wuv, CKB, nblk)
    if not samp:
        bl = [self.spill_k(j, c.pos0, N), self.spill_v(j, c.pos0, nblk)]
        b = Buf()
        self.dma("pool", "spr", (self.KRscr[j][:, c.pos0:c.pos0 + N], [b]), KRB)
        bl.append(b)
        self.kvbuf[(j, c.t)] = bl
    srcs = []
    if samp:
        for s in range(4):
            msk = self.AM[:, 4 * TN + s * 128:4 * TN + (s + 1) * 128]
            srcs.append((self.SEQ + s * PAST, PAST, self.kvbuf[(j, "cache")], msk))
    else:
        t0 = 0
        while t0 < c.t:
            nt = min(2, c.t - t0)
            bl = []
            for tt_ in range(t0, t0 + nt):
                bl += self.kvbuf[(j, tt_)]
            srcs.append((t0 * TN, nt * TN, bl, None))
            t0 += nt
    nkb_total = sum(n // 128 for (_, n, _, _) in srcs) + nblk
    PO = self.P[5]
    PD = self.P[4]
    for h in range(8):
        wq = self.wnext(("mla_uq", j, h))
        ps = self.P[pb % 2]
        pb += 1
        for kc in range(4):
            self.mm(ps[:, :N], wq[:, kc, 0:128], H[4 + kc][:, :N], start=(kc == 0), stop=(kc == 3))
        QT = H[27][:, :N]
        self.evac(QT, ps[:, :N])
        ps = self.P[pb % 2]
        pb += 1
        for kc in range(4):
            self.mm(ps[0:64, :N], wq[:, kc, 128:192], H[4 + kc][:, :N], start=(kc == 0), stop=(kc == 3))
        QR = A[19][0:64, :N]
        self.evac(QR, ps[0:64, :N])
        QRR = A[20][0:64, :N]
        self.rope(c, QR, QRR, A[18][0:64, :N])
        QRB = H[28][0:64, :N]
        self.cp("pool", QRB, QRR)
        n = 0

        def block(Kv, KRv, Vv, mask, c0):
            nonlocal n
            ps_ = self.P[2 + n % 2]
            self.mm(ps_[:, c0:N], Kv, QT[:, c0:N], start=True, stop=False)
            self.mm(ps_[:, c0:N], KRv, QRB[:, c0:N], start=False, stop=True)
            PT = H[29 + n % 3][:, c0:N]
            self.act(PT, ps_[:, c0:N], AF.Exp, scale=MLA_SCALE)
            if mask is not None:
                self.tt("dve", PT, PT, mask, ALU.mult)
            first = (n == 0)
            last = (n == nkb_total - 1)
            self.mm(PO[:, c0:N], Vv, PT, start=first, stop=last)
            self.mm(PD[:, c0:N], self.ONESB[:], PT, start=first, stop=last)
            n += 1

        for (key0, nk, bl, msk) in srcs:
            sl = self.hslot % 2
            self.hslot += 1
            nb = nk // 128
            self.dma("sp", f"hk{sl}", self.HK[sl][:, 0:nk], (self.Kscr[j, h][:, key0:key0 + nk], bl))
            self.dma("sp", f"hv{sl}", self.HV[sl][:, 0:nb, :], (self.Vscr[j][:, key0 // 128:key0 // 128 + nb, h * 128:(h + 1) * 128], bl))
            self.dma("sp", f"hr{sl}", self.HKR[sl][:, 0:nk], (self.KRscr[j][:, key0:key0 + nk], bl))
            for kb in range(nb):
                block(self.HK[sl][:, kb * 128:(kb + 1) * 128], self.HKR[sl][:, kb * 128:(kb + 1) * 128],
                      self.HV[sl][:, kb, :], msk, 0)
        for kb in range(nblk):
            if samp:
                msk = self.AM[:, 4 * TN + 4 * 128:4 * TN + 5 * 128]
                c0 = 0
            else:
                c0 = kb * 128
                msk = self.AM[:, kb * TN + c0:(kb + 1) * TN]
            block(H[19 + h][:, kb * 128:(kb + 1) * 128], H[18][0:64, kb * 128:(kb + 1) * 128],
                  self.VT[kb][:, h * 128:(h + 1) * 128], msk, c0)
        rden = A[21][:, :N]
        self.recip(rden, PD[:, :N])
        self.tt("dve", H[8 + h][:, :N], PO[:, :N], rden, ALU.mult)
    self.out_proj(c, ("mla_o", j))


MK.mla_layer = mla_layer
MK.sample_cache_kv = sample_cache_kv
MK.rope = rope
MK.make_k = make_k
MK.make_v = make_v
MK.spill_k = spill_k
MK.spill_v = spill_v


PROMPT_CORES = [0, 1, 4, 5]
_NC_CACHE = {}


def kernel(**inputs):
    NT = SEQ_FULL // TN
    if "nc" not in _NC_CACHE:
        mk = MK(NT=NT, NLAY=4, sample=True)
        _NC_CACHE["nc"] = mk.build(mixers=True)
    nc = _NC_CACHE["nc"]
    core_seq = [None] * 8
    for b, cidx in enumerate(PROMPT_CORES):
        core_seq[cidx] = b
    core_samp = [[4 * c + i for i in range(4)] for c in range(8)]
    maps = host_inputs(inputs, NT, core_seq, core_samp)
    res = run_bass_kernel_spmd(nc, maps, core_ids=list(range(8)))
    R = res.results
    f32 = np.float32
    y_prompt = np.empty((4, SEQ_FULL, D), f32)
    y_sample = np.empty((32, 16, D), f32)
    p_conv = np.empty((2, 4, 3, 3072), f32)
    p_rec = np.empty((2, 4, 8, 128, 128), f32)
    p_ckv = np.empty((2, 4, SEQ_FULL, 256), f32)
    p_kr = np.empty((2, 4, SEQ_FULL, 64), f32)
    s_conv = np.empty((2, 32, 3, 3072), f32)
    s_rec = np.empty((2, 32, 8, 128, 128), f32)
    s_ckv = np.empty((2, 32, 16, 256), f32)
    s_kr = np.empty((2, 32, 16, 64), f32)
    for b, cidx in enumerate(PROMPT_CORES):
        r = R[cidx]
        y_prompt[b] = r["o_yT"].T
        p_conv[:, b] = np.transpose(r["o_pconv"], (0, 2, 1))
        p_rec[:, b] = r["o_prec"]
        p_ckv[:, b] = np.transpose(r["o_pckv"], (0, 2, 1))
        p_kr[:, b] = np.transpose(r["o_pkr"], (0, 2, 1))
    for c in range(8):
        r = R[c]
        sl = slice(4 * c, 4 * c + 4)
        y_sample[sl] = np.transpose(r["o_ysT"].reshape(D, 4, 32)[:, :, 16:], (1, 2, 0))
        s_conv[:, sl] = np.transpose(r["o_sconv"], (0, 1, 3, 2))
        s_rec[:, sl] = r["o_srec"]
        s_ckv[:, sl] = np.transpose(r["o_sckv"].reshape(2, 256, 4, 32)[..., 16:], (0, 2, 3, 1))
        s_kr[:, sl] = np.transpose(r["o_skr"].reshape(2, 64, 4, 32)[..., 16:], (0, 2, 3, 1))
    return (y_prompt, y_sample, p_conv, p_rec, p_ckv, p_kr, s_conv, s_rec, s_ckv, s_kr)
```

```python
import numpy as np
from contextlib import ExitStack
import concourse.bass as bass
import concourse.mybir as mybir
from concourse.bass_utils import run_bass_kernel_spmd

F32 = mybir.dt.float32
BF16 = mybir.dt.bfloat16
AF = mybir.ActivationFunctionType
ALU = mybir.AluOpType

D = 1024
DFF = 4096
NL = 4
ALPHA = 8.0 ** 0.25
EPS = 1e-6
MLA_SCALE = 192.0 ** -0.5
NEG = -30000.0
SEQ_FULL = 8192
TN = 512
SN = 128
PAST = 1024


class Buf:
    __slots__ = ("lw", "rd")

    def __init__(self):
        self.lw = None
        self.rd = {}


class Prod:
    def __init__(self, name, sem, step):
        self.name = name
        self.sem = sem
        self.step = step
        self.n = 0


class Sched:
    def __init__(self, nc, stack):
        self.nc = nc
        self.stack = stack
        self.h = {"pe": nc.tensor, "act": nc.scalar, "dve": nc.vector, "pool": nc.gpsimd, "sp": nc.sync}
        self.eng = {}
        for k in self.h:
            self.eng[k] = Prod(k, stack.enter_context(nc.semaphore("s_" + k)), 1)
        self.waited = {k: {} for k in self.h}
        self.chan = {}
        self.ninstr = 0

    def channel(self, name):
        if name not in self.chan:
            self.chan[name] = Prod(name, self.stack.enter_context(self.nc.semaphore("c_" + name)), 16)
        return self.chan[name]

    def _need(self, e, reads, writes):
        need = {}
        me = self.eng.get(e)
        for b in reads:
            if b.lw is not None:
                p, i = b.lw
                if need.get(p, 0) < i:
                    need[p] = i
        for b in writes:
            if b.lw is not None:
                p, i = b.lw
                if p is not me and need.get(p, 0) < i:
                    need[p] = i
            for p, i in b.rd.items():
                if p is not me and need.get(p, 0) < i:
                    need[p] = i
        return need

    def _waits(self, e, need):
        w = self.waited[e]
        h = self.h[e]
        for p, i in need.items():
            if w.get(p, 0) < i:
                h.wait_ge(p.sem, i)
                w[p] = i
                self.ninstr += 1

    def op(self, e, fn, reads=(), writes=()):
        need = self._need(e, reads, writes)
        if e == "pe":
            need.pop(self.eng["pe"], None)
        self._waits(e, need)
        prod = self.eng[e]
        inst = fn(self.h[e])
        prod.n += 1
        inst.then_inc(prod.sem, 1)
        self.ninstr += 1
        for b in reads:
            b.rd[prod] = prod.n
        for b in writes:
            b.lw = (prod, prod.n)
            b.rd = {}

    def dma(self, q, chan, out, in_, reads=(), writes=(), **kw):
        c = self.channel(chan)
        need = self._need(q, reads, writes)
        if c.n > 0 and need.get(c, 0) < c.n:
            need[c] = c.n
        self._waits(q, need)
        inst = self.h[q].dma_start(out=out, in_=in_, **kw)
        c.n += 16
        inst.then_inc(c.sem, 16)
        self.ninstr += 1
        for b in reads:
            b.rd[c] = c.n
        for b in writes:
            b.lw = (c, c.n)
            b.rd = {}


class Tl:
    def __init__(self, ap, buf=None, excl=False):
        self.ap = ap
        self.b = buf if buf is not None else Buf()
        self.excl = excl

    def __getitem__(self, idx):
        return Vw(self, self.ap[idx])

    def v(self, ap):
        return Vw(self, ap)


class Vw:
    def __init__(self, tl, ap):
        self.tl = tl
        self.ap = ap

    def __getitem__(self, idx):
        return Vw(self.tl, self.ap[idx])

    def re(self, s, **kw):
        return Vw(self.tl, self.ap.rearrange(s, **kw))

    def bc(self, shape):
        return Vw(self.tl, self.ap.to_broadcast(shape))


def _rw(outs, ins):
    r, w = [], []
    for v in ins:
        if isinstance(v, Vw):
            (w if v.tl.excl else r).append(v.tl.b)
    for v in outs:
        w.append(v.tl.b)
    return r, w


def _a(x):
    return x.ap if isinstance(x, Vw) else x


class TileCtx:
    pass


class MK:
    def __init__(self, NT=16, NLAY=4, sample=True, solve_dt=F32):
        self.NT = NT
        self.NLAY = NLAY
        self.sample = sample
        self.SEQ = NT * TN
        self.SD = solve_dt
        self.nc = bass.Bass("TRN2", target_bir_lowering=False)
        self.st = ExitStack()
        self.S = Sched(self.nc, self.st)
        self.rr = 0
        self.ev = 0
        self.castn = 0
        self._decl()
        self._alloc()

    def din(self, name, shape, dt=F32):
        return self.nc.dram_tensor(name, list(shape), dt, kind="ExternalInput").ap()

    def dout(self, name, shape, dt=F32):
        return self.nc.dram_tensor(name, list(shape), dt, kind="ExternalOutput").ap()

    def dint(self, name, shape, dt=BF16):
        return self.nc.dram_tensor(name, list(shape), dt, kind="Internal").ap()

    def _decl(self):
        SEQ = self.SEQ
        i = self.din
        self.xT = i("xT", [D, SEQ])
        self.pT = i("pT", [NL, 256, SEQ])
        self.xsT = i("xsT", [D, SN])
        self.psT = i("psT", [NL, 256, SN])
        self.sconvT = i("sconvT", [2, 4, 3072, 3])
        self.srec = i("srec", [2, 4, 8, 128, 128])
        self.ckvT = i("ckvT", [2, 4, 256, PAST])
        self.krT = i("krT", [2, 4, 64, PAST])
        self.w = {
            "up": i("mlp_w_up", [NL, D, DFF]), "down": i("mlp_w_down", [NL, DFF, D]),
            "pproj": i("ple_w_proj", [NL, 256, D]), "gate": i("ple_w_gate", [NL, D, D]),
            "dn_in": i("dn_w_in", [2, D, 4112]), "dn_o": i("dn_w_o", [2, D, D]),
            "mla_in": i("mla_w_in", [2, D, 832]), "mla_uq": i("mla_w_uq", [2, 512, 1536]),
            "mla_uk": i("mla_w_uk", [2, 256, 1024]), "mla_uv": i("mla_w_uv", [2, 256, 1024]),
            "mla_o": i("mla_w_o", [2, D, D]),
        }
        self.wb = {k: self.dint("wb_" + k, v.shape) for k, v in self.w.items()}
        self.wbuf = {k: [] for k in self.w}
        self.vec_d = i("vec1024", [128, NL * 5 * 8])
        self.convw_d = i("convw", [128, 2 * 24 * 4])
        self.qn_d = i("qnorm", [128, 2 * 4])
        self.kvn_d = i("kvnorm", [128, 2 * 2])
        self.on_d = i("onorm", [128, 2])
        self.alog_d = i("alog", [8, 2])
        self.dtb_d = i("dtb", [8, 2])
        self.ident_d = i("ident", [128, 128])
        self.oh8_d = i("oh8", [8, 8])
        self.mb_d = i("maskbias", [128, 4 * 128])
        self.gm_d = i("gatemask", [8, TN + 2 * SN])
        self.tokm_d = i("tokmask", [128, 8])
        self.am_d = i("attnmask", [128, 4 * TN + 18 * 128])
        self.prot_d = i("prot", [64, 64])
        self.ropeC = i("ropeC", [64, SEQ + SN])
        self.ropeS = i("ropeS", [64, SEQ + SN])
        o = self.dout
        self.o_yT = o("o_yT", [D, SEQ])
        self.o_ysT = o("o_ysT", [D, SN])
        self.o_pconv = o("o_pconv", [2, 3072, 3])
        self.o_prec = o("o_prec", [2, 8, 128, 128])
        self.o_pckv = o("o_pckv", [2, 256, SEQ])
        self.o_pkr = o("o_pkr", [2, 64, SEQ])
        self.o_sconv = o("o_sconv", [2, 4, 3072, 3])
        self.o_srec = o("o_srec", [2, 4, 8, 128, 128])
        self.o_sckv = o("o_sckv", [2, 256, SN])
        self.o_skr = o("o_skr", [2, 64, SN])
        self.obufs = []
        KS = SEQ + 4 * PAST
        self.KS = KS
        self.Kscr = self.dint("Kscr", [2, 8, 128, KS])
        self.Vscr = self.dint("Vscr", [2, 128, KS // 128, 1024])
        self.KRscr = self.dint("KRscr", [2, 64, KS])
        self.kvbuf = {}

    def sb(self, name, shape, dt):
        return self.st.enter_context(self.nc.sbuf_tensor(name, list(shape), dt))

    def _alloc(self):
        nc = self.nc
        sb = self.sb
        yt = sb("Y", [128, 8, TN], F32)
        self._ytensor = yt
        self.Y = [Tl(yt[:, c]) for c in range(8)]
        ybt = sb("YB", [128, 8, TN], BF16)
        self.YB = [Tl(ybt[:, c]) for c in range(8)]
        NA = 20
        at = sb("A", [128, NA, TN], F32)
        self.A = [Tl(at[:, c]) for c in range(NA)]
        self.Awide = at
        ht = sb("H", [128, 32, TN], BF16)
        self.H = [Tl(ht[:, c]) for c in range(32)]
        self.Ht = ht
        NBX = 10
        bt = sb("BX", [128, NBX, TN], BF16)
        self.BX = [Tl(bt[:, c]) for c in range(NBX)]
        self.BXt = bt
        self.VT = [Tl(bt[:, 2 * kb:2 * kb + 2, :].rearrange("p a n -> p (a n)")) for kb in range(4)]
        NM = 34
        mt = sb("M", [128, NM, 128], BF16)
        self.M = [Tl(mt[:, c]) for c in range(NM)]
        self.mi = 0
        mf = sb("MF", [128, 31, 128], BF16)
        self.MF = [Tl(mf[:, c]) for c in range(31)]
        xs = sb("XS", [128, 3, TN + 8], F32)
        self.XS = [Tl(xs[:, c]) for c in range(3)]
        self.NSLOT = 4
        rt = sb("RING", [128, self.NSLOT, 4096], BF16)
        self.ring = [Tl(rt[:, c]) for c in range(self.NSLOT)]
        hk = sb("HK", [128, 2, 1024], BF16)
        hv = sb("HV", [128, 2, 8, 128], BF16)
        hr = sb("HKR", [64, 2, 1024], BF16)
        self.HK = [Tl(hk[:, c]) for c in range(2)]
        self.HV = [Tl(hv[:, c]) for c in range(2)]
        self.HKR = [Tl(hr[:, c]) for c in range(2)]
        self.hslot = 0
        self.P = [Tl(self.st.enter_context(nc.psum_tensor(f"P{i}", [128, TN], F32))[:], excl=True) for i in range(8)]
        self.IDENT = Tl(sb("IDENT", [128, 128], F32)[:])
        self.IDENTB = Tl(sb("IDENTB", [128, 128], BF16)[:])
        ssb = sb("SSB", [128, 2, 128], BF16)
        self.SSB = [Tl(ssb[:, k]) for k in range(2)]
        sbb = sb("SBs", [128, 4, 128], BF16)
        self.SB = [Tl(sbb[:, c]) for c in range(4)]
        self.ONESB = Tl(sb("ONESB", [128, 128], BF16)[:])
        self.ONESF = Tl(sb("ONESF", [128, 128], F32)[:])
        self.OH8 = Tl(sb("OH8", [8, 8], F32)[:])
        self.ONES8 = Tl(sb("ONES8", [8, 128], F32)[:])
        self.MB = Tl(sb("MB", [128, 4 * 128], F32)[:])
        self.GM = Tl(sb("GM", [8, TN + 2 * SN], F32)[:])
        self.TOKM = Tl(sb("TOKM", [128, 8], F32)[:])
        self.AM = Tl(sb("AM", [128, 4 * TN + 18 * 128], BF16)[:])
        self.PROT = Tl(sb("PROT", [64, 64], F32)[:])
        self.COS = Tl(sb("COS", [64, TN], F32)[:])
        self.SIN = Tl(sb("SIN", [64, TN], F32)[:])
        self.VEC = Tl(sb("VEC", [128, NL * 5 * 8], F32)[:])
        self.CONVW = Tl(sb("CONVW", [128, 2 * 24 * 4], F32)[:])
        self.QN = Tl(sb("QN", [128, 8], F32)[:])
        self.KVN = Tl(sb("KVN", [128, 4], F32)[:])
        self.ON = Tl(sb("ON", [128, 2], F32)[:])
        self.ALOG = Tl(sb("ALOG", [8, 2], F32)[:])
        self.NEGA = Tl(sb("NEGA", [8, 2], F32)[:])
        self.DTB = Tl(sb("DTB", [8, 2], F32)[:])
        sst = sb("SST", [128, 2, 8, 128], F32)
        self.SST = [[Tl(sst[:, j, h]) for h in range(8)] for j in range(2)]
        self.SSTt = sst
        halo = sb("HALO", [128, 2, 24, 3], F32)
        self.HALO = [[Tl(halo[:, j, c]) for c in range(24)] for j in range(2)]
        self.HALOt = halo
        ts = sb("TS", [128, 4, 32], F32)
        self.TS = [Tl(ts[:, b]) for b in range(4)]
        self.DKS = Tl(sb("DKS", [128, 32], F32)[:])
        eg = sb("EGLB", [128, 2, 16], F32)
        self.EGLB2 = [Tl(eg[:, k]) for k in range(2)]
        s0 = sb("S0", [128, 4, 128], F32)
        self.S0 = [Tl(s0[:, c]) for c in range(4)]
        self.EGLT = Tl(sb("EGLT", [8, 16], F32)[:])
        s1 = sb("S1", [128, 2, 128], F32)
        self.S1 = [Tl(s1[:, c]) for c in range(2)]
        self.s0i = 0

    def mm(self, out, lhsT, rhs, start=True, stop=True, **kw):
        r, w = _rw([out], [lhsT, rhs])
        self.S.op("pe", lambda t: t.matmul(out.ap, lhsT.ap, rhs.ap, start=start, stop=stop, **kw), r, w)

    def tr(self, out, in_, ident):
        r, w = _rw([out], [in_, ident])
        self.S.op("pe", lambda t: t.transpose(out.ap, in_.ap, ident.ap), r, w)

    def act(self, out, in_, func, scale=1.0, bias=None):
        r, w = _rw([out], [in_, scale, bias])
        kw = {}
        if bias is not None:
            kw["bias"] = _a(bias)
        self.S.op("act", lambda a: a.activation(out=out.ap, in_=in_.ap, func=func, scale=_a(scale), **kw), r, w)

    def tt(self, e, out, in0, in1, op):
        r, w = _rw([out], [in0, in1])
        self.S.op(e, lambda v: v.tensor_tensor(out.ap, in0.ap, in1.ap, op), r, w)

    def ts(self, e, out, in0, s1, op0, s2=None, op1=None):
        r, w = _rw([out], [in0, s1, s2])
        if op1 is None:
            self.S.op(e, lambda v: v.tensor_scalar(out.ap, in0.ap, _a(s1), None, op0), r, w)
        else:
            self.S.op(e, lambda v: v.tensor_scalar(out.ap, in0.ap, _a(s1), _a(s2), op0, op1), r, w)

    def stt(self, out, in0, scalar, in1, op0, op1):
        r, w = _rw([out], [in0, scalar, in1])
        self.S.op("dve", lambda v: v.scalar_tensor_tensor(out.ap, in0.ap, _a(scalar), in1.ap, op0, op1), r, w)

    def cp(self, e, out, in_):
        r, w = _rw([out], [in_])
        if e == "act":
            self.S.op("act", lambda a: a.activation(out=out.ap, in_=in_.ap, func=AF.Copy), r, w)
        else:
            self.S.op(e, lambda v: v.tensor_copy(out.ap, in_.ap), r, w)

    def evac(self, out, in_):
        self.ev += 1
        self.cp("act" if self.ev % 2 else "dve", out, in_)

    def recip(self, out, in_):
        r, w = _rw([out], [in_])
        self.S.op("dve", lambda v: v.reciprocal(out.ap, in_.ap), r, w)

    def memset(self, e, out, val):
        r, w = _rw([out], [])
        self.S.op(e, lambda v: v.memset(out.ap, val), r, w)

    def dma(self, q, chan, out, in_, **kw):
        reads, writes = [], []
        if isinstance(in_, Vw):
            reads.append(in_.tl.b)
            ia = in_.ap
        else:
            ia, bl = in_
            reads += bl
        if isinstance(out, Vw):
            writes.append(out.tl.b)
            oa = out.ap
        else:
            oa, bl = out
            writes += bl
        self.S.dma(q, chan, oa, ia, reads=reads, writes=writes, **kw)

    def newM(self):
        m = self.M[self.mi % len(self.M)]
        self.mi += 1
        return m

    def R(self):
        i = self.rr
        self.rr += 1
        bank = self.P[(3, 4, 6, 7)[i % 4]]
        c = ((i // 4) % 4) * 128
        return bank[:, c:c + 128]

    def Rb(self):
        r = self.R()
        return Vw(r.tl, r.ap[:, 0:64].bitcast(BF16))

    def cast_weights(self):
        order = ["dn_in", "dn_o", "up", "down", "pproj", "gate", "mla_in", "mla_uk", "mla_uv", "mla_uq", "mla_o"]
        for k in order:
            src = self.w[k]
            dst = self.wb[k]
            L = src.shape[0]
            R_ = src.shape[1]
            if k == "dn_in":
                for l in range(L):
                    for h in range(8):
                        b = Buf()
                        self.wbuf[k].append((l, b))
                        sv = src[l, :, 0:4096].rearrange("r (g h n) -> r g h n", g=4, h=8)[:, :, h, :]
                        dv = dst[l, :, h * 512:(h + 1) * 512].rearrange("r (g n) -> r g n", g=4)
                        self.S.dma("pool", f"cast{self.castn % 4}", dv, sv, writes=[b])
                        self.castn += 1
                    b = Buf()
                    self.wbuf[k].append((l, b))
                    self.S.dma("pool", f"cast{self.castn % 4}", dst[l, :, 4096:4112], src[l, :, 4096:4112], writes=[b])
                    self.castn += 1
                continue
            for l in range(L):
                nsplit = max(1, (R_ * src.shape[2]) // (1 << 20))
                rs = R_ // nsplit
                for s in range(nsplit):
                    b = Buf()
                    self.wbuf[k].append((l, b))
                    self.S.dma("pool", f"cast{self.castn % 4}", dst[l, s * rs:(s + 1) * rs, :], src[l, s * rs:(s + 1) * rs, :], writes=[b])
                    self.castn += 1

    def piece_src(self, key):
        k = key[0]
        l = key[1]
        wbv = self.wb[k][l]
        if k in ("up", "gate", "dn_o", "mla_o"):
            j = key[2]
            v = wbv.rearrange("(c p) n -> p c n", p=128)[:, :, j * 512:(j + 1) * 512]
            return v, [8, 512]
        if k == "down":
            m = key[2]
            v = wbv.rearrange("(c p) n -> p c n", p=128)[:, :, m * 128:(m + 1) * 128]
            return v, [32, 128]
        if k == "pproj":
            return wbv.rearrange("(c p) n -> p c n", p=128), [2, 1024]
        if k == "dn_in":
            h = key[2]
            if h == "ba":
                v = wbv.rearrange("(c p) n -> p c n", p=128)[:, :, 4096:4112]
                return v, [8, 16]
            v = wbv.rearrange("(c p) n -> p c n", p=128)[:, :, h * 512:(h + 1) * 512]
            return v, [8, 4, 128]
        if k == "mla_in":
            if key[2] == 0:
                return wbv.rearrange("(c p) n -> p c n", p=128)[:, :, 0:512], [8, 512]
            return wbv.rearrange("(c p) n -> p c n", p=128)[:, :, 512:832], [8, 320]
        if k == "mla_uq":
            h = key[2]
            return wbv.rearrange("(c p) n -> p c n", p=128)[:, :, h * 192:(h + 1) * 192], [4, 192]
        if k in ("mla_uk", "mla_uv"):
            return wbv.rearrange("(c p) n -> p c n", p=128), [2, 1024]
        raise KeyError(key)

    def plan_layer(self, l):
        j = l // 2
        pl = []
        if l % 2 == 0:
            pl.append(("dn_in", j, "ba"))
            pl += [("dn_in", j, h) for h in range(8)]
            pl += [("dn_o", j, 0), ("dn_o", j, 1)]
        else:
            pl += [("mla_in", j, 0), ("mla_in", j, 1), ("mla_uk", j), ("mla_uv", j)]
            pl += [("mla_uq", j, h) for h in range(8)]
            pl += [("mla_o", j, 0), ("mla_o", j, 1)]
        pl += [("up", l, jj) for jj in range(8)]
        pl += [("down", l, m) for m in range(8)]
        pl += [("pproj", l), ("gate", l, 0), ("gate", l, 1)]
        return pl

    def set_plan(self, plan):
        self.plan = plan
        self.pi = 0
        self.pl = 0

    def _load_piece(self, idx):
        key = self.plan[idx]
        src, shp = self.piece_src(key)
        slot = self.ring[idx % self.NSLOT]
        n = int(np.prod(shp))
        dst = slot[:, 0:n].re("p (c n) -> p c n", c=shp[0])
        bl = [b for (l, b) in self.wbuf[key[0]] if l == key[1]]
        self.dma("sp", f"w{idx % self.NSLOT}", dst, (src, bl))

    def wnext(self, key):
        assert self.plan[self.pi] == key, (self.plan[self.pi], key)
        while self.pl < len(self.plan) and self.pl < self.pi + self.NSLOT:
            self._load_piece(self.pl)
            self.pl += 1
        idx = self.pi
        self.pi += 1
        _, shp = self.piece_src(key)
        slot = self.ring[idx % self.NSLOT]
        n = int(np.prod(shp))
        v = slot[:, 0:n]
        if len(shp) == 2:
            return v.re("p (c n) -> p c n", c=shp[0])
        return v.re("p (c g n) -> p c g n", c=shp[0], g=shp[1])

    def setup(self):
        ld = lambda tl, d: self.dma("sp", "const", tl[:], (d, []))
        ld(self.IDENT, self.ident_d[:, :])
        ld(self.OH8, self.oh8_d[:, :])
        self.memset("dve", self.ONES8[:], 1.0)
        ld(self.MB, self.mb_d[:, :])
        ld(self.GM, self.gm_d[:, :])
        ld(self.TOKM, self.tokm_d[:, :])
        AMW = 4 * TN + 18 * 128
        amf = self.Awide[:, 0:9, :].rearrange("p a n -> p (a n)")[:, 0:AMW]
        self.S.dma("sp", "const", amf, self.am_d[:, :], reads=[], writes=[self.A[k].b for k in range(9)])
        ld(self.PROT, self.prot_d[:, :])
        ld(self.VEC, self.vec_d[:, :])
        ld(self.CONVW, self.convw_d[:, :])
        ld(self.QN, self.qn_d[:, :])
        ld(self.KVN, self.kvn_d[:, :])
        ld(self.ON, self.on_d[:, :])
        ld(self.ALOG, self.alog_d[:, :])
        ld(self.DTB, self.dtb_d[:, :])
        self.S.op("dve", lambda v: v.tensor_copy(self.AM.ap[:, :], amf), reads=[self.A[k].b for k in range(9)], writes=[self.AM.b])
        self.cp("dve", self.IDENTB[:], self.IDENT[:])
        self.memset("dve", self.ONESB[:], 1.0)
        self.memset("dve", self.ONESF[:], 1.0)
        for g in range(3):
            self.memset("pool", self.XS[g][:, :], 0.0)
        self.act(self.NEGA[:], self.ALOG[:], AF.Exp)
        self.ts("dve", self.NEGA[:], self.NEGA[:], -1.0, ALU.mult)
        for j in range(2):
            for h in range(8):
                self.memset("dve", self.SST[j][h][:], 0.0)
            for cc in range(24):
                self.memset("pool", self.HALO[j][cc][:], 0.0)

    def vec(self, l, kind, c):
        i = (l * 5 + kind) * 8 + c
        return self.VEC[:, i:i + 1]

    def prompt_ctx(self, t):
        c = TileCtx()
        c.sample = False
        c.t = t
        c.N = TN
        c.pos0 = t * TN
        c.nblk = 4
        c.U = 128
        c.nunits = 4
        c.last = (t == self.NT - 1)
        c.L = 6
        return c

    def sample_ctx(self):
        c = TileCtx()
        c.sample = True
        c.t = self.NT
        c.N = SN
        c.pos0 = self.SEQ
        c.nblk = 1
        c.U = 32
        c.nunits = 4
        c.last = True
        c.L = 4
        return c

    def load_tile(self, c):
        N = c.N
        src = (self.xsT if c.sample else self.xT).rearrange("(c p) n -> p c n", p=128)
        src = src[:, :, 0:N] if c.sample else src[:, :, c.pos0:c.pos0 + N]
        self.S.dma("sp", "ldx", self._ytensor[:, :, 0:N], src, reads=[], writes=[y.b for y in self.Y])
        for k in range(8):
            self.cp("pool", self.YB[k][:, :N], self.Y[k][:, :N])
        self.dma("sp", "ldrope", self.COS[:, :N], (self.ropeC[:, c.pos0:c.pos0 + N], []))
        self.dma("sp", "ldrope", self.SIN[:, :N], (self.ropeS[:, c.pos0:c.pos0 + N], []))

    def bstats(self, c, srcs, want_mean, nfeat, PSs, PSq):
        N = c.N
        n = len(srcs)
        for k, s in enumerate(srcs):
            sq = self.H[k % 4][:, :N]
            self.act(sq, s, AF.Square)
            self.mm(PSq[:, :N], self.ONESB[:], sq, start=(k == 0), stop=(k == n - 1))
            if want_mean:
                rb = self.H[4 + k % 4][:, :N]
                self.cp("dve", rb, s)
                self.mm(PSs[:, :N], self.ONESB[:], rb, start=(k == 0), stop=(k == n - 1))
        A = self.A
        inv = 1.0 / nfeat
        if want_mean:
            self.act(A[0][:, :N], PSs[:, :N], AF.Copy, scale=inv)
            self.tt("dve", A[1][:, :N], A[0][:, :N], A[0][:, :N], ALU.mult)
            self.stt(A[2][:, :N], PSq[:, :N], inv, A[1][:, :N], ALU.mult, ALU.subtract)
            self.act(A[2][:, :N], A[2][:, :N], AF.Ln, bias=EPS)
        else:
            self.act(A[2][:, :N], PSq[:, :N], AF.Ln, scale=inv, bias=EPS)
        self.act(A[3][:, :N], A[2][:, :N], AF.Exp, scale=-0.5)
        return A[3]

    def layer_norm(self, c, l, kg, kb):
        N = c.N
        A = self.A
        rstd = self.bstats(c, [self.Y[k][:, :N] for k in range(8)], True, float(D), self.P[2], self.P[3])
        for k in range(8):
            t = A[4 + k % 2][:, :N]
            self.tt("dve", t, self.Y[k][:, :N], A[0][:, :N], ALU.subtract)
            self.tt("dve", t, t, rstd[:, :N], ALU.mult)
            self.act(self.Y[k][:, :N], t, AF.Identity, scale=self.vec(l, kg, k), bias=self.vec(l, kb, k))
            self.act(self.YB[k][:, :N], t, AF.Identity, scale=self.vec(l, kg, k), bias=self.vec(l, kb, k))

    def residual(self, c, k, ps):
        N = c.N
        self.stt(self.Y[k][:, :N], self.Y[k][:, :N], ALPHA, ps, ALU.mult, ALU.add)

    def finish(self, c, l):
        N = c.N
        A = self.A
        self.layer_norm(c, l, 0, 1)
        pb = 0
        for jj in range(8):
            wu = self.wnext(("up", l, jj))
            for m in range(4):
                ff = jj * 4 + m
                ps = self.P[pb % 2]
                pb += 1
                for kc in range(8):
                    self.mm(ps[:, :N], wu[:, kc, m * 128:(m + 1) * 128], self.YB[kc][:, :N], start=(kc == 0), stop=(kc == 7))
                r = A[6 + ff % 2][:, :N]
                self.act(r, ps[:, :N], AF.Relu)
                self.tt("dve" if ff % 2 else "pool", self.H[ff][:, :N], r, r, ALU.mult)
        for m in range(8):
            wd = self.wnext(("down", l, m))
            ps = self.P[pb % 2]
            pb += 1
            for kc in range(32):
                self.mm(ps[:, :N], wd[:, kc, :], self.H[kc][:, :N], start=(kc == 0), stop=(kc == 31))
            self.residual(c, m, ps[:, :N])
        self.layer_norm(c, l, 2, 3)
        psrc = (self.psT if c.sample else self.pT)[l].rearrange("(c p) n -> p c n", p=128)
        psrc = psrc[:, :, 0:N] if c.sample else psrc[:, :, c.pos0:c.pos0 + N]
        PT = [A[18], A[19]]
        self.S.dma("sp", "ldp", self.Awide[:, 18:20, 0:N], psrc, reads=[], writes=[PT[0].b, PT[1].b])
        PB = [self.BX[8], self.BX[9]]
        for kc in range(2):
            self.cp("pool", PB[kc][:, :N], PT[kc][:, :N])
        wpj = self.wnext(("pproj", l))
        E = [A[8 + m] for m in range(8)]
        for m in range(8):
            ps = self.P[pb % 2]
            pb += 1
            for kc in range(2):
                self.mm(ps[:, :N], wpj[:, kc, m * 128:(m + 1) * 128], PB[kc][:, :N], start=(kc == 0), stop=(kc == 1))
            self.cp("act", E[m][:, :N], ps[:, :N])
        rstd = self.bstats(c, [E[m][:, :N] for m in range(8)], False, float(D), self.P[2], self.P[3])
        for jj in range(2):
            wg = self.wnext(("gate", l, jj))
            for m4 in range(4):
                m = jj * 4 + m4
                ps = self.P[pb % 2]
                pb += 1
                for kc in range(8):
                    self.mm(ps[:, :N], wg[:, kc, m4 * 128:(m4 + 1) * 128], self.YB[kc][:, :N], start=(kc == 0), stop=(kc == 7))
                gt = A[16 + m % 2][:, :N]
                self.act(gt, ps[:, :N], AF.Sigmoid)
                t = A[4 + m % 2][:, :N]
                self.tt("pool" if m % 2 else "dve", t, E[m][:, :N], rstd[:, :N], ALU.mult)
                self.tt("dve", t, t, gt, ALU.mult)
                self.stt(self.Y[m][:, :N], t, self.vec(l, 4, m), self.Y[m][:, :N], ALU.mult, ALU.add)
        for m in range(8):
            self.cp("act" if m % 2 else "dve", self.YB[m][:, :N], self.Y[m][:, :N])

    def store(self, dram_ap, src, chan="st"):
        b = Buf()
        self.obufs.append((chan, b))
        self.dma("pool", chan, (dram_ap, [b]), src)

    def store_y(self, c):
        N = c.N
        dst = (self.o_ysT if c.sample else self.o_yT).rearrange("(c p) n -> p c n", p=128)
        dst = dst[:, :, 0:N] if c.sample else dst[:, :, c.pos0:c.pos0 + N]
        b = Buf()
        self.S.dma("pool", "sty", dst, self._ytensor[:, :, 0:N], reads=[y.b for y in self.Y], writes=[b])

    def build(self, mixers=True):
        self.mixers = mixers
        self.setup()
        self.cast_weights()
        tiles = [self.prompt_ctx(t) for t in range(self.NT)]
        if self.sample:
            tiles.append(self.sample_ctx())
        plan = []
        for c in tiles:
            for l in range(self.NLAY):
                pl = self.plan_layer(l)
                if not mixers:
                    pl = [k for k in pl if k[0] in ("up", "down", "pproj", "gate")]
                plan += pl
        if self.sample and mixers:
            pre = []
            for j in range(2):
                if 2 * j + 1 < self.NLAY:
                    pre += [("mla_uk", j), ("mla_uv", j)]
            plan = pre + plan
        self.set_plan(plan)
        if self.sample and mixers:
            self.sample_cache_kv()
        for c in tiles:
            self.load_tile(c)
            for l in range(self.NLAY):
                if mixers:
                    if l % 2 == 0:
                        self.dn_layer(c, l)
                    else:
                        self.mla_layer(c, l)
                else:
                    for k in range(8):
                        self.ts("dve", self.Y[k][:, :c.N], self.Y[k][:, :c.N], ALPHA, ALU.mult)
                self.finish(c, l)
            self.store_y(c)
        for name, ch in self.S.chan.items():
            if name.startswith("st"):
                self.nc.gpsimd.wait_ge(ch.sem, ch.n)
        self.st.close()
        return self.nc


def _consts(SEQ):
    c = {}
    c["ident"] = np.eye(128, dtype=np.float32)
    c["oh8"] = np.eye(8, dtype=np.float32)
    j = np.arange(128)[:, None]
    i = np.arange(128)[None, :]
    p_incl = np.where(i >= j, 0.0, NEG)
    p_strict = np.where(i > j, 0.0, NEG)
    same = (i // 32) == (j // 32)
    jreal = (j % 32) >= 16
    s_incl = np.where(same & jreal & (i >= j), 0.0, NEG)
    s_strict = np.where(same & jreal & (i > j), 0.0, NEG)
    c["maskbias"] = np.concatenate([p_incl, p_strict, s_incl, s_strict], 1).astype(np.float32)
    tt = np.arange(TN)
    reset_p = np.where(tt % 128 == 0, 0.0, 1.0)
    reset_s = np.where(tt % 32 == 0, 0.0, 1.0)[:SN]
    real_s = np.where((tt % 32) >= 16, 1.0, 0.0)[:SN]
    gm = np.concatenate([reset_p, reset_s, real_s], 0).reshape(1, TN + 2 * SN)
    c["gatemask"] = np.repeat(gm, 8, 0).astype(np.float32)
    tok = np.arange(128)
    tokm = np.zeros((128, 8), np.float32)
    tokm[:, 0] = (tok % 32) >= 16
    for s in range(4):
        tokm[:, 1 + s] = ((tok // 32) == s) & ((tok % 32) >= 16)
    c["tokmask"] = tokm
    am = np.zeros((128, 4 * TN + 18 * 128), np.float32)
    q = np.arange(TN)[None, :]
    for kb in range(4):
        kp = kb * 128 + np.arange(128)[:, None]
        am[:, kb * TN:(kb + 1) * TN] = (kp // 64) <= (q // 64)
    qs = np.arange(128)[None, :]
    for s in range(4):
        am[:, 4 * TN + s * 128:4 * TN + (s + 1) * 128] = np.broadcast_to((qs // 32) == s, (128, 128))
    ks = np.arange(128)[:, None]
    am[:, 4 * TN + 4 * 128:4 * TN + 5 * 128] = ((ks // 32) == (qs // 32)) & ((ks % 32) >= 16)
    o0 = 4 * TN + 5 * 128
    ri = np.arange(128)[:, None]
    ci = np.arange(128)[None, :]
    am[:, o0:o0 + 128] = (ri // 2) == (ci // 2)
    for lev, s_ in enumerate((2, 4, 8, 16, 32, 64)):
        ML = ((ri // (2 * s_)) == (ci // (2 * s_))) & ((ri % (2 * s_)) >= s_) & ((ci % (2 * s_)) < s_)
        am[:, o0 + (1 + 2 * lev) * 128:o0 + (2 + 2 * lev) * 128] = ML
        am[:, o0 + (2 + 2 * lev) * 128:o0 + (3 + 2 * lev) * 128] = ML.T
    c["attnmask"] = am
    prot = np.zeros((64, 64), np.float32)
    for m in range(32):
        prot[m + 32, m] = -1.0
        prot[m, m + 32] = 1.0
    c["prot"] = prot
    half = 32
    inv_freq = (10000.0 ** (-np.arange(half, dtype=np.float32) / half)).astype(np.float32)
    pos = np.concatenate([np.arange(SEQ), np.tile(np.concatenate([np.zeros(16), PAST + np.arange(16)]), 4)]).astype(np.float32)
    ang = pos[None, :] * inv_freq[:, None]
    c["ropeC"] = np.concatenate([np.cos(ang), np.cos(ang)], 0).astype(np.float32)
    c["ropeS"] = np.concatenate([np.sin(ang), np.sin(ang)], 0).astype(np.float32)
    return c


def _fm(v, nchunk):
    return np.ascontiguousarray(np.moveaxis(v.reshape(v.shape[:-1] + (nchunk, 128)), -1, 0))


def host_inputs(inp, NT, core_seq, core_samp):
    SEQ = NT * TN
    f = lambda a: np.ascontiguousarray(np.asarray(a, dtype=np.float32))
    cst = _consts(SEQ)
    shared = dict(cst)
    for k in ("mlp_w_up", "mlp_w_down", "ple_w_proj", "ple_w_gate", "dn_w_in", "dn_w_o", "mla_w_in", "mla_w_uq", "mla_w_o"):
        shared[k] = f(inp[k])
    shared["mla_w_uk"] = f(inp["mla_w_uk"]).reshape(2, 256, 1024)
    shared["mla_w_uv"] = f(inp["mla_w_uv"]).reshape(2, 256, 1024)
    vec = np.stack([f(inp[k]) for k in ("ln1_g", "ln1_b", "ln2_g", "ln2_b", "ple_norm")], 1)
    shared["vec1024"] = _fm(vec, 8).reshape(128, NL * 5 * 8)
    shared["convw"] = np.ascontiguousarray(np.transpose(f(inp["dn_conv_w"]).reshape(2, 4, 24, 128), (3, 0, 2, 1))).reshape(128, 2 * 24 * 4)
    shared["qnorm"] = _fm(f(inp["mla_q_norm"]), 4).reshape(128, 8)
    shared["kvnorm"] = _fm(f(inp["mla_kv_norm"]), 2).reshape(128, 4)
    shared["onorm"] = np.ascontiguousarray(f(inp["dn_o_norm"]).T)
    shared["alog"] = np.ascontiguousarray(f(inp["dn_a_log"]).T)
    shared["dtb"] = np.ascontiguousarray(f(inp["dn_dt_bias"]).T)
    maps = []
    xp = f(inp["x_prompt"])
    pp = f(inp["p_prompt"])
    xs = f(inp["x_sample"])
    ps = f(inp["p_sample"])
    for c in range(len(core_seq)):
        m = dict(shared)
        b = core_seq[c]
        if b is None:
            m["xT"] = np.zeros((D, SEQ), np.float32)
            m["pT"] = np.zeros((NL, 256, SEQ), np.float32)
        else:
            m["xT"] = np.ascontiguousarray(xp[b, :SEQ].T)
            m["pT"] = np.ascontiguousarray(np.transpose(pp[:, b, :SEQ], (0, 2, 1)))
        ss = core_samp[c]
        xsT = np.zeros((D, 4, 32), np.float32)
        psT = np.zeros((NL, 256, 4, 32), np.float32)
        for i, s in enumerate(ss):
            xsT[:, i, 16:] = xs[s].T
            psT[:, :, i, 16:] = np.transpose(ps[:, s], (0, 2, 1))
        m["xsT"] = xsT.reshape(D, SN)
        m["psT"] = psT.reshape(NL, 256, SN)
        m["sconvT"] = np.ascontiguousarray(np.transpose(f(inp["state_dn_conv"])[:, ss], (0, 1, 3, 2)))
        m["srec"] = np.ascontiguousarray(f(inp["state_dn_recurrent"])[:, ss])
        m["ckvT"] = np.ascontiguousarray(np.transpose(f(inp["cache_mla_ckv"])[:, ss], (0, 1, 3, 2)))
        m["krT"] = np.ascontiguousarray(np.transpose(f(inp["cache_mla_krope"])[:, ss], (0, 1, 3, 2)))
        maps.append(m)
    return maps


def dn_layer(self, c, l):
    j = l // 2
    N = c.N
    A = self.A
    H = self.H
    nblk = c.nblk
    U = c.U
    nun = N // U
    samp = c.sample
    mbI = self.MB[:, (2 if samp else 0) * 128:(3 if samp else 1) * 128]
    mbS = self.MB[:, (3 if samp else 1) * 128:(4 if samp else 2) * 128]
    reset = self.GM[:, TN:TN + SN] if samp else self.GM[:, 0:TN]
    realT = self.GM[:, TN + SN:TN + 2 * SN]
    wba = self.wnext(("dn_in", j, "ba"))
    pb_, pa_ = self.P[2], self.P[3]
    for kc in range(8):
        self.mm(pb_[0:8, :N], wba[:, kc, 0:8], self.YB[kc][:, :N], start=(kc == 0), stop=(kc == 7))
    for kc in range(8):
        self.mm(pa_[0:8, :N], wba[:, kc, 8:16], self.YB[kc][:, :N], start=(kc == 0), stop=(kc == 7))
    BETA, LNB, G, GC, GCB, EGC, DKD, NGC, NS1, TMP = [A[k][0:8, :N] for k in range(10)]
    self.act(BETA, pb_[0:8, :N], AF.Sigmoid)
    self.act(LNB, BETA, AF.Ln)
    self.act(G, pa_[0:8, :N], AF.Exp, bias=self.DTB[:, j:j + 1])
    self.act(G, G, AF.Ln, bias=1.0)
    self.ts("dve", G, G, self.NEGA[:, j:j + 1], ALU.mult)
    if samp:
        self.tt("dve", G, G, realT, ALU.mult)
    r, w = _rw([GC], [reset, G])
    self.S.op("dve", lambda v: v.tensor_tensor_scan(GC.ap, reset.ap, G.ap, 0.0, ALU.mult, ALU.add), r, w)
    self.tt("dve", GCB, GC, LNB, ALU.add)
    self.act(EGC, GC, AF.Exp)
    gc3 = GC.re("p (u k) -> p u k", k=U)
    gl = gc3[:, :, U - 1:U]
    self.tt("dve", TMP.re("p (u k) -> p u k", k=U), gl.bc([8, nun, U]), gc3, ALU.subtract)
    self.act(DKD, TMP, AF.Exp)
    if samp:
        self.tt("dve", DKD, DKD, realT, ALU.mult)
    EGL = self.EGLT[:, 0:nun]
    self.act(EGL.re("p (u k) -> p u k", k=1), gl, AF.Exp)
    self.ts("dve", NGC, GC, -1.0, ALU.mult)
    self.act(NS1, GCB, AF.Exp)
    self.ts("dve", NS1, NS1, -1.0, ALU.mult)
    for b in range(nblk):
        cb = slice(b * 128, (b + 1) * 128)
        ps = self.P[6 + b % 2]
        for q, src in enumerate((NGC, NS1, BETA, DKD)):
            self.tr(ps[:, q * 8:(q + 1) * 8], src[:, cb], self.IDENT[0:8, 0:8])
        self.cp("dve", self.TS[b][:, :], ps[:, 0:32])
    if samp:
        for s in range(4):
            self.ts("dve", self.DKS[:, s * 8:(s + 1) * 8], self.TS[0][:, 24:32], self.TOKM[:, 1 + s:2 + s], ALU.mult)
    if samp:
        segs = [(s * 32, 32, s) for s in range(4)]
    else:
        segs = [(0, 128, 0)]
    sets = [dict(VSb=H[1], QHb=H[2], KHb=H[3], QGb=H[4], ZSb=H[5], DI=A[15], DS=A[16], EGLB=self.EGLB2[0]),
            dict(VSb=H[18], QHb=H[19], KHb=H[20], QGb=H[21], ZSb=H[22], DI=A[0], DS=A[1], EGLB=self.EGLB2[1])]
    blks = range(nblk)
    cbs = [slice(b * 128, (b + 1) * 128) for b in blks]
    MF = self.MF
    o0 = 4 * TN + 5 * 128
    msk = lambda k: self.AM[:, o0 + k * 128:o0 + (k + 1) * 128]
    PO = self.P[5]
    st = {"pbi": 0}

    def prep(h):
        S_ = sets[h % 2]
        wp = self.wnext(("dn_in", j, h))
        QKV = [A[10], A[11], A[12]]
        VSb = S_["VSb"][:, :N]
        for g in range(4):
            ps = self.P[st["pbi"] % 2]
            st["pbi"] += 1
            for kc in range(8):
                self.mm(ps[:, :N], wp[:, kc, g, :], self.YB[kc][:, :N], start=(kc == 0), stop=(kc == 7))
            if g == 3:
                self.act(S_["ZSb"][:, :N], ps[:, :N], AF.Silu)
                yield
                continue
            xs = self.XS[g]
            self.cp("act", xs[:, 3:3 + N], ps[:, :N])
            ch = g * 8 + h
            if samp:
                for s in range(4):
                    self.dma("sp", "ldcs", xs[:, 3 + 32 * s + 13:3 + 32 * s + 16],
                             (self.sconvT[j, s, ch * 128:(ch + 1) * 128, :], []))
                    self.store(self.o_sconv[j, s, ch * 128:(ch + 1) * 128, :], xs[:, 3 + 32 * s + 29:3 + 32 * s + 32], "stc")
            else:
                self.cp("dve", xs[:, 0:3], self.HALO[j][ch][:, :])
                self.cp("dve", self.HALO[j][ch][:, :], xs[:, N:N + 3])
                if c.last:
                    self.store(self.o_pconv[j, ch * 128:(ch + 1) * 128, :], self.HALO[j][ch][:, :], "stc")
            yield
            cw = lambda i: self.CONVW[:, ((j * 24 + ch) * 4 + i):((j * 24 + ch) * 4 + i + 1)]
            acc = QKV[g][:, :N]
            self.ts("dve", acc, xs[:, 0:N], cw(0), ALU.mult)
            self.stt(acc, xs[:, 1:1 + N], cw(1), acc, ALU.mult, ALU.add)
            yield
            self.stt(acc, xs[:, 2:2 + N], cw(2), acc, ALU.mult, ALU.add)
            self.stt(acc, xs[:, 3:3 + N], cw(3), acc, ALU.mult, ALU.add)
            if g == 2:
                self.act(VSb, acc, AF.Silu)
            else:
                self.act(acc, acc, AF.Silu)
            yield
        QS, KS = QKV[0][:, :N], QKV[1][:, :N]
        QHb = S_["QHb"][:, :N]
        KHb = S_["KHb"][:, :N]
        for src, dst, sc, rt in ((QS, QHb, 128.0 ** -0.5, A[13]), (KS, KHb, 1.0, A[14])):
            sq = H[0][:, :N]
            self.act(sq, src, AF.Square)
            self.mm(self.P[2][:, :N], self.ONESB[:], sq)
            self.act(rt[:, :N], self.P[2][:, :N], AF.Ln, bias=EPS)
            self.act(rt[:, :N], rt[:, :N], AF.Exp, scale=-0.5)
            self.stt(dst, src, sc, rt[:, :N], ALU.mult, ALU.mult)
            yield
        oh = self.OH8[:, h:h + 1]
        DI = S_["DI"][:, :N]
        DS = S_["DS"][:, :N]
        QGb = S_["QGb"][:, :N]
        PB = self.P[2]
        tmpr = A[9][0:8, :N]

        def bcast(src, n_):
            self.ts("dve", tmpr[:, 0:n_], src, oh, ALU.mult)
            self.mm(PB[:, 0:n_], self.ONES8[:, :], tmpr[:, 0:n_])

        bcast(GC, N)
        for b in range(nblk):
            self.tt("dve", DI[:, cbs[b]], PB[:, cbs[b]], mbI, ALU.add)
        yield
        bcast(GCB, N)
        for b in range(nblk):
            self.tt("dve", DS[:, cbs[b]], PB[:, cbs[b]], mbS, ALU.add)
        yield
        bcast(EGC, N)
        self.tt("dve", QGb, QHb, PB[:, :N], ALU.mult)
        bcast(EGL, nun)
        self.cp("dve", S_["EGLB"][:, 0:nun], PB[:, 0:nun])
        yield

    def solve(h):
        S_ = sets[h % 2]
        VSb = S_["VSb"][:, :N]
        QHb = S_["QHb"][:, :N]
        KHb = S_["KHb"][:, :N]
        QGb = S_["QGb"][:, :N]
        DI = S_["DI"][:, :N]
        DS = S_["DS"][:, :N]
        EGLB = S_["EGLB"]
        attnT, TT, T_, vb, kds, n0, n0t = ({} for _ in range(7))
        for b in blks:
            ngc = self.TS[b][:, h:h + 1]
            DTs = self.newM()
            DTi = self.newM()
            self.act(DTs[:, :], DS[:, cbs[b]], AF.Exp, bias=ngc)
            self.act(DTi[:, :], DI[:, cbs[b]], AF.Exp, bias=ngc)
            r0 = self.R()
            self.mm(r0, KHb[:, cbs[b]], KHb[:, cbs[b]])
            n0t[b] = MF[b * 6 + 5]
            self.stt(n0t[b][:, :], r0, -1.0, DTs[:, :], ALU.mult, ALU.mult)
            r1 = self.Rb()
            self.tr(r1, n0t[b][:, :], self.IDENTB[:])
            n0[b] = MF[b * 6 + 4]
            self.evac(n0[b][:, :], r1)
            T_[b] = self.newM()
            self.tt("pool", T_[b][:, :], n0[b][:, :], msk(0), ALU.mult)
            self.tt("pool", T_[b][:, :], T_[b][:, :], self.IDENTB[:], ALU.add)
            TT[b] = self.newM()
            self.tt("pool", TT[b][:, :], n0t[b][:, :], msk(0), ALU.mult)
            self.tt("pool", TT[b][:, :], TT[b][:, :], self.IDENTB[:], ALU.add)
            yield
            r2 = self.R()
            self.mm(r2, KHb[:, cbs[b]], QHb[:, cbs[b]])
            attnT[b] = MF[b * 6 + 0]
            self.tt("dve", attnT[b][:, :], r2, DTi[:, :], ALU.mult)
            r3 = self.Rb()
            self.tr(r3, VSb[:, cbs[b]], self.IDENTB[:])
            vb[b] = MF[b * 6 + 1]
            self.ts("dve", vb[b][:, :], r3, self.TS[b][:, 16 + h:17 + h], ALU.mult)
            r4 = self.Rb()
            self.tr(r4, KHb[:, cbs[b]], self.IDENTB[:])
            kds[b] = []
            for si, (p0, pl_, s) in enumerate(segs):
                kd = MF[b * 6 + 2] if si == 0 else MF[23 + si]
                if samp:
                    self.ts("dve", kd[:, :], r4, self.DKS[:, s * 8 + h:s * 8 + h + 1], ALU.mult)
                else:
                    self.ts("dve", kd[:, :], r4, self.TS[b][:, 24 + h:25 + h], ALU.mult)
                kds[b].append(kd)
            yield
        for lev in range(c.L):
            last = (lev == c.L - 1)
            X = {}
            for b in blks:
                rx = self.R()
                self.mm(rx, n0t[b][:, :], T_[b][:, :])
                X[b] = self.newM()
                self.tt("dve", X[b][:, :], rx, msk(1 + 2 * lev), ALU.mult)
            yield
            for b in blks:
                rp = self.R()
                self.mm(rp, X[b][:, :], TT[b][:, :])
                TT2 = MF[b * 6 + 3] if last else self.newM()
                self.tt("dve", TT2[:, :], rp, TT[b][:, :], ALU.add)
                if not last:
                    rq = self.R()
                    self.mm(rq, TT[b][:, :], X[b][:, :])
                    T2 = self.newM()
                    self.tt("dve", T2[:, :], rq, T_[b][:, :], ALU.add)
                    T_[b] = T2
                TT[b] = TT2
                if b % 2 == 1:
                    yield
        if not samp:
            Sbh = self.SSB[h % 2]
            self.cp("act", Sbh[:, :], self.SST[j][h][:, :])
        for b in blks:
            cb = cbs[b]
            ns1 = self.TS[b][:, 8 + h:9 + h]
            Ss = []
            r5 = self.R()
            for si, (p0, pl_, s) in enumerate(segs):
                if samp:
                    Sf = self.S0[s]
                    self.dma("sp", "lds0", Sf[:, :], (self.srec[j, s, h], []))
                    Sb_ = self.SB[s]
                    self.cp("act", Sb_[:, :], Sf[:, :])
                else:
                    Sf = self.SST[j][h]
                    Sb_ = Sbh
                Ss.append((Sf, Sb_))
                kw = {} if pl_ == 128 else {"tile_position": (0, p0)}
                self.mm(r5[p0:p0 + pl_, :], KHb[:, b * 128 + p0:b * 128 + p0 + pl_], Sb_[:, :], **kw)
            rr_ = MF[27 + b % 2]
            self.stt(rr_[:, :], r5, ns1, vb[b][:, :], ALU.mult, ALU.add)
            yield
            r6 = self.R()
            self.mm(r6, TT[b][:, :], rr_[:, :])
            vn = MF[29 + b % 2]
            self.cp("act", vn[:, :], r6)
            yield
            for si, (p0, pl_, s) in enumerate(segs):
                cs = slice(b * 128 + p0, b * 128 + p0 + pl_)
                self.mm(PO[:, cs], Ss[si][1][:, :], QGb[:, cs], start=(si == 0), stop=False)
            self.mm(PO[:, cb], vn[:, :], attnT[b][:, :], start=False, stop=True)
            for si, (p0, pl_, s) in enumerate(segs):
                Sf, Sb_ = Ss[si]
                r7 = self.R()
                self.mm(r7, kds[b][si][:, :], vn[:, :])
                u = b * (128 // U) + (si if samp else 0)
                if samp:
                    So = self.S1[self.s0i % 2]
                    self.s0i += 1
                    self.stt(So[:, :], Sf[:, :], EGLB[:, u:u + 1], r7, ALU.mult, ALU.add)
                    self.store(self.o_srec[j, s, h], So[:, :], "sts")
                else:
                    self.stt(Sf[:, :], Sf[:, :], EGLB[:, u:u + 1], r7, ALU.mult, ALU.add)
                    self.cp("act", Sb_[:, :], Sf[:, :])
            yield
        sq = H[23][:, :N]
        self.act(sq, PO[:, :N], AF.Square)
        self.mm(self.P[2][:, :N], self.ONESB[:], sq)
        rn = A[18][:, :N]
        self.act(rn, self.P[2][:, :N], AF.Ln, scale=1.0 / 128.0, bias=EPS)
        self.act(rn, rn, AF.Exp, scale=-0.5)
        self.stt(rn, PO[:, :N], self.ON[:, j:j + 1], rn, ALU.mult, ALU.mult)
        self.tt("dve", H[8 + h][:, :N], rn, S_["ZSb"][:, :N], ALU.mult)
        if (not samp) and c.last:
            self.store(self.o_prec[j, h], self.SST[j][h][:, :], "sts")
        yield

    def drive(gens):
        gens = [g for g in gens if g is not None]
        while gens:
            for g in list(gens):
                try:
                    next(g)
                except StopIteration:
                    gens.remove(g)

    drive([prep(0)])
    for h in range(8):
        drive([solve(h), prep(h + 1) if h < 7 else None])
    self.out_proj(c, ("dn_o", j))


def out_proj(self, c, key):
    N = c.N
    pb = 0
    for half in range(2):
        wo = self.wnext(key + (half,))
        for m4 in range(4):
            m = half * 4 + m4
            ps = self.P[pb % 2]
            pb += 1
            for hh in range(8):
                self.mm(ps[:, :N], wo[:, hh, m4 * 128:(m4 + 1) * 128], self.H[8 + hh][:, :N], start=(hh == 0), stop=(hh == 7))
            self.residual(c, m, ps[:, :N])


MK.dn_layer = dn_layer
MK.out_proj = out_proj


def rope(self, c, src, dst, tmp):
    N = c.N
    ps = self.P[6]
    self.mm(ps[0:64, :N], self.PROT[:, :], src)
    self.tt("dve", tmp, src, self.COS[:, :N], ALU.mult)
    self.tt("dve", dst, ps[0:64, :N], self.SIN[:, :N], ALU.mult)
    self.tt("pool", dst, dst, tmp, ALU.add)


def make_k(self, wuk, CKB, N):
    for h in range(8):
        ps = self.P[h % 2]
        for kc in range(2):
            self.mm(ps[:, :N], wuk[:, kc, h * 128:(h + 1) * 128], CKB[kc][:, :N], start=(kc == 0), stop=(kc == 1))
        self.evac(self.H[19 + h][:, :N], ps[:, :N])


def make_v(self, wuv, CKB, nblk):
    for kb in range(nblk):
        for half in range(2):
            ps = self.P[half]
            for kc in range(2):
                self.mm(ps[:, :], CKB[kc][:, kb * 128:(kb + 1) * 128], wuv[:, kc, half * 512:(half + 1) * 512], start=(kc == 0), stop=(kc == 1))
            self.evac(self.VT[kb][:, half * 512:(half + 1) * 512], ps[:, :])


def spill_k(self, j, key0, N):
    b = Buf()
    dst = self.Kscr[j].rearrange("h p n -> p h n")[:, :, key0:key0 + N]
    self.S.dma("pool", "spk", dst, self.Ht[:, 19:27, 0:N], reads=[self.H[19 + h].b for h in range(8)], writes=[b])
    return b


def spill_v(self, j, key0, nblk):
    b = Buf()
    dst = self.Vscr[j][:, key0 // 128:key0 // 128 + nblk, :]
    src = self.BXt[:, 0:2 * nblk, :].rearrange("p (b a) n -> p b (a n)", a=2)
    self.S.dma("pool", "spv", dst, src, reads=[self.VT[kb].b for kb in range(nblk)], writes=[b])
    return b


def sample_cache_kv(self):
    A = self.A
    H = self.H
    for j in range(2):
        if 2 * j + 1 >= self.NLAY:
            continue
        bl = []
        grp = [(s, g) for s in range(4) for g in range(2)]
        for gi, (s, g) in enumerate(grp):
            src = self.ckvT[j, s].rearrange("(c p) n -> p c n", p=128)[:, :, g * 512:(g + 1) * 512]
            self.S.dma("sp", "ldck", self.Awide[:, 12:14, :], src, reads=[], writes=[A[12].b, A[13].b])
            for kc in range(2):
                self.cp("pool", H[2 * gi + kc][:, :], A[12 + kc][:, :])
        for s in range(4):
            b = Buf()
            self.S.dma("pool", "spr", self.KRscr[j][:, self.SEQ + s * PAST:self.SEQ + (s + 1) * PAST], self.krT[j, s], reads=[], writes=[b])
            bl.append(b)
        wuk = self.wnext(("mla_uk", j))
        for gi, (s, g) in enumerate(grp):
            self.make_k(wuk, [H[2 * gi], H[2 * gi + 1]], TN)
            bl.append(self.spill_k(j, self.SEQ + s * PAST + g * 512, TN))
        wuv = self.wnext(("mla_uv", j))
        for gi, (s, g) in enumerate(grp):
            self.make_v(wuv, [H[2 * gi], H[2 * gi + 1]], 4)
            bl.append(self.spill_v(j, self.SEQ + s * PAST + g * 512, 4))
        self.kvbuf[(j, "cache")] = bl


def mla_layer(self, c, l):
    j = l // 2
    N = c.N
    A = self.A
    H = self.H
    samp = c.sample
    nblk = c.nblk
    pb = 0
    wi0 = self.wnext(("mla_in", j, 0))
    CQ = [A[8 + m] for m in range(4)]
    for m in range(4):
        ps = self.P[pb % 2]
        pb += 1
        for kc in range(8):
            self.mm(ps[:, :N], wi0[:, kc, m * 128:(m + 1) * 128], self.YB[kc][:, :N], start=(kc == 0), stop=(kc == 7))
        self.evac(CQ[m][:, :N], ps[:, :N])
    rstd = self.bstats(c, [CQ[m][:, :N] for m in range(4)], False, 512.0, self.P[2], self.P[3])
    for m in range(4):
        self.stt(H[4 + m][:, :N], CQ[m][:, :N], self.QN[:, j * 4 + m:j * 4 + m + 1], rstd[:, :N], ALU.mult, ALU.mult)
    wi1 = self.wnext(("mla_in", j, 1))
    CKV = [A[12], A[13]]
    CKVN = [A[14], A[15]]
    for m in range(2):
        ps = self.P[pb % 2]
        pb += 1
        for kc in range(8):
            self.mm(ps[:, :N], wi1[:, kc, m * 128:(m + 1) * 128], self.YB[kc][:, :N], start=(kc == 0), stop=(kc == 7))
        self.evac(CKV[m][:, :N], ps[:, :N])
    ps = self.P[pb % 2]
    pb += 1
    for kc in range(8):
        self.mm(ps[0:64, :N], wi1[:, kc, 256:320], self.YB[kc][:, :N], start=(kc == 0), stop=(kc == 7))
    KR = A[16][0:64, :N]
    self.evac(KR, ps[0:64, :N])
    rstd = self.bstats(c, [CKV[m][:, :N] for m in range(2)], False, 256.0, self.P[2], self.P[3])
    CKB = [H[16], H[17]]
    for m in range(2):
        self.stt(CKVN[m][:, :N], CKV[m][:, :N], self.KVN[:, j * 2 + m:j * 2 + m + 1], rstd[:, :N], ALU.mult, ALU.mult)
        if samp:
            self.store(self.o_sckv[j, m * 128:(m + 1) * 128, 0:N], CKVN[m][:, :N], "stk")
        else:
            self.store(self.o_pckv[j, m * 128:(m + 1) * 128, c.pos0:c.pos0 + N], CKVN[m][:, :N], "stk")
        self.cp("pool", CKB[m][:, :N], CKVN[m][:, :N])
    KRR = A[17][0:64, :N]
    self.rope(c, KR, KRR, A[18][0:64, :N])
    if samp:
        self.store(self.o_skr[j, :, 0:N], KRR, "stk")
    else:
        self.store(self.o_pkr[j, :, c.pos0:c.pos0 + N], KRR, "stk")
    KRB = H[18][0:64, :N]
    self.cp("pool", KRB, KRR)
    wuk = self.wnext(("mla_uk", j))
    self.make_k(wuk, CKB, N)
    wuv = self.wnext(("mla_uv", j))
    self.make_v(wuv, CKB, nblk)
    if not samp:
        bl = [self.spill_k(j, c.pos0, N), self.spill_v(j, c.pos0, nblk)]
        b = Buf()
        self.dma("pool", "spr", (self.KRscr[j][:, c.pos0:c.pos0 + N], [b]), KRB)
        bl.append(b)
        self.kvbuf[(j, c.t)] = bl
    srcs = []
    if samp:
        for s in range(4):
            msk = self.AM[:, 4 * TN + s * 128:4 * TN + (s + 1) * 128]
            srcs.append((self.SEQ + s * PAST, PAST, self.kvbuf[(j, "cache")], msk))
    else:
        t0 = 0
        while t0 < c.t:
            nt = min(2, c.t - t0)
            bl = []
            for tt_ in range(t0, t0 + nt):
                bl += self.kvbuf[(j, tt_)]
            srcs.append((t0 * TN, nt * TN, bl, None))
            t0 += nt
    PO = self.P[5]
    PD = self.P[4]
    for h in range(8):
        wq = self.wnext(("mla_uq", j, h))
        ps = self.P[pb % 2]
        pb += 1
        for kc in range(4):
            self.mm(ps[:, :N], wq[:, kc, 0:128], H[4 + kc][:, :N], start=(kc == 0), stop=(kc == 3))
        QT = H[27][:, :N]
        self.evac(QT, ps[:, :N])
        ps = self.P[pb % 2]
        pb += 1
        for kc in range(4):
            self.mm(ps[0:64, :N], wq[:, kc, 128:192], H[4 + kc][:, :N], start=(kc == 0), stop=(kc == 3))
        QR = A[10][0:64, :N]
        self.evac(QR, ps[0:64, :N])
        QRR = A[11][0:64, :N]
        self.rope(c, QR, QRR, A[18][0:64, :N])
        QRB = H[28][0:64, :N]
        self.cp("pool", QRB, QRR)
        blist = []
        for (key0, nk, bl, msk) in srcs:
            for kb in range(nk // 128):
                blist.append(("hist", key0, nk, bl, msk, kb))
        for kb in range(nblk):
            blist.append(("own", kb))
        nb_tot = len(blist)
        DEN = A[9][:, :N]
        cur = {}

        def emit_st(i):
            d = blist[i]
            if d[0] == "hist":
                _, key0, nk, bl, msk, kb = d
                if kb == 0:
                    sl = self.hslot % 2
                    self.hslot += 1
                    nb = nk // 128
                    self.dma("sp", f"hk{sl}", self.HK[sl][:, 0:nk], (self.Kscr[j, h][:, key0:key0 + nk], bl))
                    self.dma("sp", f"hv{sl}", self.HV[sl][:, 0:nb, :], (self.Vscr[j][:, key0 // 128:key0 // 128 + nb, h * 128:(h + 1) * 128], bl))
                    self.dma("sp", f"hr{sl}", self.HKR[sl][:, 0:nk], (self.KRscr[j][:, key0:key0 + nk], bl))
                    cur["sl"] = sl
                sl = cur["sl"]
                Kv = self.HK[sl][:, kb * 128:(kb + 1) * 128]
                KRv = self.HKR[sl][:, kb * 128:(kb + 1) * 128]
                Vv = self.HV[sl][:, kb, :]
                c0 = 0
            else:
                kb = d[1]
                Kv = H[19 + h][:, kb * 128:(kb + 1) * 128]
                KRv = H[18][0:64, kb * 128:(kb + 1) * 128]
                Vv = self.VT[kb][:, h * 128:(h + 1) * 128]
                if samp:
                    msk = self.AM[:, 4 * TN + 4 * 128:4 * TN + 5 * 128]
                    c0 = 0
                else:
                    c0 = kb * 128
                    msk = self.AM[:, kb * TN + c0:(kb + 1) * TN]
            ps_ = self.P[2 + i % 2]
            self.mm(ps_[:, c0:N], Kv, QT[:, c0:N], start=True, stop=False)
            self.mm(ps_[:, c0:N], KRv, QRB[:, c0:N], start=False, stop=True)
            PT = H[29 + i % 3][:, c0:N]
            self.act(PT, ps_[:, c0:N], AF.Exp, scale=MLA_SCALE)
            if msk is not None:
                self.tt("dve", PT, PT, msk, ALU.mult)
            return (PT, Vv, c0)

        pend = emit_st(0)
        for i in range(nb_tot):
            PT, Vv, c0 = pend
            if i + 1 < nb_tot:
                pend = emit_st(i + 1)
            self.mm(PO[:, c0:N], Vv, PT, start=(i == 0), stop=(i == nb_tot - 1))
            if i == 0:
                self.cp("dve", DEN[:, c0:N], PT)
            else:
                self.tt("dve", DEN[:, c0:N], DEN[:, c0:N], PT, ALU.add)
        self.mm(PD[:, :N], self.ONESF[:], DEN)
        rden = A[8][:, :N]
        self.recip(rden, PD[:, :N])
        self.tt("dve", H[8 + h][:, :N], PO[:, :N], rden, ALU.mult)
    self.out_proj(c, ("mla_o", j))


MK.mla_layer = mla_layer
MK.sample_cache_kv = sample_cache_kv
MK.rope = rope
MK.make_k = make_k
MK.make_v = make_v
MK.spill_k = spill_k
MK.spill_v = spill_v


PROMPT_CORES = [0, 1, 4, 5]
_NC_CACHE = {}


def kernel(**inputs):
    NT = SEQ_FULL // TN
    if "nc" not in _NC_CACHE:
        mk = MK(NT=NT, NLAY=4, sample=True)
        _NC_CACHE["nc"] = mk.build(mixers=True)
    nc = _NC_CACHE["nc"]
    core_seq = [None] * 8
    for b, cidx in enumerate(PROMPT_CORES):
        core_seq[cidx] = b
    core_samp = [[4 * c + i for i in range(4)] for c in range(8)]
    maps = host_inputs(inputs, NT, core_seq, core_samp)
    res = run_bass_kernel_spmd(nc, maps, core_ids=list(range(8)))
    R = res.results
    f32 = np.float32
    y_prompt = np.empty((4, SEQ_FULL, D), f32)
    y_sample = np.empty((32, 16, D), f32)
    p_conv = np.empty((2, 4, 3, 3072), f32)
    p_rec = np.empty((2, 4, 8, 128, 128), f32)
    p_ckv = np.empty((2, 4, SEQ_FULL, 256), f32)
    p_kr = np.empty((2, 4, SEQ_FULL, 64), f32)
    s_conv = np.empty((2, 32, 3, 3072), f32)
    s_rec = np.empty((2, 32, 8, 128, 128), f32)
    s_ckv = np.empty((2, 32, 16, 256), f32)
    s_kr = np.empty((2, 32, 16, 64), f32)
    for b, cidx in enumerate(PROMPT_CORES):
        r = R[cidx]
        y_prompt[b] = r["o_yT"].T
        p_conv[:, b] = np.transpose(r["o_pconv"], (0, 2, 1))
        p_rec[:, b] = r["o_prec"]
        p_ckv[:, b] = np.transpose(r["o_pckv"], (0, 2, 1))
        p_kr[:, b] = np.transpose(r["o_pkr"], (0, 2, 1))
    for c in range(8):
        r = R[c]
        sl = slice(4 * c, 4 * c + 4)
        y_sample[sl] = np.transpose(r["o_ysT"].reshape(D, 4, 32)[:, :, 16:], (1, 2, 0))
        s_conv[:, sl] = np.transpose(r["o_sconv"], (0, 1, 3, 2))
        s_rec[:, sl] = r["o_srec"]
        s_ckv[:, sl] = np.transpose(r["o_sckv"].reshape(2, 256, 4, 32)[..., 16:], (0, 2, 3, 1))
        s_kr[:, sl] = np.transpose(r["o_skr"].reshape(2, 64, 4, 32)[..., 16:], (0, 2, 3, 1))
    return (y_prompt, y_sample, p_conv, p_rec, p_ckv, p_kr, s_conv, s_rec, s_ckv, s_kr)
```

```python
import numpy as np
from contextlib import ExitStack
import concourse.bass as bass
import concourse.mybir as mybir
from concourse.bass_utils import run_bass_kernel_spmd

F32 = mybir.dt.float32
BF16 = mybir.dt.bfloat16
AF = mybir.ActivationFunctionType
ALU = mybir.AluOpType

D = 1024
DFF = 4096
NL = 4
ALPHA = 8.0 ** 0.25
EPS = 1e-6
MLA_SCALE = 192.0 ** -0.5
NEG = -30000.0
SEQ_FULL = 8192
TN = 512
SN = 128
PAST = 1024


class Buf:
    __slots__ = ("lw", "rd")

    def __init__(self):
        self.lw = None
        self.rd = {}


class Prod:
    def __init__(self, name, sem, step):
        self.name = name
        self.sem = sem
        self.step = step
        self.n = 0


class Sched:
    def __init__(self, nc, stack):
        self.nc = nc
        self.stack = stack
        self.h = {"pe": nc.tensor, "act": nc.scalar, "dve": nc.vector, "pool": nc.gpsimd, "sp": nc.sync}
        self.eng = {}
        for k in self.h:
            self.eng[k] = Prod(k, stack.enter_context(nc.semaphore("s_" + k)), 1)
        self.waited = {k: {} for k in self.h}
        self.chan = {}
        self.ninstr = 0

    def channel(self, name):
        if name not in self.chan:
            self.chan[name] = Prod(name, self.stack.enter_context(self.nc.semaphore("c_" + name)), 16)
        return self.chan[name]

    def _need(self, e, reads, writes):
        need = {}
        me = self.eng.get(e)
        for b in reads:
            if b.lw is not None:
                p, i = b.lw
                if need.get(p, 0) < i:
                    need[p] = i
        for b in writes:
            if b.lw is not None:
                p, i = b.lw
                if p is not me and need.get(p, 0) < i:
                    need[p] = i
            for p, i in b.rd.items():
                if p is not me and need.get(p, 0) < i:
                    need[p] = i
        return need

    def _waits(self, e, need):
        w = self.waited[e]
        h = self.h[e]
        for p, i in need.items():
            if w.get(p, 0) < i:
                h.wait_ge(p.sem, i)
                w[p] = i
                self.ninstr += 1

    def op(self, e, fn, reads=(), writes=()):
        need = self._need(e, reads, writes)
        if e == "pe":
            need.pop(self.eng["pe"], None)
        self._waits(e, need)
        prod = self.eng[e]
        inst = fn(self.h[e])
        prod.n += 1
        inst.then_inc(prod.sem, 1)
        self.ninstr += 1
        for b in reads:
            b.rd[prod] = prod.n
        for b in writes:
            b.lw = (prod, prod.n)
            b.rd = {}

    def dma(self, q, chan, out, in_, reads=(), writes=(), **kw):
        c = self.channel(chan)
        need = self._need(q, reads, writes)
        if c.n > 0 and need.get(c, 0) < c.n:
            need[c] = c.n
        self._waits(q, need)
        inst = self.h[q].dma_start(out=out, in_=in_, **kw)
        c.n += 16
        inst.then_inc(c.sem, 16)
        self.ninstr += 1
        for b in reads:
            b.rd[c] = c.n
        for b in writes:
            b.lw = (c, c.n)
            b.rd = {}


class Tl:
    def __init__(self, ap, buf=None, excl=False):
        self.ap = ap
        self.b = buf if buf is not None else Buf()
        self.excl = excl

    def __getitem__(self, idx):
        return Vw(self, self.ap[idx])

    def v(self, ap):
        return Vw(self, ap)


class Vw:
    def __init__(self, tl, ap):
        self.tl = tl
        self.ap = ap

    def __getitem__(self, idx):
        return Vw(self.tl, self.ap[idx])

    def re(self, s, **kw):
        return Vw(self.tl, self.ap.rearrange(s, **kw))

    def bc(self, shape):
        return Vw(self.tl, self.ap.to_broadcast(shape))


def _rw(outs, ins):
    r, w = [], []
    for v in ins:
        if isinstance(v, Vw):
            (w if v.tl.excl else r).append(v.tl.b)
    for v in outs:
        w.append(v.tl.b)
    return r, w


def _a(x):
    return x.ap if isinstance(x, Vw) else x


class TileCtx:
    pass


class MK:
    def __init__(self, NT=16, NLAY=4, sample=True, solve_dt=F32):
        self.NT = NT
        self.NLAY = NLAY
        self.sample = sample
        self.SEQ = NT * TN
        self.SD = solve_dt
        self.nc = bass.Bass("TRN2", target_bir_lowering=False)
        self.st = ExitStack()
        self.S = Sched(self.nc, self.st)
        self.rr = 0
        self.ev = 0
        self.castn = 0
        self._decl()
        self._alloc()

    def din(self, name, shape, dt=F32):
        return self.nc.dram_tensor(name, list(shape), dt, kind="ExternalInput").ap()

    def dout(self, name, shape, dt=F32):
        return self.nc.dram_tensor(name, list(shape), dt, kind="ExternalOutput").ap()

    def dint(self, name, shape, dt=BF16):
        return self.nc.dram_tensor(name, list(shape), dt, kind="Internal").ap()

    def _decl(self):
        SEQ = self.SEQ
        i = self.din
        self.xT = i("xT", [D, SEQ])
        self.pT = i("pT", [NL, 256, SEQ])
        self.xsT = i("xsT", [D, SN])
        self.psT = i("psT", [NL, 256, SN])
        self.sconvT = i("sconvT", [2, 4, 3072, 3])
        self.srec = i("srec", [2, 4, 8, 128, 128])
        self.ckvT = i("ckvT", [2, 4, 256, PAST])
        self.krT = i("krT", [2, 4, 64, PAST])
        self.w = {
            "up": i("mlp_w_up", [NL, D, DFF]), "down": i("mlp_w_down", [NL, DFF, D]),
            "pproj": i("ple_w_proj", [NL, 256, D]), "gate": i("ple_w_gate", [NL, D, D]),
            "dn_in": i("dn_w_in", [2, D, 4112]), "dn_o": i("dn_w_o", [2, D, D]),
            "mla_in": i("mla_w_in", [2, D, 832]), "mla_uq": i("mla_w_uq", [2, 512, 1536]),
            "mla_uk": i("mla_w_uk", [2, 256, 1024]), "mla_uv": i("mla_w_uv", [2, 256, 1024]),
            "mla_o": i("mla_w_o", [2, D, D]),
        }
        self.wb = {k: self.dint("wb_" + k, v.shape) for k, v in self.w.items()}
        self.wbuf = {k: [] for k in self.w}
        self.vec_d = i("vec1024", [128, NL * 5 * 8])
        self.convw_d = i("convw", [128, 2 * 24 * 4])
        self.qn_d = i("qnorm", [128, 2 * 4])
        self.kvn_d = i("kvnorm", [128, 2 * 2])
        self.on_d = i("onorm", [128, 2])
        self.alog_d = i("alog", [8, 2])
        self.dtb_d = i("dtb", [8, 2])
        self.ident_d = i("ident", [128, 128])
        self.oh8_d = i("oh8", [8, 8])
        self.mb_d = i("maskbias", [128, 4 * 128])
        self.gm_d = i("gatemask", [8, TN + 2 * SN])
        self.tokm_d = i("tokmask", [128, 8])
        self.am_d = i("attnmask", [128, 4 * TN + 18 * 128])
        self.prot_d = i("prot", [64, 64])
        self.ropeC = i("ropeC", [64, SEQ + SN])
        self.ropeS = i("ropeS", [64, SEQ + SN])
        o = self.dout
        self.o_yT = o("o_yT", [D, SEQ])
        self.o_ysT = o("o_ysT", [D, SN])
        self.o_pconv = o("o_pconv", [2, 3072, 3])
        self.o_prec = o("o_prec", [2, 8, 128, 128])
        self.o_pckv = o("o_pckv", [2, 256, SEQ])
        self.o_pkr = o("o_pkr", [2, 64, SEQ])
        self.o_sconv = o("o_sconv", [2, 4, 3072, 3])
        self.o_srec = o("o_srec", [2, 4, 8, 128, 128])
        self.o_sckv = o("o_sckv", [2, 256, SN])
        self.o_skr = o("o_skr", [2, 64, SN])
        self.obufs = []
        KS = SEQ + 4 * PAST
        self.KS = KS
        self.Kscr = self.dint("Kscr", [2, 8, 128, KS])
        self.Vscr = self.dint("Vscr", [2, 128, KS // 128, 1024])
        self.KRscr = self.dint("KRscr", [2, 64, KS])
        self.kvbuf = {}

    def sb(self, name, shape, dt):
        return self.st.enter_context(self.nc.sbuf_tensor(name, list(shape), dt))

    def _alloc(self):
        nc = self.nc
        sb = self.sb
        yt = sb("Y", [128, 8, TN], F32)
        self._ytensor = yt
        self.Y = [Tl(yt[:, c]) for c in range(8)]
        ybt = sb("YB", [128, 8, TN], BF16)
        self.YB = [Tl(ybt[:, c]) for c in range(8)]
        NA = 20
        at = sb("A", [128, NA, TN], F32)
        self.A = [Tl(at[:, c]) for c in range(NA)]
        self.Awide = at
        ht = sb("H", [128, 32, TN], BF16)
        self.H = [Tl(ht[:, c]) for c in range(32)]
        self.Ht = ht
        NBX = 10
        bt = sb("BX", [128, NBX, TN], BF16)
        self.BX = [Tl(bt[:, c]) for c in range(NBX)]
        self.BXt = bt
        self.VT = [Tl(bt[:, 2 * kb:2 * kb + 2, :].rearrange("p a n -> p (a n)")) for kb in range(4)]
        NM = 34
        mt = sb("M", [128, NM, 128], BF16)
        self.M = [Tl(mt[:, c]) for c in range(NM)]
        self.mi = 0
        mf = sb("MF", [128, 31, 128], BF16)
        self.MF = [Tl(mf[:, c]) for c in range(31)]
        xs = sb("XS", [128, 3, TN + 8], F32)
        self.XS = [Tl(xs[:, c]) for c in range(3)]
        self.NSLOT = 4
        rt = sb("RING", [128, self.NSLOT, 4096], BF16)
        self.ring = [Tl(rt[:, c]) for c in range(self.NSLOT)]
        hk = sb("HK", [128, 2, 1024], BF16)
        hv = sb("HV", [128, 2, 8, 128], BF16)
        hr = sb("HKR", [64, 2, 1024], BF16)
        self.HK = [Tl(hk[:, c]) for c in range(2)]
        self.HV = [Tl(hv[:, c]) for c in range(2)]
        self.HKR = [Tl(hr[:, c]) for c in range(2)]
        self.hslot = 0
        self.P = [Tl(self.st.enter_context(nc.psum_tensor(f"P{i}", [128, TN], F32))[:], excl=True) for i in range(8)]
        self.IDENT = Tl(sb("IDENT", [128, 128], F32)[:])
        self.IDENTB = Tl(sb("IDENTB", [128, 128], BF16)[:])
        ssb = sb("SSB", [128, 2, 128], BF16)
        self.SSB = [Tl(ssb[:, k]) for k in range(2)]
        sbb = sb("SBs", [128, 4, 128], BF16)
        self.SB = [Tl(sbb[:, c]) for c in range(4)]
        self.ONESB = Tl(sb("ONESB", [128, 128], BF16)[:])
        self.ONESF = Tl(sb("ONESF", [128, 128], F32)[:])
        self.OH8 = Tl(sb("OH8", [8, 8], F32)[:])
        self.ONES8 = Tl(sb("ONES8", [8, 128], F32)[:])
        self.MB = Tl(sb("MB", [128, 4 * 128], F32)[:])
        self.GM = Tl(sb("GM", [8, TN + 2 * SN], F32)[:])
        self.TOKM = Tl(sb("TOKM", [128, 8], F32)[:])
        self.AM = Tl(sb("AM", [128, 4 * TN + 18 * 128], BF16)[:])
        self.PROT = Tl(sb("PROT", [64, 64], F32)[:])
        self.COS = Tl(sb("COS", [64, TN], F32)[:])
        self.SIN = Tl(sb("SIN", [64, TN], F32)[:])
        self.VEC = Tl(sb("VEC", [128, NL * 5 * 8], F32)[:])
        self.CONVW = Tl(sb("CONVW", [128, 2 * 24 * 4], F32)[:])
        self.QN = Tl(sb("QN", [128, 8], F32)[:])
        self.KVN = Tl(sb("KVN", [128, 4], F32)[:])
        self.ON = Tl(sb("ON", [128, 2], F32)[:])
        self.ALOG = Tl(sb("ALOG", [8, 2], F32)[:])
        self.NEGA = Tl(sb("NEGA", [8, 2], F32)[:])
        self.DTB = Tl(sb("DTB", [8, 2], F32)[:])
        sst = sb("SST", [128, 2, 8, 128], F32)
        self.SST = [[Tl(sst[:, j, h]) for h in range(8)] for j in range(2)]
        self.SSTt = sst
        halo = sb("HALO", [128, 2, 24, 3], F32)
        self.HALO = [[Tl(halo[:, j, c]) for c in range(24)] for j in range(2)]
        self.HALOt = halo
        ts = sb("TS", [128, 4, 32], F32)
        self.TS = [Tl(ts[:, b]) for b in range(4)]
        self.DKS = Tl(sb("DKS", [128, 32], F32)[:])
        eg = sb("EGLB", [128, 2, 16], F32)
        self.EGLB2 = [Tl(eg[:, k]) for k in range(2)]
        s0 = sb("S0", [128, 4, 128], F32)
        self.S0 = [Tl(s0[:, c]) for c in range(4)]
        self.EGLT = Tl(sb("EGLT", [8, 16], F32)[:])
        s1 = sb("S1", [128, 2, 128], F32)
        self.S1 = [Tl(s1[:, c]) for c in range(2)]
        self.s0i = 0

    def mm(self, out, lhsT, rhs, start=True, stop=True, **kw):
        r, w = _rw([out], [lhsT, rhs])
        self.S.op("pe", lambda t: t.matmul(out.ap, lhsT.ap, rhs.ap, start=start, stop=stop, **kw), r, w)

    def tr(self, out, in_, ident):
        r, w = _rw([out], [in_, ident])
        self.S.op("pe", lambda t: t.transpose(out.ap, in_.ap, ident.ap), r, w)

    def act(self, out, in_, func, scale=1.0, bias=None):
        r, w = _rw([out], [in_, scale, bias])
        kw = {}
        if bias is not None:
            kw["bias"] = _a(bias)
        self.S.op("act", lambda a: a.activation(out=out.ap, in_=in_.ap, func=func, scale=_a(scale), **kw), r, w)

    def tt(self, e, out, in0, in1, op):
        r, w = _rw([out], [in0, in1])
        self.S.op(e, lambda v: v.tensor_tensor(out.ap, in0.ap, in1.ap, op), r, w)

    def ts(self, e, out, in0, s1, op0, s2=None, op1=None):
        r, w = _rw([out], [in0, s1, s2])
        if op1 is None:
            self.S.op(e, lambda v: v.tensor_scalar(out.ap, in0.ap, _a(s1), None, op0), r, w)
        else:
            self.S.op(e, lambda v: v.tensor_scalar(out.ap, in0.ap, _a(s1), _a(s2), op0, op1), r, w)

    def stt(self, out, in0, scalar, in1, op0, op1):
        r, w = _rw([out], [in0, scalar, in1])
        self.S.op("dve", lambda v: v.scalar_tensor_tensor(out.ap, in0.ap, _a(scalar), in1.ap, op0, op1), r, w)

    def cp(self, e, out, in_):
        r, w = _rw([out], [in_])
        if e == "act":
            self.S.op("act", lambda a: a.activation(out=out.ap, in_=in_.ap, func=AF.Copy), r, w)
        else:
            self.S.op(e, lambda v: v.tensor_copy(out.ap, in_.ap), r, w)

    def evac(self, out, in_):
        self.ev += 1
        self.cp("act" if self.ev % 2 else "dve", out, in_)

    def recip(self, out, in_):
        r, w = _rw([out], [in_])
        self.S.op("dve", lambda v: v.reciprocal(out.ap, in_.ap), r, w)

    def memset(self, e, out, val):
        r, w = _rw([out], [])
        self.S.op(e, lambda v: v.memset(out.ap, val), r, w)

    def dma(self, q, chan, out, in_, **kw):
        reads, writes = [], []
        if isinstance(in_, Vw):
            reads.append(in_.tl.b)
            ia = in_.ap
        else:
            ia, bl = in_
            reads += bl
        if isinstance(out, Vw):
            writes.append(out.tl.b)
            oa = out.ap
        else:
            oa, bl = out
            writes += bl
        self.S.dma(q, chan, oa, ia, reads=reads, writes=writes, **kw)

    def newM(self):
        m = self.M[self.mi % len(self.M)]
        self.mi += 1
        return m

    def R(self):
        i = self.rr
        self.rr += 1
        bank = self.P[(3, 4, 6, 7)[i % 4]]
        c = ((i // 4) % 4) * 128
        return bank[:, c:c + 128]

    def Rb(self):
        r = self.R()
        return Vw(r.tl, r.ap[:, 0:64].bitcast(BF16))

    def cast_weights(self):
        order = ["dn_in", "dn_o", "up", "down", "pproj", "gate", "mla_in", "mla_uk", "mla_uv", "mla_uq", "mla_o"]
        for k in order:
            src = self.w[k]
            dst = self.wb[k]
            L = src.shape[0]
            R_ = src.shape[1]
            if k == "dn_in":
                for l in range(L):
                    for h in range(8):
                        b = Buf()
                        self.wbuf[k].append((l, b))
                        sv = src[l, :, 0:4096].rearrange("r (g h n) -> r g h n", g=4, h=8)[:, :, h, :]
                        dv = dst[l, :, h * 512:(h + 1) * 512].rearrange("r (g n) -> r g n", g=4)
                        self.S.dma("pool", f"cast{self.castn % 4}", dv, sv, writes=[b])
                        self.castn += 1
                    b = Buf()
                    self.wbuf[k].append((l, b))
                    self.S.dma("pool", f"cast{self.castn % 4}", dst[l, :, 4096:4112], src[l, :, 4096:4112], writes=[b])
                    self.castn += 1
                continue
            for l in range(L):
                nsplit = max(1, (R_ * src.shape[2]) // (1 << 20))
                rs = R_ // nsplit
                for s in range(nsplit):
                    b = Buf()
                    self.wbuf[k].append((l, b))
                    self.S.dma("pool", f"cast{self.castn % 4}", dst[l, s * rs:(s + 1) * rs, :], src[l, s * rs:(s + 1) * rs, :], writes=[b])
                    self.castn += 1

    def piece_src(self, key):
        k = key[0]
        l = key[1]
        wbv = self.wb[k][l]
        if k in ("up", "gate", "dn_o", "mla_o"):
            j = key[2]
            v = wbv.rearrange("(c p) n -> p c n", p=128)[:, :, j * 512:(j + 1) * 512]
            return v, [8, 512]
        if k == "down":
            m = key[2]
            v = wbv.rearrange("(c p) n -> p c n", p=128)[:, :, m * 128:(m + 1) * 128]
            return v, [32, 128]
        if k == "pproj":
            return wbv.rearrange("(c p) n -> p c n", p=128), [2, 1024]
        if k == "dn_in":
            h = key[2]
            if h == "ba":
                v = wbv.rearrange("(c p) n -> p c n", p=128)[:, :, 4096:4112]
                return v, [8, 16]
            v = wbv.rearrange("(c p) n -> p c n", p=128)[:, :, h * 512:(h + 1) * 512]
            return v, [8, 4, 128]
        if k == "mla_in":
            if key[2] == 0:
                return wbv.rearrange("(c p) n -> p c n", p=128)[:, :, 0:512], [8, 512]
            return wbv.rearrange("(c p) n -> p c n", p=128)[:, :, 512:832], [8, 320]
        if k == "mla_uq":
            h = key[2]
            return wbv.rearrange("(c p) n -> p c n", p=128)[:, :, h * 192:(h + 1) * 192], [4, 192]
        if k in ("mla_uk", "mla_uv"):
            return wbv.rearrange("(c p) n -> p c n", p=128), [2, 1024]
        raise KeyError(key)

    def plan_layer(self, l):
        j = l // 2
        pl = []
        if l % 2 == 0:
            pl.append(("dn_in", j, "ba"))
            pl += [("dn_in", j, h) for h in range(8)]
            pl += [("dn_o", j, 0), ("dn_o", j, 1)]
        else:
            pl += [("mla_in", j, 0), ("mla_in", j, 1), ("mla_uk", j), ("mla_uv", j)]
            pl += [("mla_uq", j, h) for h in range(8)]
            pl += [("mla_o", j, 0), ("mla_o", j, 1)]
        pl += [("up", l, jj) for jj in range(8)]
        pl += [("down", l, m) for m in range(8)]
        pl += [("pproj", l), ("gate", l, 0), ("gate", l, 1)]
        return pl

    def set_plan(self, plan):
        self.plan = plan
        self.pi = 0
        self.pl = 0

    def _load_piece(self, idx):
        key = self.plan[idx]
        src, shp = self.piece_src(key)
        slot = self.ring[idx % self.NSLOT]
        n = int(np.prod(shp))
        dst = slot[:, 0:n].re("p (c n) -> p c n", c=shp[0])
        bl = [b for (l, b) in self.wbuf[key[0]] if l == key[1]]
        self.dma("sp", f"w{idx % self.NSLOT}", dst, (src, bl))

    def wnext(self, key):
        assert self.plan[self.pi] == key, (self.plan[self.pi], key)
        while self.pl < len(self.plan) and self.pl < self.pi + self.NSLOT:
            self._load_piece(self.pl)
            self.pl += 1
        idx = self.pi
        self.pi += 1
        _, shp = self.piece_src(key)
        slot = self.ring[idx % self.NSLOT]
        n = int(np.prod(shp))
        v = slot[:, 0:n]
        if len(shp) == 2:
            return v.re("p (c n) -> p c n", c=shp[0])
        return v.re("p (c g n) -> p c g n", c=shp[0], g=shp[1])

    def setup(self):
        ld = lambda tl, d: self.dma("sp", "const", tl[:], (d, []))
        ld(self.IDENT, self.ident_d[:, :])
        ld(self.OH8, self.oh8_d[:, :])
        self.memset("dve", self.ONES8[:], 1.0)
        ld(self.MB, self.mb_d[:, :])
        ld(self.GM, self.gm_d[:, :])
        ld(self.TOKM, self.tokm_d[:, :])
        AMW = 4 * TN + 18 * 128
        amf = self.Awide[:, 0:9, :].rearrange("p a n -> p (a n)")[:, 0:AMW]
        self.S.dma("sp", "const", amf, self.am_d[:, :], reads=[], writes=[self.A[k].b for k in range(9)])
        ld(self.PROT, self.prot_d[:, :])
        ld(self.VEC, self.vec_d[:, :])
        ld(self.CONVW, self.convw_d[:, :])
        ld(self.QN, self.qn_d[:, :])
        ld(self.KVN, self.kvn_d[:, :])
        ld(self.ON, self.on_d[:, :])
        ld(self.ALOG, self.alog_d[:, :])
        ld(self.DTB, self.dtb_d[:, :])
        self.S.op("dve", lambda v: v.tensor_copy(self.AM.ap[:, :], amf), reads=[self.A[k].b for k in range(9)], writes=[self.AM.b])
        self.cp("dve", self.IDENTB[:], self.IDENT[:])
        self.memset("dve", self.ONESB[:], 1.0)
        self.memset("dve", self.ONESF[:], 1.0)
        for g in range(3):
            self.memset("pool", self.XS[g][:, :], 0.0)
        self.act(self.NEGA[:], self.ALOG[:], AF.Exp)
        self.ts("dve", self.NEGA[:], self.NEGA[:], -1.0, ALU.mult)
        for j in range(2):
            for h in range(8):
                self.memset("dve", self.SST[j][h][:], 0.0)
            for cc in range(24):
                self.memset("pool", self.HALO[j][cc][:], 0.0)

    def vec(self, l, kind, c):
        i = (l * 5 + kind) * 8 + c
        return self.VEC[:, i:i + 1]

    def prompt_ctx(self, t):
        c = TileCtx()
        c.sample = False
        c.t = t
        c.N = TN
        c.pos0 = t * TN
        c.nblk = 4
        c.U = 128
        c.nunits = 4
        c.last = (t == self.NT - 1)
        c.L = 6
        return c

    def sample_ctx(self):
        c = TileCtx()
        c.sample = True
        c.t = self.NT
        c.N = SN
        c.pos0 = self.SEQ
        c.nblk = 1
        c.U = 32
        c.nunits = 4
        c.last = True
        c.L = 4
        return c

    def load_tile(self, c):
        N = c.N
        src = (self.xsT if c.sample else self.xT).rearrange("(c p) n -> p c n", p=128)
        src = src[:, :, 0:N] if c.sample else src[:, :, c.pos0:c.pos0 + N]
        self.S.dma("sp", "ldx", self._ytensor[:, :, 0:N], src, reads=[], writes=[y.b for y in self.Y])
        for k in range(8):
            self.cp("pool", self.YB[k][:, :N], self.Y[k][:, :N])
        self.dma("sp", "ldrope", self.COS[:, :N], (self.ropeC[:, c.pos0:c.pos0 + N], []))
        self.dma("sp", "ldrope", self.SIN[:, :N], (self.ropeS[:, c.pos0:c.pos0 + N], []))

    def bstats(self, c, srcs, want_mean, nfeat, PSs, PSq):
        N = c.N
        n = len(srcs)
        for k, s in enumerate(srcs):
            sq = self.H[k % 4][:, :N]
            self.act(sq, s, AF.Square)
            self.mm(PSq[:, :N], self.ONESB[:], sq, start=(k == 0), stop=(k == n - 1))
            if want_mean:
                rb = self.H[4 + k % 4][:, :N]
                self.cp("dve", rb, s)
                self.mm(PSs[:, :N], self.ONESB[:], rb, start=(k == 0), stop=(k == n - 1))
        A = self.A
        inv = 1.0 / nfeat
        if want_mean:
            self.act(A[0][:, :N], PSs[:, :N], AF.Copy, scale=inv)
            self.tt("dve", A[1][:, :N], A[0][:, :N], A[0][:, :N], ALU.mult)
            self.stt(A[2][:, :N], PSq[:, :N], inv, A[1][:, :N], ALU.mult, ALU.subtract)
            self.act(A[2][:, :N], A[2][:, :N], AF.Ln, bias=EPS)
        else:
            self.act(A[2][:, :N], PSq[:, :N], AF.Ln, scale=inv, bias=EPS)
        self.act(A[3][:, :N], A[2][:, :N], AF.Exp, scale=-0.5)
        return A[3]

    def layer_norm(self, c, l, kg, kb):
        N = c.N
        A = self.A
        rstd = self.bstats(c, [self.Y[k][:, :N] for k in range(8)], True, float(D), self.P[2], self.P[3])
        for k in range(8):
            t = A[4 + k % 2][:, :N]
            self.tt("dve", t, self.Y[k][:, :N], A[0][:, :N], ALU.subtract)
            self.tt("dve", t, t, rstd[:, :N], ALU.mult)
            self.act(self.Y[k][:, :N], t, AF.Identity, scale=self.vec(l, kg, k), bias=self.vec(l, kb, k))
            self.act(self.YB[k][:, :N], t, AF.Identity, scale=self.vec(l, kg, k), bias=self.vec(l, kb, k))

    def residual(self, c, k, ps):
        N = c.N
        self.stt(self.Y[k][:, :N], self.Y[k][:, :N], ALPHA, ps, ALU.mult, ALU.add)

    def finish(self, c, l):
        N = c.N
        A = self.A
        self.layer_norm(c, l, 0, 1)
        pb = 0
        for jj in range(8):
            wu = self.wnext(("up", l, jj))
            for m in range(4):
                ff = jj * 4 + m
                ps = self.P[pb % 2]
                pb += 1
                for kc in range(8):
                    self.mm(ps[:, :N], wu[:, kc, m * 128:(m + 1) * 128], self.YB[kc][:, :N], start=(kc == 0), stop=(kc == 7))
                r = A[6 + ff % 2][:, :N]
                self.act(r, ps[:, :N], AF.Relu)
                self.tt("dve" if ff % 2 else "pool", self.H[ff][:, :N], r, r, ALU.mult)
        for m in range(8):
            wd = self.wnext(("down", l, m))
            ps = self.P[pb % 2]
            pb += 1
            for kc in range(32):
                self.mm(ps[:, :N], wd[:, kc, :], self.H[kc][:, :N], start=(kc == 0), stop=(kc == 31))
            self.residual(c, m, ps[:, :N])
        self.layer_norm(c, l, 2, 3)
        psrc = (self.psT if c.sample else self.pT)[l].rearrange("(c p) n -> p c n", p=128)
        psrc = psrc[:, :, 0:N] if c.sample else psrc[:, :, c.pos0:c.pos0 + N]
        PT = [A[18], A[19]]
        self.S.dma("sp", "ldp", self.Awide[:, 18:20, 0:N], psrc, reads=[], writes=[PT[0].b, PT[1].b])
        PB = [self.BX[8], self.BX[9]]
        for kc in range(2):
            self.cp("pool", PB[kc][:, :N], PT[kc][:, :N])
        wpj = self.wnext(("pproj", l))
        E = [A[8 + m] for m in range(8)]
        for m in range(8):
            ps = self.P[pb % 2]
            pb += 1
            for kc in range(2):
                self.mm(ps[:, :N], wpj[:, kc, m * 128:(m + 1) * 128], PB[kc][:, :N], start=(kc == 0), stop=(kc == 1))
            self.cp("act", E[m][:, :N], ps[:, :N])
        rstd = self.bstats(c, [E[m][:, :N] for m in range(8)], False, float(D), self.P[2], self.P[3])
        for jj in range(2):
            wg = self.wnext(("gate", l, jj))
            for m4 in range(4):
                m = jj * 4 + m4
                ps = self.P[pb % 2]
                pb += 1
                for kc in range(8):
                    self.mm(ps[:, :N], wg[:, kc, m4 * 128:(m4 + 1) * 128], self.YB[kc][:, :N], start=(kc == 0), stop=(kc == 7))
                gt = A[16 + m % 2][:, :N]
                self.act(gt, ps[:, :N], AF.Sigmoid)
                t = A[4 + m % 2][:, :N]
                self.tt("pool" if m % 2 else "dve", t, E[m][:, :N], rstd[:, :N], ALU.mult)
                self.tt("dve", t, t, gt, ALU.mult)
                self.stt(self.Y[m][:, :N], t, self.vec(l, 4, m), self.Y[m][:, :N], ALU.mult, ALU.add)
        for m in range(8):
            self.cp("act" if m % 2 else "dve", self.YB[m][:, :N], self.Y[m][:, :N])

    def store(self, dram_ap, src, chan="st"):
        b = Buf()
        self.obufs.append((chan, b))
        self.dma("pool", chan, (dram_ap, [b]), src)

    def store_y(self, c):
        N = c.N
        dst = (self.o_ysT if c.sample else self.o_yT).rearrange("(c p) n -> p c n", p=128)
        dst = dst[:, :, 0:N] if c.sample else dst[:, :, c.pos0:c.pos0 + N]
        b = Buf()
        self.S.dma("pool", "sty", dst, self._ytensor[:, :, 0:N], reads=[y.b for y in self.Y], writes=[b])

    def build(self, mixers=True):
        self.mixers = mixers
        self.setup()
        self.cast_weights()
        tiles = [self.prompt_ctx(t) for t in range(self.NT)]
        if self.sample:
            tiles.append(self.sample_ctx())
        plan = []
        for c in tiles:
            for l in range(self.NLAY):
                pl = self.plan_layer(l)
                if not mixers:
                    pl = [k for k in pl if k[0] in ("up", "down", "pproj", "gate")]
                plan += pl
        if self.sample and mixers:
            pre = []
            for j in range(2):
                if 2 * j + 1 < self.NLAY:
                    pre += [("mla_uk", j), ("mla_uv", j)]
            plan = pre + plan
        self.set_plan(plan)
        if self.sample and mixers:
            self.sample_cache_kv()
        for c in tiles:
            self.load_tile(c)
            for l in range(self.NLAY):
                if mixers:
                    if l % 2 == 0:
                        self.dn_layer(c, l)
                    else:
                        self.mla_layer(c, l)
                else:
                    for k in range(8):
                        self.ts("dve", self.Y[k][:, :c.N], self.Y[k][:, :c.N], ALPHA, ALU.mult)
                self.finish(c, l)
            self.store_y(c)
        for name, ch in self.S.chan.items():
            if name.startswith("st"):
                self.nc.gpsimd.wait_ge(ch.sem, ch.n)
        self.st.close()
        return self.nc


def _consts(SEQ):
    c = {}
    c["ident"] = np.eye(128, dtype=np.float32)
    c["oh8"] = np.eye(8, dtype=np.float32)
    j = np.arange(128)[:, None]
    i = np.arange(128)[None, :]
    p_incl = np.where(i >= j, 0.0, NEG)
    p_strict = np.where(i > j, 0.0, NEG)
    same = (i // 32) == (j // 32)
    jreal = (j % 32) >= 16
    s_incl = np.where(same & jreal & (i >= j), 0.0, NEG)
    s_strict = np.where(same & jreal & (i > j), 0.0, NEG)
    c["maskbias"] = np.concatenate([p_incl, p_strict, s_incl, s_strict], 1).astype(np.float32)
    tt = np.arange(TN)
    reset_p = np.where(tt % 128 == 0, 0.0, 1.0)
    reset_s = np.where(tt % 32 == 0, 0.0, 1.0)[:SN]
    real_s = np.where((tt % 32) >= 16, 1.0, 0.0)[:SN]
    gm = np.concatenate([reset_p, reset_s, real_s], 0).reshape(1, TN + 2 * SN)
    c["gatemask"] = np.repeat(gm, 8, 0).astype(np.float32)
    tok = np.arange(128)
    tokm = np.zeros((128, 8), np.float32)
    tokm[:, 0] = (tok % 32) >= 16
    for s in range(4):
        tokm[:, 1 + s] = ((tok // 32) == s) & ((tok % 32) >= 16)
    c["tokmask"] = tokm
    am = np.zeros((128, 4 * TN + 18 * 128), np.float32)
    q = np.arange(TN)[None, :]
    for kb in range(4):
        kp = kb * 128 + np.arange(128)[:, None]
        am[:, kb * TN:(kb + 1) * TN] = (kp // 64) <= (q // 64)
    qs = np.arange(128)[None, :]
    for s in range(4):
        am[:, 4 * TN + s * 128:4 * TN + (s + 1) * 128] = np.broadcast_to((qs // 32) == s, (128, 128))
    ks = np.arange(128)[:, None]
    am[:, 4 * TN + 4 * 128:4 * TN + 5 * 128] = ((ks // 32) == (qs // 32)) & ((ks % 32) >= 16)
    o0 = 4 * TN + 5 * 128
    ri = np.arange(128)[:, None]
    ci = np.arange(128)[None, :]
    am[:, o0:o0 + 128] = (ri // 2) == (ci // 2)
    for lev, s_ in enumerate((2, 4, 8, 16, 32, 64)):
        ML = ((ri // (2 * s_)) == (ci // (2 * s_))) & ((ri % (2 * s_)) >= s_) & ((ci % (2 * s_)) < s_)
        am[:, o0 + (1 + 2 * lev) * 128:o0 + (2 + 2 * lev) * 128] = ML
        am[:, o0 + (2 + 2 * lev) * 128:o0 + (3 + 2 * lev) * 128] = ML.T
    c["attnmask"] = am
    prot = np.zeros((64, 64), np.float32)
    for m in range(32):
        prot[m + 32, m] = -1.0
        prot[m, m + 32] = 1.0
    c["prot"] = prot
    half = 32
    inv_freq = (10000.0 ** (-np.arange(half, dtype=np.float32) / half)).astype(np.float32)
    pos = np.concatenate([np.arange(SEQ), np.tile(np.concatenate([np.zeros(16), PAST + np.arange(16)]), 4)]).astype(np.float32)
    ang = pos[None, :] * inv_freq[:, None]
    c["ropeC"] = np.concatenate([np.cos(ang), np.cos(ang)], 0).astype(np.float32)
    c["ropeS"] = np.concatenate([np.sin(ang), np.sin(ang)], 0).astype(np.float32)
    return c


def _fm(v, nchunk):
    return np.ascontiguousarray(np.moveaxis(v.reshape(v.shape[:-1] + (nchunk, 128)), -1, 0))


def host_inputs(inp, NT, core_seq, core_samp):
    SEQ = NT * TN
    f = lambda a: np.ascontiguousarray(np.asarray(a, dtype=np.float32))
    cst = _consts(SEQ)
    shared = dict(cst)
    for k in ("mlp_w_up", "mlp_w_down", "ple_w_proj", "ple_w_gate", "dn_w_in", "dn_w_o", "mla_w_in", "mla_w_uq", "mla_w_o"):
        shared[k] = f(inp[k])
    shared["mla_w_uk"] = f(inp["mla_w_uk"]).reshape(2, 256, 1024)
    shared["mla_w_uv"] = f(inp["mla_w_uv"]).reshape(2, 256, 1024)
    vec = np.stack([f(inp[k]) for k in ("ln1_g", "ln1_b", "ln2_g", "ln2_b", "ple_norm")], 1)
    shared["vec1024"] = _fm(vec, 8).reshape(128, NL * 5 * 8)
    shared["convw"] = np.ascontiguousarray(np.transpose(f(inp["dn_conv_w"]).reshape(2, 4, 24, 128), (3, 0, 2, 1))).reshape(128, 2 * 24 * 4)
    shared["qnorm"] = _fm(f(inp["mla_q_norm"]), 4).reshape(128, 8)
    shared["kvnorm"] = _fm(f(inp["mla_kv_norm"]), 2).reshape(128, 4)
    shared["onorm"] = np.ascontiguousarray(f(inp["dn_o_norm"]).T)
    shared["alog"] = np.ascontiguousarray(f(inp["dn_a_log"]).T)
    shared["dtb"] = np.ascontiguousarray(f(inp["dn_dt_bias"]).T)
    maps = []
    xp = f(inp["x_prompt"])
    pp = f(inp["p_prompt"])
    xs = f(inp["x_sample"])
    ps = f(inp["p_sample"])
    for c in range(len(core_seq)):
        m = dict(shared)
        b = core_seq[c]
        if b is None:
            m["xT"] = np.zeros((D, SEQ), np.float32)
            m["pT"] = np.zeros((NL, 256, SEQ), np.float32)
        else:
            m["xT"] = np.ascontiguousarray(xp[b, :SEQ].T)
            m["pT"] = np.ascontiguousarray(np.transpose(pp[:, b, :SEQ], (0, 2, 1)))
        ss = core_samp[c]
        xsT = np.zeros((D, 4, 32), np.float32)
        psT = np.zeros((NL, 256, 4, 32), np.float32)
        for i, s in enumerate(ss):
            xsT[:, i, 16:] = xs[s].T
            psT[:, :, i, 16:] = np.transpose(ps[:, s], (0, 2, 1))
        m["xsT"] = xsT.reshape(D, SN)
        m["psT"] = psT.reshape(NL, 256, SN)
        m["sconvT"] = np.ascontiguousarray(np.transpose(f(inp["state_dn_conv"])[:, ss], (0, 1, 3, 2)))
        m["srec"] = np.ascontiguousarray(f(inp["state_dn_recurrent"])[:, ss])
        m["ckvT"] = np.ascontiguousarray(np.transpose(f(inp["cache_mla_ckv"])[:, ss], (0, 1, 3, 2)))
        m["krT"] = np.ascontiguousarray(np.transpose(f(inp["cache_mla_krope"])[:, ss], (0, 1, 3, 2)))
        maps.append(m)
    return maps


def dn_layer(self, c, l):
    j = l // 2
    N = c.N
    A = self.A
    H = self.H
    nblk = c.nblk
    U = c.U
    nun = N // U
    samp = c.sample
    mbI = self.MB[:, (2 if samp else 0) * 128:(3 if samp else 1) * 128]
    mbS = self.MB[:, (3 if samp else 1) * 128:(4 if samp else 2) * 128]
    reset = self.GM[:, TN:TN + SN] if samp else self.GM[:, 0:TN]
    realT = self.GM[:, TN + SN:TN + 2 * SN]
    wba = self.wnext(("dn_in", j, "ba"))
    pb_, pa_ = self.P[2], self.P[3]
    for kc in range(8):
        self.mm(pb_[0:8, :N], wba[:, kc, 0:8], self.YB[kc][:, :N], start=(kc == 0), stop=(kc == 7))
    for kc in range(8):
        self.mm(pa_[0:8, :N], wba[:, kc, 8:16], self.YB[kc][:, :N], start=(kc == 0), stop=(kc == 7))
    BETA, LNB, G, GC, GCB, EGC, DKD, NGC, NS1, TMP = [A[k][0:8, :N] for k in range(10)]
    self.act(BETA, pb_[0:8, :N], AF.Sigmoid)
    self.act(LNB, BETA, AF.Ln)
    self.act(G, pa_[0:8, :N], AF.Exp, bias=self.DTB[:, j:j + 1])
    self.act(G, G, AF.Ln, bias=1.0)
    self.ts("dve", G, G, self.NEGA[:, j:j + 1], ALU.mult)
    if samp:
        self.tt("dve", G, G, realT, ALU.mult)
    r, w = _rw([GC], [reset, G])
    self.S.op("dve", lambda v: v.tensor_tensor_scan(GC.ap, reset.ap, G.ap, 0.0, ALU.mult, ALU.add), r, w)
    self.tt("dve", GCB, GC, LNB, ALU.add)
    self.act(EGC, GC, AF.Exp)
    gc3 = GC.re("p (u k) -> p u k", k=U)
    gl = gc3[:, :, U - 1:U]
    self.tt("dve", TMP.re("p (u k) -> p u k", k=U), gl.bc([8, nun, U]), gc3, ALU.subtract)
    self.act(DKD, TMP, AF.Exp)
    if samp:
        self.tt("dve", DKD, DKD, realT, ALU.mult)
    EGL = self.EGLT[:, 0:nun]
    self.act(EGL.re("p (u k) -> p u k", k=1), gl, AF.Exp)
    self.ts("dve", NGC, GC, -1.0, ALU.mult)
    self.act(NS1, GCB, AF.Exp)
    self.ts("dve", NS1, NS1, -1.0, ALU.mult)
    for b in range(nblk):
        cb = slice(b * 128, (b + 1) * 128)
        ps = self.P[6 + b % 2]
        for q, src in enumerate((NGC, NS1, BETA, DKD)):
            self.tr(ps[:, q * 8:(q + 1) * 8], src[:, cb], self.IDENT[0:8, 0:8])
        self.cp("dve", self.TS[b][:, :], ps[:, 0:32])
    if samp:
        for s in range(4):
            self.ts("dve", self.DKS[:, s * 8:(s + 1) * 8], self.TS[0][:, 24:32], self.TOKM[:, 1 + s:2 + s], ALU.mult)
    if samp:
        segs = [(s * 32, 32, s) for s in range(4)]
    else:
        segs = [(0, 128, 0)]
    sets = [dict(VSb=H[1], QHb=H[2], KHb=H[3], QGb=H[4], ZSb=H[5], DI=A[15], DS=A[16], EGLB=self.EGLB2[0]),
            dict(VSb=H[18], QHb=H[19], KHb=H[20], QGb=H[21], ZSb=H[22], DI=A[0], DS=A[1], EGLB=self.EGLB2[1])]
    blks = range(nblk)
    cbs = [slice(b * 128, (b + 1) * 128) for b in blks]
    MF = self.MF
    o0 = 4 * TN + 5 * 128
    msk = lambda k: self.AM[:, o0 + k * 128:o0 + (k + 1) * 128]
    PO = self.P[5]
    st = {"pbi": 0}

    def prep(h):
        S_ = sets[h % 2]
        wp = self.wnext(("dn_in", j, h))
        QKV = [A[10], A[11], A[12]]
        VSb = S_["VSb"][:, :N]
        for g in range(4):
            ps = self.P[st["pbi"] % 2]
            st["pbi"] += 1
            for kc in range(8):
                self.mm(ps[:, :N], wp[:, kc, g, :], self.YB[kc][:, :N], start=(kc == 0), stop=(kc == 7))
            if g == 3:
                sg = A[17][:, :N]
                self.act(sg, ps[:, :N], AF.Exp, scale=-1.0)
                self.act(sg, sg, AF.Ln, bias=1.0)
                self.act(sg, sg, AF.Exp, scale=-1.0)
                self.tt("dve", S_["ZSb"][:, :N], ps[:, :N], sg, ALU.mult)
                yield
                continue
            xs = self.XS[g]
            self.cp("act", xs[:, 3:3 + N], ps[:, :N])
            ch = g * 8 + h
            if samp:
                for s in range(4):
                    self.dma("sp", "ldcs", xs[:, 3 + 32 * s + 13:3 + 32 * s + 16],
                             (self.sconvT[j, s, ch * 128:(ch + 1) * 128, :], []))
                    self.store(self.o_sconv[j, s, ch * 128:(ch + 1) * 128, :], xs[:, 3 + 32 * s + 29:3 + 32 * s + 32], "stc")
            else:
                self.cp("dve", xs[:, 0:3], self.HALO[j][ch][:, :])
                self.cp("dve", self.HALO[j][ch][:, :], xs[:, N:N + 3])
                if c.last:
                    self.store(self.o_pconv[j, ch * 128:(ch + 1) * 128, :], self.HALO[j][ch][:, :], "stc")
            yield
            cw = lambda i: self.CONVW[:, ((j * 24 + ch) * 4 + i):((j * 24 + ch) * 4 + i + 1)]
            acc = QKV[g][:, :N]
            self.ts("dve", acc, xs[:, 0:N], cw(0), ALU.mult)
            self.stt(acc, xs[:, 1:1 + N], cw(1), acc, ALU.mult, ALU.add)
            yield
            self.stt(acc, xs[:, 2:2 + N], cw(2), acc, ALU.mult, ALU.add)
            self.stt(acc, xs[:, 3:3 + N], cw(3), acc, ALU.mult, ALU.add)
            sg = A[17][:, :N]
            self.act(sg, acc, AF.Exp, scale=-1.0)
            self.act(sg, sg, AF.Ln, bias=1.0)
            self.act(sg, sg, AF.Exp, scale=-1.0)
            yield
            if g == 2:
                self.tt("dve", VSb, acc, sg, ALU.mult)
            else:
                self.tt("dve", acc, acc, sg, ALU.mult)
            yield
        QS, KS = QKV[0][:, :N], QKV[1][:, :N]
        QHb = S_["QHb"][:, :N]
        KHb = S_["KHb"][:, :N]
        nrm = ((QS, QHb, 128.0 ** -0.5, A[13], H[0]), (KS, KHb, 1.0, A[14], H[23]))
        pss = []
        for src, dst, sc, rt, sqt in nrm:
            self.act(sqt[:, :N], src, AF.Square)
        yield
        for src, dst, sc, rt, sqt in nrm:
            ps = self.P[st["pbi"] % 2]
            st["pbi"] += 1
            pss.append(ps)
            self.mm(ps[:, :N], self.ONESB[:], sqt[:, :N])
        yield
        for (src, dst, sc, rt, sqt), ps in zip(nrm, pss):
            self.act(rt[:, :N], ps[:, :N], AF.Ln, bias=EPS)
            self.act(rt[:, :N], rt[:, :N], AF.Exp, scale=-0.5)
        yield
        for src, dst, sc, rt, sqt in nrm:
            self.stt(dst, src, sc, rt[:, :N], ALU.mult, ALU.mult)
        oh = self.OH8[:, h:h + 1]
        DI = S_["DI"][:, :N]
        DS = S_["DS"][:, :N]
        QGb = S_["QGb"][:, :N]
        PB = self.P[2]
        tm = [A[2][0:8, :N], A[6][0:8, :N], A[7][0:8, :N], A[8][0:8, :N]]
        self.ts("dve", tm[0], GC, oh, ALU.mult)
        self.ts("dve", tm[1], GCB, oh, ALU.mult)
        self.ts("dve", tm[2], EGC, oh, ALU.mult)
        self.ts("dve", tm[3][:, 0:nun], EGL, oh, ALU.mult)
        yield
        self.mm(PB[:, :N], self.ONES8[:, :], tm[0])
        yield
        for b in range(nblk):
            self.tt("dve", DI[:, cbs[b]], PB[:, cbs[b]], mbI, ALU.add)
        self.mm(PB[:, :N], self.ONES8[:, :], tm[1])
        yield
        for b in range(nblk):
            self.tt("dve", DS[:, cbs[b]], PB[:, cbs[b]], mbS, ALU.add)
        self.mm(PB[:, :N], self.ONES8[:, :], tm[2])
        yield
        self.tt("dve", QGb, QHb, PB[:, :N], ALU.mult)
        self.mm(PB[:, 0:nun], self.ONES8[:, :], tm[3][:, 0:nun])
        yield
        self.cp("dve", S_["EGLB"][:, 0:nun], PB[:, 0:nun])
        yield

    def solve(h):
        S_ = sets[h % 2]
        VSb = S_["VSb"][:, :N]
        QHb = S_["QHb"][:, :N]
        KHb = S_["KHb"][:, :N]
        QGb = S_["QGb"][:, :N]
        DI = S_["DI"][:, :N]
        DS = S_["DS"][:, :N]
        EGLB = S_["EGLB"]
        attnT, TT, T_, vb, kds, n0, n0t = ({} for _ in range(7))
        DTs, DTi, r0s, r1s, r2s, r3s, r4s = ({} for _ in range(7))
        for b in blks:
            ngc = self.TS[b][:, h:h + 1]
            DTs[b] = self.newM()
            DTi[b] = self.newM()
            self.act(DTs[b][:, :], DS[:, cbs[b]], AF.Exp, bias=ngc)
            self.act(DTi[b][:, :], DI[:, cbs[b]], AF.Exp, bias=ngc)
            r0s[b] = self.R()
            self.mm(r0s[b], KHb[:, cbs[b]], KHb[:, cbs[b]])
        yield
        for b in blks:
            r2s[b] = self.R()
            self.mm(r2s[b], KHb[:, cbs[b]], QHb[:, cbs[b]])
            n0t[b] = MF[b * 6 + 5]
            self.stt(n0t[b][:, :], r0s[b], -1.0, DTs[b][:, :], ALU.mult, ALU.mult)
        yield
        for b in blks:
            r3s[b] = self.Rb()
            self.tr(r3s[b], VSb[:, cbs[b]], self.IDENTB[:])
            attnT[b] = MF[b * 6 + 0]
            self.tt("dve", attnT[b][:, :], r2s[b], DTi[b][:, :], ALU.mult)
            TT[b] = self.newM()
            self.tt("pool", TT[b][:, :], n0t[b][:, :], msk(0), ALU.mult)
            self.tt("pool", TT[b][:, :], TT[b][:, :], self.IDENTB[:], ALU.add)
        yield
        for b in blks:
            r1s[b] = self.Rb()
            self.tr(r1s[b], n0t[b][:, :], self.IDENTB[:])
            vb[b] = MF[b * 6 + 1]
            self.ts("dve", vb[b][:, :], r3s[b], self.TS[b][:, 16 + h:17 + h], ALU.mult)
        yield
        for b in blks:
            r4s[b] = self.Rb()
            self.tr(r4s[b], KHb[:, cbs[b]], self.IDENTB[:])
            n0[b] = MF[b * 6 + 4]
            T_[b] = self.newM()
            self.tt("dve", T_[b][:, :], r1s[b], msk(0), ALU.mult)
            self.cp("act", n0[b][:, :], r1s[b])
            self.tt("pool", T_[b][:, :], T_[b][:, :], self.IDENTB[:], ALU.add)
        yield
        for b in blks:
            kds[b] = []
            for si, (p0, pl_, s) in enumerate(segs):
                kd = MF[b * 6 + 2] if si == 0 else MF[23 + si]
                if samp:
                    self.ts("dve", kd[:, :], r4s[b], self.DKS[:, s * 8 + h:s * 8 + h + 1], ALU.mult)
                else:
                    self.ts("dve", kd[:, :], r4s[b], self.TS[b][:, 24 + h:25 + h], ALU.mult)
                kds[b].append(kd)
        yield
        for lev in range(c.L):
            last = (lev == c.L - 1)
            X = {}
            for b in blks:
                rx = self.R()
                self.mm(rx, n0t[b][:, :], T_[b][:, :])
                X[b] = self.newM()
                self.tt("dve", X[b][:, :], rx, msk(1 + 2 * lev), ALU.mult)
            yield
            for b in blks:
                rp = self.R()
                self.mm(rp, X[b][:, :], TT[b][:, :])
                TT2 = MF[b * 6 + 3] if last else self.newM()
                self.tt("dve", TT2[:, :], rp, TT[b][:, :], ALU.add)
                if not last:
                    rq = self.R()
                    self.mm(rq, TT[b][:, :], X[b][:, :])
                    T2 = self.newM()
                    self.tt("dve", T2[:, :], rq, T_[b][:, :], ALU.add)
                    T_[b] = T2
                TT[b] = TT2
                if b % 2 == 1:
                    yield
        if not samp:
            Sbh = self.SSB[h % 2]
            self.cp("act", Sbh[:, :], self.SST[j][h][:, :])
        for b in blks:
            cb = cbs[b]
            ns1 = self.TS[b][:, 8 + h:9 + h]
            Ss = []
            r5 = self.R()
            for si, (p0, pl_, s) in enumerate(segs):
                if samp:
                    Sf = self.S0[s]
                    self.dma("sp", "lds0", Sf[:, :], (self.srec[j, s, h], []))
                    Sb_ = self.SB[s]
                    self.cp("act", Sb_[:, :], Sf[:, :])
                else:
                    Sf = self.SST[j][h]
                    Sb_ = Sbh
                Ss.append((Sf, Sb_))
                kw = {} if pl_ == 128 else {"tile_position": (0, p0)}
                self.mm(r5[p0:p0 + pl_, :], KHb[:, b * 128 + p0:b * 128 + p0 + pl_], Sb_[:, :], **kw)
            rr_ = MF[27 + b % 2]
            self.stt(rr_[:, :], r5, ns1, vb[b][:, :], ALU.mult, ALU.add)
            yield
            r6 = self.R()
            self.mm(r6, TT[b][:, :], rr_[:, :])
            vn = MF[29 + b % 2]
            self.cp("act", vn[:, :], r6)
            yield
            for si, (p0, pl_, s) in enumerate(segs):
                cs = slice(b * 128 + p0, b * 128 + p0 + pl_)
                self.mm(PO[:, cs], Ss[si][1][:, :], QGb[:, cs], start=(si == 0), stop=False)
            self.mm(PO[:, cb], vn[:, :], attnT[b][:, :], start=False, stop=True)
            for si, (p0, pl_, s) in enumerate(segs):
                Sf, Sb_ = Ss[si]
                r7 = self.R()
                self.mm(r7, kds[b][si][:, :], vn[:, :])
                u = b * (128 // U) + (si if samp else 0)
                if samp:
                    So = self.S1[self.s0i % 2]
                    self.s0i += 1
                    self.stt(So[:, :], Sf[:, :], EGLB[:, u:u + 1], r7, ALU.mult, ALU.add)
                    self.store(self.o_srec[j, s, h], So[:, :], "sts")
                else:
                    self.stt(Sf[:, :], Sf[:, :], EGLB[:, u:u + 1], r7, ALU.mult, ALU.add)
                    self.cp("act", Sb_[:, :], Sf[:, :])
            yield
        sq = H[24][:, :N]
        self.act(sq, PO[:, :N], AF.Square)
        self.mm(self.P[2][:, :N], self.ONESB[:], sq)
        rn = A[18][:, :N]
        self.act(rn, self.P[2][:, :N], AF.Ln, scale=1.0 / 128.0, bias=EPS)
        self.act(rn, rn, AF.Exp, scale=-0.5)
        self.stt(rn, PO[:, :N], self.ON[:, j:j + 1], rn, ALU.mult, ALU.mult)
        self.tt("dve", H[8 + h][:, :N], rn, S_["ZSb"][:, :N], ALU.mult)
        if (not samp) and c.last:
            self.store(self.o_prec[j, h], self.SST[j][h][:, :], "sts")
        yield

    def drive(gens):
        gens = [g for g in gens if g is not None]
        while gens:
            for g in list(gens):
                try:
                    next(g)
                except StopIteration:
                    gens.remove(g)

    drive([prep(0)])
    for h in range(8):
        drive([solve(h), prep(h + 1) if h < 7 else None])
    self.out_proj(c, ("dn_o", j))


def out_proj(self, c, key):
    N = c.N
    pb = 0
    for half in range(2):
        wo = self.wnext(key + (half,))
        for m4 in range(4):
            m = half * 4 + m4
            ps = self.P[pb % 2]
            pb += 1
            for hh in range(8):
                self.mm(ps[:, :N], wo[:, hh, m4 * 128:(m4 + 1) * 128], self.H[8 + hh][:, :N], start=(hh == 0), stop=(hh == 7))
            self.residual(c, m, ps[:, :N])


MK.dn_layer = dn_layer
MK.out_proj = out_proj


def rope(self, c, src, dst, tmp):
    N = c.N
    ps = self.P[6]
    self.mm(ps[0:64, :N], self.PROT[:, :], src)
    self.tt("dve", tmp, src, self.COS[:, :N], ALU.mult)
    self.tt("dve", dst, ps[0:64, :N], self.SIN[:, :N], ALU.mult)
    self.tt("pool", dst, dst, tmp, ALU.add)


def make_k(self, wuk, CKB, N):
    for h in range(8):
        ps = self.P[h % 2]
        for kc in range(2):
            self.mm(ps[:, :N], wuk[:, kc, h * 128:(h + 1) * 128], CKB[kc][:, :N], start=(kc == 0), stop=(kc == 1))
        self.evac(self.H[19 + h][:, :N], ps[:, :N])


def make_v(self, wuv, CKB, nblk):
    for kb in range(nblk):
        for half in range(2):
            ps = self.P[half]
            for kc in range(2):
                self.mm(ps[:, :], CKB[kc][:, kb * 128:(kb + 1) * 128], wuv[:, kc, half * 512:(half + 1) * 512], start=(kc == 0), stop=(kc == 1))
            self.evac(self.VT[kb][:, half * 512:(half + 1) * 512], ps[:, :])


def spill_k(self, j, key0, N):
    b = Buf()
    dst = self.Kscr[j].rearrange("h p n -> p h n")[:, :, key0:key0 + N]
    self.S.dma("pool", "spk", dst, self.Ht[:, 19:27, 0:N], reads=[self.H[19 + h].b for h in range(8)], writes=[b])
    return b


def spill_v(self, j, key0, nblk):
    b = Buf()
    dst = self.Vscr[j][:, key0 // 128:key0 // 128 + nblk, :]
    src = self.BXt[:, 0:2 * nblk, :].rearrange("p (b a) n -> p b (a n)", a=2)
    self.S.dma("pool", "spv", dst, src, reads=[self.VT[kb].b for kb in range(nblk)], writes=[b])
    return b


def sample_cache_kv(self):
    A = self.A
    H = self.H
    for j in range(2):
        if 2 * j + 1 >= self.NLAY:
            continue
        bl = []
        grp = [(s, g) for s in range(4) for g in range(2)]
        for gi, (s, g) in enumerate(grp):
            src = self.ckvT[j, s].rearrange("(c p) n -> p c n", p=128)[:, :, g * 512:(g + 1) * 512]
            self.S.dma("sp", "ldck", self.Awide[:, 12:14, :], src, reads=[], writes=[A[12].b, A[13].b])
            for kc in range(2):
                self.cp("pool", H[2 * gi + kc][:, :], A[12 + kc][:, :])
        for s in range(4):
            b = Buf()
            self.S.dma("pool", "spr", self.KRscr[j][:, self.SEQ + s * PAST:self.SEQ + (s + 1) * PAST], self.krT[j, s], reads=[], writes=[b])
            bl.append(b)
        wuk = self.wnext(("mla_uk", j))
        for gi, (s, g) in enumerate(grp):
            self.make_k(wuk, [H[2 * gi], H[2 * gi + 1]], TN)
            bl.append(self.spill_k(j, self.SEQ + s * PAST + g * 512, TN))
        wuv = self.wnext(("mla_uv", j))
        for gi, (s, g) in enumerate(grp):
            self.make_v(wuv, [H[2 * gi], H[2 * gi + 1]], 4)
            bl.append(self.spill_v(j, self.SEQ + s * PAST + g * 512, 4))
        self.kvbuf[(j, "cache")] = bl


def mla_layer(self, c, l):
    j = l // 2
    N = c.N
    A = self.A
    H = self.H
    samp = c.sample
    nblk = c.nblk
    pb = 0
    wi0 = self.wnext(("mla_in", j, 0))
    CQ = [A[8 + m] for m in range(4)]
    for m in range(4):
        ps = self.P[pb % 2]
        pb += 1
        for kc in range(8):
            self.mm(ps[:, :N], wi0[:, kc, m * 128:(m + 1) * 128], self.YB[kc][:, :N], start=(kc == 0), stop=(kc == 7))
        self.evac(CQ[m][:, :N], ps[:, :N])
    rstd = self.bstats(c, [CQ[m][:, :N] for m in range(4)], False, 512.0, self.P[2], self.P[3])
    for m in range(4):
        self.stt(H[4 + m][:, :N], CQ[m][:, :N], self.QN[:, j * 4 + m:j * 4 + m + 1], rstd[:, :N], ALU.mult, ALU.mult)
    wi1 = self.wnext(("mla_in", j, 1))
    CKV = [A[12], A[13]]
    CKVN = [A[14], A[15]]
    for m in range(2):
        ps = self.P[pb % 2]
        pb += 1
        for kc in range(8):
            self.mm(ps[:, :N], wi1[:, kc, m * 128:(m + 1) * 128], self.YB[kc][:, :N], start=(kc == 0), stop=(kc == 7))
        self.evac(CKV[m][:, :N], ps[:, :N])
    ps = self.P[pb % 2]
    pb += 1
    for kc in range(8):
        self.mm(ps[0:64, :N], wi1[:, kc, 256:320], self.YB[kc][:, :N], start=(kc == 0), stop=(kc == 7))
    KR = A[16][0:64, :N]
    self.evac(KR, ps[0:64, :N])
    rstd = self.bstats(c, [CKV[m][:, :N] for m in range(2)], False, 256.0, self.P[2], self.P[3])
    CKB = [H[16], H[17]]
    for m in range(2):
        self.stt(CKVN[m][:, :N], CKV[m][:, :N], self.KVN[:, j * 2 + m:j * 2 + m + 1], rstd[:, :N], ALU.mult, ALU.mult)
        if samp:
            self.store(self.o_sckv[j, m * 128:(m + 1) * 128, 0:N], CKVN[m][:, :N], "stk")
        else:
            self.store(self.o_pckv[j, m * 128:(m + 1) * 128, c.pos0:c.pos0 + N], CKVN[m][:, :N], "stk")
        self.cp("pool", CKB[m][:, :N], CKVN[m][:, :N])
    KRR = A[17][0:64, :N]
    self.rope(c, KR, KRR, A[18][0:64, :N])
    if samp:
        self.store(self.o_skr[j, :, 0:N], KRR, "stk")
    else:
        self.store(self.o_pkr[j, :, c.pos0:c.pos0 + N], KRR, "stk")
    KRB = H[18][0:64, :N]
    self.cp("pool", KRB, KRR)
    wuk = self.wnext(("mla_uk", j))
    self.make_k(wuk, CKB, N)
    wuv = self.wnext(("mla_uv", j))
    self.make_v(wuv, CKB, nblk)
    if not samp:
        bl = [self.spill_k(j, c.pos0, N), self.spill_v(j, c.pos0, nblk)]
        b = Buf()
        self.dma("pool", "spr", (self.KRscr[j][:, c.pos0:c.pos0 + N], [b]), KRB)
        bl.append(b)
        self.kvbuf[(j, c.t)] = bl
    srcs = []
    if samp:
        for s in range(4):
            msk = self.AM[:, 4 * TN + s * 128:4 * TN + (s + 1) * 128]
            srcs.append((self.SEQ + s * PAST, PAST, self.kvbuf[(j, "cache")], msk))
    else:
        t0 = 0
        while t0 < c.t:
            nt = min(2, c.t - t0)
            bl = []
            for tt_ in range(t0, t0 + nt):
                bl += self.kvbuf[(j, tt_)]
            srcs.append((t0 * TN, nt * TN, bl, None))
            t0 += nt
    PO = self.P[5]
    PD = self.P[4]
    for h in range(8):
        wq = self.wnext(("mla_uq", j, h))
        ps = self.P[pb % 2]
        pb += 1
        for kc in range(4):
            self.mm(ps[:, :N], wq[:, kc, 0:128], H[4 + kc][:, :N], start=(kc == 0), stop=(kc == 3))
        QT = H[27][:, :N]
        self.evac(QT, ps[:, :N])
        ps = self.P[pb % 2]
        pb += 1
        for kc in range(4):
            self.mm(ps[0:64, :N], wq[:, kc, 128:192], H[4 + kc][:, :N], start=(kc == 0), stop=(kc == 3))
        QR = A[10][0:64, :N]
        self.evac(QR, ps[0:64, :N])
        QRR = A[11][0:64, :N]
        self.rope(c, QR, QRR, A[18][0:64, :N])
        QRB = H[28][0:64, :N]
        self.cp("pool", QRB, QRR)
        blist = []
        for (key0, nk, bl, msk) in srcs:
            for kb in range(nk // 128):
                blist.append(("hist", key0, nk, bl, msk, kb))
        for kb in range(nblk):
            blist.append(("own", kb))
        nb_tot = len(blist)
        DEN = A[9][:, :N]
        cur = {}

        def emit_st(i):
            d = blist[i]
            if d[0] == "hist":
                _, key0, nk, bl, msk, kb = d
                if kb == 0:
                    sl = self.hslot % 2
                    self.hslot += 1
                    nb = nk // 128
                    self.dma("sp", f"hk{sl}", self.HK[sl][:, 0:nk], (self.Kscr[j, h][:, key0:key0 + nk], bl))
                    self.dma("sp", f"hv{sl}", self.HV[sl][:, 0:nb, :], (self.Vscr[j][:, key0 // 128:key0 // 128 + nb, h * 128:(h + 1) * 128], bl))
                    self.dma("sp", f"hr{sl}", self.HKR[sl][:, 0:nk], (self.KRscr[j][:, key0:key0 + nk], bl))
                    cur["sl"] = sl
                sl = cur["sl"]
                Kv = self.HK[sl][:, kb * 128:(kb + 1) * 128]
                KRv = self.HKR[sl][:, kb * 128:(kb + 1) * 128]
                Vv = self.HV[sl][:, kb, :]
                c0 = 0
            else:
                kb = d[1]
                Kv = H[19 + h][:, kb * 128:(kb + 1) * 128]
                KRv = H[18][0:64, kb * 128:(kb + 1) * 128]
                Vv = self.VT[kb][:, h * 128:(h + 1) * 128]
                if samp:
                    msk = self.AM[:, 4 * TN + 4 * 128:4 * TN + 5 * 128]
                    c0 = 0
                else:
                    c0 = kb * 128
                    msk = self.AM[:, kb * TN + c0:(kb + 1) * TN]
            ps_ = self.P[(2, 3, 6, 7)[i % 4]]
            self.mm(ps_[:, c0:N], Kv, QT[:, c0:N], start=True, stop=False)
            self.mm(ps_[:, c0:N], KRv, QRB[:, c0:N], start=False, stop=True)
            PT = (H[29], H[30], H[31], H[0], H[1], H[2])[i % 6][:, c0:N]
            self.act(PT, ps_[:, c0:N], AF.Exp, scale=MLA_SCALE)
            if msk is not None:
                self.tt("dve", PT, PT, msk, ALU.mult)
            return (PT, Vv, c0)

        GB = 2
        groups = [list(range(g0, min(g0 + GB, nb_tot))) for g0 in range(0, nb_tot, GB)]
        pend = [emit_st(i) for i in groups[0]]
        for gi, grp in enumerate(groups):
            cur_items = pend
            if gi + 1 < len(groups):
                pend = [emit_st(i) for i in groups[gi + 1]]
            for i, (PT, Vv, c0) in zip(grp, cur_items):
                self.mm(PO[:, c0:N], Vv, PT, start=(i == 0), stop=(i == nb_tot - 1))
            for i, (PT, Vv, c0) in zip(grp, cur_items):
                if i == 0:
                    self.cp("dve", DEN[:, c0:N], PT)
                else:
                    self.tt("dve", DEN[:, c0:N], DEN[:, c0:N], PT, ALU.add)
        self.mm(PD[:, :N], self.ONESF[:], DEN)
        rden = A[8][:, :N]
        self.recip(rden, PD[:, :N])
        self.tt("dve", H[8 + h][:, :N], PO[:, :N], rden, ALU.mult)
    self.out_proj(c, ("mla_o", j))


MK.mla_layer = mla_layer
MK.sample_cache_kv = sample_cache_kv
MK.rope = rope
MK.make_k = make_k
MK.make_v = make_v
MK.spill_k = spill_k
MK.spill_v = spill_v


PROMPT_CORES = [0, 1, 4, 5]
_NC_CACHE = {}


def kernel(**inputs):
    NT = SEQ_FULL // TN
    if "nc" not in _NC_CACHE:
        mk = MK(NT=NT, NLAY=4, sample=True)
        _NC_CACHE["nc"] = mk.build(mixers=True)
    nc = _NC_CACHE["nc"]
    core_seq = [None] * 8
    for b, cidx in enumerate(PROMPT_CORES):
        core_seq[cidx] = b
    core_samp = [[4 * c + i for i in range(4)] for c in range(8)]
    maps = host_inputs(inputs, NT, core_seq, core_samp)
    res = run_bass_kernel_spmd(nc, maps, core_ids=list(range(8)))
    R = res.results
    f32 = np.float32
    y_prompt = np.empty((4, SEQ_FULL, D), f32)
    y_sample = np.empty((32, 16, D), f32)
    p_conv = np.empty((2, 4, 3, 3072), f32)
    p_rec = np.empty((2, 4, 8, 128, 128), f32)
    p_ckv = np.empty((2, 4, SEQ_FULL, 256), f32)
    p_kr = np.empty((2, 4, SEQ_FULL, 64), f32)
    s_conv = np.empty((2, 32, 3, 3072), f32)
    s_rec = np.empty((2, 32, 8, 128, 128), f32)
    s_ckv = np.empty((2, 32, 16, 256), f32)
    s_kr = np.empty((2, 32, 16, 64), f32)
    for b, cidx in enumerate(PROMPT_CORES):
        r = R[cidx]
        y_prompt[b] = r["o_yT"].T
        p_conv[:, b] = np.transpose(r["o_pconv"], (0, 2, 1))
        p_rec[:, b] = r["o_prec"]
        p_ckv[:, b] = np.transpose(r["o_pckv"], (0, 2, 1))
        p_kr[:, b] = np.transpose(r["o_pkr"], (0, 2, 1))
    for c in range(8):
        r = R[c]
        sl = slice(4 * c, 4 * c + 4)
        y_sample[sl] = np.transpose(r["o_ysT"].reshape(D, 4, 32)[:, :, 16:], (1, 2, 0))
        s_conv[:, sl] = np.transpose(r["o_sconv"], (0, 1, 3, 2))
        s_rec[:, sl] = r["o_srec"]
        s_ckv[:, sl] = np.transpose(r["o_sckv"].reshape(2, 256, 4, 32)[..., 16:], (0, 2, 3, 1))
        s_kr[:, sl] = np.transpose(r["o_skr"].reshape(2, 64, 4, 32)[..., 16:], (0, 2, 3, 1))
    return (y_prompt, y_sample, p_conv, p_rec, p_ckv, p_kr, s_conv, s_rec, s_ckv, s_kr)
```

```python
import numpy as np
from contextlib import ExitStack
import concourse.bass as bass
import concourse.mybir as mybir
from concourse.bass_utils import run_bass_kernel_spmd

F32 = mybir.dt.float32
BF16 = mybir.dt.bfloat16
AF = mybir.ActivationFunctionType
ALU = mybir.AluOpType

D = 1024
DFF = 4096
NL = 4
ALPHA = 8.0 ** 0.25
EPS = 1e-6
MLA_SCALE = 192.0 ** -0.5
NEG = -30000.0
SEQ_FULL = 8192
TN = 512
SN = 128
PAST = 1024


class Buf:
    __slots__ = ("lw", "rd")

    def __init__(self):
        self.lw = None
        self.rd = {}


class Prod:
    def __init__(self, name, sem, step):
        self.name = name
        self.sem = sem
        self.step = step
        self.n = 0


class Sched:
    def __init__(self, nc, stack):
        self.nc = nc
        self.stack = stack
        self.h = {"pe": nc.tensor, "act": nc.scalar, "dve": nc.vector, "pool": nc.gpsimd, "sp": nc.sync}
        self.eng = {}
        for k in self.h:
            self.eng[k] = Prod(k, stack.enter_context(nc.semaphore("s_" + k)), 1)
        self.waited = {k: {} for k in self.h}
        self.chan = {}
        self.ninstr = 0

    def channel(self, name):
        if name not in self.chan:
            self.chan[name] = Prod(name, self.stack.enter_context(self.nc.semaphore("c_" + name)), 16)
        return self.chan[name]

    def _need(self, e, reads, writes):
        need = {}
        me = self.eng.get(e)
        for b in reads:
            if b.lw is not None:
                p, i = b.lw
                if need.get(p, 0) < i:
                    need[p] = i
        for b in writes:
            if b.lw is not None:
                p, i = b.lw
                if p is not me and need.get(p, 0) < i:
                    need[p] = i
            for p, i in b.rd.items():
                if p is not me and need.get(p, 0) < i:
                    need[p] = i
        return need

    def _waits(self, e, need):
        w = self.waited[e]
        h = self.h[e]
        for p, i in need.items():
            if w.get(p, 0) < i:
                h.wait_ge(p.sem, i)
                w[p] = i
                self.ninstr += 1

    def op(self, e, fn, reads=(), writes=()):
        need = self._need(e, reads, writes)
        if e == "pe":
            need.pop(self.eng["pe"], None)
        self._waits(e, need)
        prod = self.eng[e]
        inst = fn(self.h[e])
        prod.n += 1
        inst.then_inc(prod.sem, 1)
        self.ninstr += 1
        for b in reads:
            b.rd[prod] = prod.n
        for b in writes:
            b.lw = (prod, prod.n)
            b.rd = {}

    def dma(self, q, chan, out, in_, reads=(), writes=(), **kw):
        c = self.channel(chan)
        need = self._need(q, reads, writes)
        if c.n > 0 and need.get(c, 0) < c.n:
            need[c] = c.n
        self._waits(q, need)
        inst = self.h[q].dma_start(out=out, in_=in_, **kw)
        c.n += 16
        inst.then_inc(c.sem, 16)
        self.ninstr += 1
        for b in reads:
            b.rd[c] = c.n
        for b in writes:
            b.lw = (c, c.n)
            b.rd = {}


class Tl:
    def __init__(self, ap, buf=None, excl=False):
        self.ap = ap
        self.b = buf if buf is not None else Buf()
        self.excl = excl

    def __getitem__(self, idx):
        return Vw(self, self.ap[idx])

    def v(self, ap):
        return Vw(self, ap)


class Vw:
    def __init__(self, tl, ap):
        self.tl = tl
        self.ap = ap

    def __getitem__(self, idx):
        return Vw(self.tl, self.ap[idx])

    def re(self, s, **kw):
        return Vw(self.tl, self.ap.rearrange(s, **kw))

    def bc(self, shape):
        return Vw(self.tl, self.ap.to_broadcast(shape))


def _rw(outs, ins):
    r, w = [], []
    for v in ins:
        if isinstance(v, Vw):
            (w if v.tl.excl else r).append(v.tl.b)
    for v in outs:
        w.append(v.tl.b)
    return r, w


def _a(x):
    return x.ap if isinstance(x, Vw) else x


class TileCtx:
    pass


class MK:
    def __init__(self, NT=16, NLAY=4, sample=True, solve_dt=F32):
        self.NT = NT
        self.NLAY = NLAY
        self.sample = sample
        self.SEQ = NT * TN
        self.SD = solve_dt
        self.nc = bass.Bass("TRN2", target_bir_lowering=False)
        self.st = ExitStack()
        self.S = Sched(self.nc, self.st)
        self.rr = 0
        self.ev = 0
        self.castn = 0
        self._decl()
        self._alloc()

    def din(self, name, shape, dt=F32):
        return self.nc.dram_tensor(name, list(shape), dt, kind="ExternalInput").ap()

    def dout(self, name, shape, dt=F32):
        return self.nc.dram_tensor(name, list(shape), dt, kind="ExternalOutput").ap()

    def dint(self, name, shape, dt=BF16):
        return self.nc.dram_tensor(name, list(shape), dt, kind="Internal").ap()

    def _decl(self):
        SEQ = self.SEQ
        i = self.din
        self.xT = i("xT", [D, SEQ])
        self.pT = i("pT", [NL, 256, SEQ])
        self.xsT = i("xsT", [D, SN])
        self.psT = i("psT", [NL, 256, SN])
        self.sconvT = i("sconvT", [2, 4, 3072, 3])
        self.srec = i("srec", [2, 4, 8, 128, 128])
        self.ckvT = i("ckvT", [2, 4, 256, PAST])
        self.krT = i("krT", [2, 4, 64, PAST])
        self.w = {
            "up": i("mlp_w_up", [NL, D, DFF]), "down": i("mlp_w_down", [NL, DFF, D]),
            "pproj": i("ple_w_proj", [NL, 256, D]), "gate": i("ple_w_gate", [NL, D, D]),
            "dn_in": i("dn_w_in", [2, D, 4112]), "dn_o": i("dn_w_o", [2, D, D]),
            "mla_in": i("mla_w_in", [2, D, 832]), "mla_uq": i("mla_w_uq", [2, 512, 1536]),
            "mla_uk": i("mla_w_uk", [2, 256, 1024]), "mla_uv": i("mla_w_uv", [2, 256, 1024]),
            "mla_o": i("mla_w_o", [2, D, D]),
        }
        self.wb = {k: self.dint("wb_" + k, v.shape) for k, v in self.w.items()}
        self.wbuf = {k: [] for k in self.w}
        self.vec_d = i("vec1024", [128, NL * 5 * 8])
        self.convw_d = i("convw", [128, 2 * 24 * 4])
        self.qn_d = i("qnorm", [128, 2 * 4])
        self.kvn_d = i("kvnorm", [128, 2 * 2])
        self.on_d = i("onorm", [128, 2])
        self.alog_d = i("alog", [8, 2])
        self.dtb_d = i("dtb", [8, 2])
        self.ident_d = i("ident", [128, 128])
        self.oh8_d = i("oh8", [8, 8])
        self.mb_d = i("maskbias", [128, 4 * 128])
        self.gm_d = i("gatemask", [8, TN + 2 * SN])
        self.tokm_d = i("tokmask", [128, 8])
        self.am_d = i("attnmask", [128, 4 * TN + 18 * 128])
        self.prot_d = i("prot", [64, 64])
        self.ropeC = i("ropeC", [64, SEQ + SN])
        self.ropeS = i("ropeS", [64, SEQ + SN])
        o = self.dout
        self.o_yT = o("o_yT", [D, SEQ])
        self.o_ysT = o("o_ysT", [D, SN])
        self.o_pconv = o("o_pconv", [2, 3072, 3])
        self.o_prec = o("o_prec", [2, 8, 128, 128])
        self.o_pckv = o("o_pckv", [2, 256, SEQ])
        self.o_pkr = o("o_pkr", [2, 64, SEQ])
        self.o_sconv = o("o_sconv", [2, 4, 3072, 3])
        self.o_srec = o("o_srec", [2, 4, 8, 128, 128])
        self.o_sckv = o("o_sckv", [2, 256, SN])
        self.o_skr = o("o_skr", [2, 64, SN])
        self.obufs = []
        KS = SEQ + 4 * PAST
        self.KS = KS
        self.Kscr = self.dint("Kscr", [2, 8, 128, KS])
        self.Vscr = self.dint("Vscr", [2, 128, KS // 128, 1024])
        self.KRscr = self.dint("KRscr", [2, 64, KS])
        self.kvbuf = {}

    def sb(self, name, shape, dt):
        return self.st.enter_context(self.nc.sbuf_tensor(name, list(shape), dt))

    def _alloc(self):
        nc = self.nc
        sb = self.sb
        yt = sb("Y", [128, 8, TN], F32)
        self._ytensor = yt
        self.Y = [Tl(yt[:, c]) for c in range(8)]
        ybt = sb("YB", [128, 8, TN], BF16)
        self.YB = [Tl(ybt[:, c]) for c in range(8)]
        NA = 20
        at = sb("A", [128, NA, TN], F32)
        self.A = [Tl(at[:, c]) for c in range(NA)]
        self.Awide = at
        ht = sb("H", [128, 32, TN], BF16)
        self.H = [Tl(ht[:, c]) for c in range(32)]
        self.Ht = ht
        NBX = 10
        bt = sb("BX", [128, NBX, TN], BF16)
        self.BX = [Tl(bt[:, c]) for c in range(NBX)]
        self.BXt = bt
        self.VT = [Tl(bt[:, 2 * kb:2 * kb + 2, :].rearrange("p a n -> p (a n)")) for kb in range(4)]
        NM = 34
        mt = sb("M", [128, NM, 128], BF16)
        self.M = [Tl(mt[:, c]) for c in range(NM)]
        self.mi = 0
        mf = sb("MF", [128, 31, 128], BF16)
        self.MF = [Tl(mf[:, c]) for c in range(31)]
        xs = sb("XS", [128, 3, TN + 8], F32)
        self.XS = [Tl(xs[:, c]) for c in range(3)]
        self.NSLOT = 4
        rt = sb("RING", [128, self.NSLOT, 4096], BF16)
        self.ring = [Tl(rt[:, c]) for c in range(self.NSLOT)]
        hk = sb("HK", [128, 2, 1024], BF16)
        hv = sb("HV", [128, 2, 8, 128], BF16)
        hr = sb("HKR", [128, 2, 1024], BF16)
        self.HKRt = hr
        self.HK = [Tl(hk[:, c]) for c in range(2)]
        self.HV = [Tl(hv[:, c]) for c in range(2)]
        self.HKR = [Tl(hr[:, c]) for c in range(2)]
        self.hslot = 0
        self.P = [Tl(self.st.enter_context(nc.psum_tensor(f"P{i}", [128, TN], F32))[:], excl=True) for i in range(8)]
        self.IDENT = Tl(sb("IDENT", [128, 128], F32)[:])
        self.IDENTB = Tl(sb("IDENTB", [128, 128], BF16)[:])
        ssb = sb("SSB", [128, 2, 128], BF16)
        self.SSB = [Tl(ssb[:, k]) for k in range(2)]
        sbb = sb("SBs", [128, 4, 128], BF16)
        self.SB = [Tl(sbb[:, c]) for c in range(4)]
        self.ONESB = Tl(sb("ONESB", [128, 128], BF16)[:])
        self.ONESF = Tl(sb("ONESF", [128, 128], F32)[:])
        self.OH8 = Tl(sb("OH8", [8, 8], F32)[:])
        self.ONES8 = Tl(sb("ONES8", [128, 128], F32)[:])
        self.MB = Tl(sb("MB", [128, 4 * 128], F32)[:])
        self.GM = Tl(sb("GM", [8, TN + 2 * SN], F32)[:])
        self.TOKM = Tl(sb("TOKM", [128, 8], F32)[:])
        self.AM = Tl(sb("AM", [128, 4 * TN + 18 * 128], BF16)[:])
        self.PROT = Tl(sb("PROT", [64, 64], F32)[:])
        self.COS = Tl(sb("COS", [64, TN], F32)[:])
        self.SIN = Tl(sb("SIN", [64, TN], F32)[:])
        self.VEC = Tl(sb("VEC", [128, NL * 5 * 8], F32)[:])
        self.CONVW = Tl(sb("CONVW", [128, 2 * 24 * 4], F32)[:])
        self.QN = Tl(sb("QN", [128, 8], F32)[:])
        self.KVN = Tl(sb("KVN", [128, 4], F32)[:])
        self.ON = Tl(sb("ON", [128, 2], F32)[:])
        self.ALOG = Tl(sb("ALOG", [8, 2], F32)[:])
        self.NEGA = Tl(sb("NEGA", [8, 2], F32)[:])
        self.DTB = Tl(sb("DTB", [8, 2], F32)[:])
        sst = sb("SST", [128, 2, 8, 128], F32)
        self.SST = [[Tl(sst[:, j, h]) for h in range(8)] for j in range(2)]
        self.SSTt = sst
        halo = sb("HALO", [128, 2, 24, 3], F32)
        self.HALO = [[Tl(halo[:, j, c]) for c in range(24)] for j in range(2)]
        self.HALOt = halo
        ts = sb("TS", [128, 4, 32], F32)
        self.TS = [Tl(ts[:, b]) for b in range(4)]
        self.DKS = Tl(sb("DKS", [128, 32], F32)[:])
        eg = sb("EGLB", [128, 2, 16], F32)
        self.EGLB2 = [Tl(eg[:, k]) for k in range(2)]
        s0 = sb("S0", [128, 4, 128], F32)
        self.S0 = [Tl(s0[:, c]) for c in range(4)]
        self.EGLT = Tl(sb("EGLT", [8, 16], F32)[:])
        s1 = sb("S1", [128, 2, 128], F32)
        self.S1 = [Tl(s1[:, c]) for c in range(2)]
        self.s0i = 0

    def mm(self, out, lhsT, rhs, start=True, stop=True, **kw):
        r, w = _rw([out], [lhsT, rhs])
        self.S.op("pe", lambda t: t.matmul(out.ap, lhsT.ap, rhs.ap, start=start, stop=stop, **kw), r, w)

    def tr(self, out, in_, ident):
        r, w = _rw([out], [in_, ident])
        self.S.op("pe", lambda t: t.transpose(out.ap, in_.ap, ident.ap), r, w)

    def act(self, out, in_, func, scale=1.0, bias=None):
        r, w = _rw([out], [in_, scale, bias])
        kw = {}
        if bias is not None:
            kw["bias"] = _a(bias)
        self.S.op("act", lambda a: a.activation(out=out.ap, in_=in_.ap, func=func, scale=_a(scale), **kw), r, w)

    def tt(self, e, out, in0, in1, op):
        r, w = _rw([out], [in0, in1])
        self.S.op(e, lambda v: v.tensor_tensor(out.ap, in0.ap, in1.ap, op), r, w)

    def ts(self, e, out, in0, s1, op0, s2=None, op1=None):
        r, w = _rw([out], [in0, s1, s2])
        if op1 is None:
            self.S.op(e, lambda v: v.tensor_scalar(out.ap, in0.ap, _a(s1), None, op0), r, w)
        else:
            self.S.op(e, lambda v: v.tensor_scalar(out.ap, in0.ap, _a(s1), _a(s2), op0, op1), r, w)

    def stt(self, out, in0, scalar, in1, op0, op1):
        r, w = _rw([out], [in0, scalar, in1])
        self.S.op("dve", lambda v: v.scalar_tensor_tensor(out.ap, in0.ap, _a(scalar), in1.ap, op0, op1), r, w)

    def cp(self, e, out, in_):
        r, w = _rw([out], [in_])
        if e == "act":
            self.S.op("act", lambda a: a.activation(out=out.ap, in_=in_.ap, func=AF.Copy), r, w)
        else:
            self.S.op(e, lambda v: v.tensor_copy(out.ap, in_.ap), r, w)

    def evac(self, out, in_):
        self.ev += 1
        self.cp("act" if self.ev % 2 else "dve", out, in_)

    def recip(self, out, in_):
        r, w = _rw([out], [in_])
        self.S.op("dve", lambda v: v.reciprocal(out.ap, in_.ap), r, w)

    def memset(self, e, out, val):
        r, w = _rw([out], [])
        self.S.op(e, lambda v: v.memset(out.ap, val), r, w)

    def dma(self, q, chan, out, in_, **kw):
        reads, writes = [], []
        if isinstance(in_, Vw):
            reads.append(in_.tl.b)
            ia = in_.ap
        else:
            ia, bl = in_
            reads += bl
        if isinstance(out, Vw):
            writes.append(out.tl.b)
            oa = out.ap
        else:
            oa, bl = out
            writes += bl
        self.S.dma(q, chan, oa, ia, reads=reads, writes=writes, **kw)

    def newM(self):
        m = self.M[self.mi % len(self.M)]
        self.mi += 1
        return m

    def R(self):
        i = self.rr
        self.rr += 1
        bank = self.P[(3, 4, 6, 7)[i % 4]]
        c = ((i // 4) % 4) * 128
        return bank[:, c:c + 128]

    def Rb(self):
        r = self.R()
        return Vw(r.tl, r.ap[:, 0:64].bitcast(BF16))

    def cast_weights(self):
        order = ["dn_in", "dn_o", "up", "down", "pproj", "gate", "mla_in", "mla_uk", "mla_uv", "mla_uq", "mla_o"]
        for k in order:
            src = self.w[k]
            dst = self.wb[k]
            L = src.shape[0]
            R_ = src.shape[1]
            if k == "dn_in":
                for l in range(L):
                    for h in range(8):
                        b = Buf()
                        self.wbuf[k].append((l, b))
                        sv = src[l, :, 0:4096].rearrange("r (g h n) -> r g h n", g=4, h=8)[:, :, h, :]
                        dv = dst[l, :, h * 512:(h + 1) * 512].rearrange("r (g n) -> r g n", g=4)
                        self.S.dma("pool", f"cast{self.castn % 4}", dv, sv, writes=[b])
                        self.castn += 1
                    b = Buf()
                    self.wbuf[k].append((l, b))
                    self.S.dma("pool", f"cast{self.castn % 4}", dst[l, :, 4096:4112], src[l, :, 4096:4112], writes=[b])
                    self.castn += 1
                continue
            for l in range(L):
                nsplit = max(1, (R_ * src.shape[2]) // (1 << 20))
                rs = R_ // nsplit
                for s in range(nsplit):
                    b = Buf()
                    self.wbuf[k].append((l, b))
                    self.S.dma("pool", f"cast{self.castn % 4}", dst[l, s * rs:(s + 1) * rs, :], src[l, s * rs:(s + 1) * rs, :], writes=[b])
                    self.castn += 1

    def piece_src(self, key):
        k = key[0]
        l = key[1]
        wbv = self.wb[k][l]
        if k in ("up", "gate", "dn_o", "mla_o"):
            j = key[2]
            v = wbv.rearrange("(c p) n -> p c n", p=128)[:, :, j * 512:(j + 1) * 512]
            return v, [8, 512]
        if k == "down":
            m = key[2]
            v = wbv.rearrange("(c p) n -> p c n", p=128)[:, :, m * 128:(m + 1) * 128]
            return v, [32, 128]
        if k == "pproj":
            return wbv.rearrange("(c p) n -> p c n", p=128), [2, 1024]
        if k == "dn_in":
            h = key[2]
            if h == "ba":
                v = wbv.rearrange("(c p) n -> p c n", p=128)[:, :, 4096:4112]
                return v, [8, 16]
            v = wbv.rearrange("(c p) n -> p c n", p=128)[:, :, h * 512:(h + 1) * 512]
            return v, [8, 4, 128]
        if k == "mla_in":
            if key[2] == 0:
                return wbv.rearrange("(c p) n -> p c n", p=128)[:, :, 0:512], [8, 512]
            return wbv.rearrange("(c p) n -> p c n", p=128)[:, :, 512:832], [8, 320]
        if k == "mla_uq":
            h = key[2]
            return wbv.rearrange("(c p) n -> p c n", p=128)[:, :, h * 192:(h + 1) * 192], [4, 192]
        if k in ("mla_uk", "mla_uv"):
            return wbv.rearrange("(c p) n -> p c n", p=128), [2, 1024]
        raise KeyError(key)

    def plan_layer(self, l):
        j = l // 2
        pl = []
        if l % 2 == 0:
            pl.append(("dn_in", j, "ba"))
            pl += [("dn_in", j, h) for h in range(8)]
            pl += [("dn_o", j, 0), ("dn_o", j, 1)]
        else:
            pl += [("mla_in", j, 0), ("mla_in", j, 1), ("mla_uk", j), ("mla_uv", j)]
            pl += [("mla_uq", j, h) for h in range(8)]
            pl += [("mla_o", j, 0), ("mla_o", j, 1)]
        pl += [("up", l, jj) for jj in range(8)]
        pl += [("down", l, m) for m in range(8)]
        pl += [("pproj", l), ("gate", l, 0), ("gate", l, 1)]
        return pl

    def set_plan(self, plan):
        self.plan = plan
        self.pi = 0
        self.pl = 0

    def _load_piece(self, idx):
        key = self.plan[idx]
        src, shp = self.piece_src(key)
        slot = self.ring[idx % self.NSLOT]
        n = int(np.prod(shp))
        dst = slot[:, 0:n].re("p (c n) -> p c n", c=shp[0])
        bl = [b for (l, b) in self.wbuf[key[0]] if l == key[1]]
        self.dma("sp", f"w{idx % self.NSLOT}", dst, (src, bl))

    def wnext(self, key):
        assert self.plan[self.pi] == key, (self.plan[self.pi], key)
        while self.pl < len(self.plan) and self.pl < self.pi + self.NSLOT:
            self._load_piece(self.pl)
            self.pl += 1
        idx = self.pi
        self.pi += 1
        _, shp = self.piece_src(key)
        slot = self.ring[idx % self.NSLOT]
        n = int(np.prod(shp))
        v = slot[:, 0:n]
        if len(shp) == 2:
            return v.re("p (c n) -> p c n", c=shp[0])
        return v.re("p (c g n) -> p c g n", c=shp[0], g=shp[1])

    def setup(self):
        ld = lambda tl, d: self.dma("sp", "const", tl[:], (d, []))
        ld(self.IDENT, self.ident_d[:, :])
        ld(self.OH8, self.oh8_d[:, :])
        self.memset("dve", self.ONES8[:], 0.0)
        self.memset("dve", self.ONES8[0:8, :], 1.0)
        ld(self.MB, self.mb_d[:, :])
        ld(self.GM, self.gm_d[:, :])
        ld(self.TOKM, self.tokm_d[:, :])
        AMW = 4 * TN + 18 * 128
        amf = self.Awide[:, 0:9, :].rearrange("p a n -> p (a n)")[:, 0:AMW]
        self.S.dma("sp", "const", amf, self.am_d[:, :], reads=[], writes=[self.A[k].b for k in range(9)])
        ld(self.PROT, self.prot_d[:, :])
        ld(self.VEC, self.vec_d[:, :])
        ld(self.CONVW, self.convw_d[:, :])
        ld(self.QN, self.qn_d[:, :])
        ld(self.KVN, self.kvn_d[:, :])
        ld(self.ON, self.on_d[:, :])
        ld(self.ALOG, self.alog_d[:, :])
        ld(self.DTB, self.dtb_d[:, :])
        self.S.op("dve", lambda v: v.tensor_copy(self.AM.ap[:, :], amf), reads=[self.A[k].b for k in range(9)], writes=[self.AM.b])
        self.cp("dve", self.IDENTB[:], self.IDENT[:])
        self.memset("dve", self.ONESB[:], 1.0)
        self.memset("dve", self.ONESF[:], 1.0)
        for k in range(2):
            self.memset("pool", self.HKR[k][64:128, :], 0.0)
        for g in range(3):
            self.memset("pool", self.XS[g][:, :], 0.0)
        self.act(self.NEGA[:], self.ALOG[:], AF.Exp)
        self.ts("dve", self.NEGA[:], self.NEGA[:], -1.0, ALU.mult)
        for j in range(2):
            for h in range(8):
                self.memset("dve", self.SST[j][h][:], 0.0)
            for cc in range(24):
                self.memset("pool", self.HALO[j][cc][:], 0.0)

    def vec(self, l, kind, c):
        i = (l * 5 + kind) * 8 + c
        return self.VEC[:, i:i + 1]

    def prompt_ctx(self, t):
        c = TileCtx()
        c.sample = False
        c.t = t
        c.N = TN
        c.pos0 = t * TN
        c.nblk = 4
        c.U = 128
        c.nunits = 4
        c.last = (t == self.NT - 1)
        c.L = 6
        return c

    def sample_ctx(self):
        c = TileCtx()
        c.sample = True
        c.t = self.NT
        c.N = SN
        c.pos0 = self.SEQ
        c.nblk = 1
        c.U = 32
        c.nunits = 4
        c.last = True
        c.L = 4
        return c

    def load_tile(self, c):
        N = c.N
        src = (self.xsT if c.sample else self.xT).rearrange("(c p) n -> p c n", p=128)
        src = src[:, :, 0:N] if c.sample else src[:, :, c.pos0:c.pos0 + N]
        self.S.dma("sp", "ldx", self._ytensor[:, :, 0:N], src, reads=[], writes=[y.b for y in self.Y])
        for k in range(8):
            self.cp("pool", self.YB[k][:, :N], self.Y[k][:, :N])
        self.dma("sp", "ldrope", self.COS[:, :N], (self.ropeC[:, c.pos0:c.pos0 + N], []))
        self.dma("sp", "ldrope", self.SIN[:, :N], (self.ropeS[:, c.pos0:c.pos0 + N], []))

    def bstats(self, c, srcs, want_mean, nfeat, PSs, PSq):
        N = c.N
        n = len(srcs)
        for k, s in enumerate(srcs):
            sq = self.H[k % 4][:, :N]
            self.act(sq, s, AF.Square)
            self.mm(PSq[:, :N], self.ONESB[:], sq, start=(k == 0), stop=(k == n - 1))
            if want_mean:
                rb = self.H[4 + k % 4][:, :N]
                self.cp("dve", rb, s)
                self.mm(PSs[:, :N], self.ONESB[:], rb, start=(k == 0), stop=(k == n - 1))
        A = self.A
        inv = 1.0 / nfeat
        if want_mean:
            self.act(A[0][:, :N], PSs[:, :N], AF.Copy, scale=inv)
            self.tt("dve", A[1][:, :N], A[0][:, :N], A[0][:, :N], ALU.mult)
            self.stt(A[2][:, :N], PSq[:, :N], inv, A[1][:, :N], ALU.mult, ALU.subtract)
            self.act(A[2][:, :N], A[2][:, :N], AF.Ln, bias=EPS)
        else:
            self.act(A[2][:, :N], PSq[:, :N], AF.Ln, scale=inv, bias=EPS)
        self.act(A[3][:, :N], A[2][:, :N], AF.Exp, scale=-0.5)
        return A[3]

    def layer_norm(self, c, l, kg, kb):
        N = c.N
        A = self.A
        rstd = self.bstats(c, [self.Y[k][:, :N] for k in range(8)], True, float(D), self.P[2], self.P[3])
        for k in range(8):
            t = A[4 + k % 2][:, :N]
            self.tt("dve", t, self.Y[k][:, :N], A[0][:, :N], ALU.subtract)
            self.tt("dve", t, t, rstd[:, :N], ALU.mult)
            self.act(self.Y[k][:, :N], t, AF.Identity, scale=self.vec(l, kg, k), bias=self.vec(l, kb, k))
            self.act(self.YB[k][:, :N], t, AF.Identity, scale=self.vec(l, kg, k), bias=self.vec(l, kb, k))

    def residual(self, c, k, ps):
        N = c.N
        self.stt(self.Y[k][:, :N], self.Y[k][:, :N], ALPHA, ps, ALU.mult, ALU.add)

    def finish(self, c, l):
        N = c.N
        A = self.A
        self.layer_norm(c, l, 0, 1)
        pb = 0
        for jj in range(8):
            wu = self.wnext(("up", l, jj))
            for m in range(4):
                ff = jj * 4 + m
                ps = self.P[pb % 2]
                pb += 1
                for kc in range(8):
                    self.mm(ps[:, :N], wu[:, kc, m * 128:(m + 1) * 128], self.YB[kc][:, :N], start=(kc == 0), stop=(kc == 7))
                r = A[6 + ff % 2][:, :N]
                self.act(r, ps[:, :N], AF.Relu)
                self.tt("dve" if ff % 2 else "pool", self.H[ff][:, :N], r, r, ALU.mult)
        for m in range(8):
            wd = self.wnext(("down", l, m))
            ps = self.P[pb % 2]
            pb += 1
            for kc in range(32):
                self.mm(ps[:, :N], wd[:, kc, :], self.H[kc][:, :N], start=(kc == 0), stop=(kc == 31))
            self.residual(c, m, ps[:, :N])
        self.layer_norm(c, l, 2, 3)
        psrc = (self.psT if c.sample else self.pT)[l].rearrange("(c p) n -> p c n", p=128)
        psrc = psrc[:, :, 0:N] if c.sample else psrc[:, :, c.pos0:c.pos0 + N]
        PT = [A[18], A[19]]
        self.S.dma("sp", "ldp", self.Awide[:, 18:20, 0:N], psrc, reads=[], writes=[PT[0].b, PT[1].b])
        PB = [self.BX[8], self.BX[9]]
        for kc in range(2):
            self.cp("pool", PB[kc][:, :N], PT[kc][:, :N])
        wpj = self.wnext(("pproj", l))
        E = [A[8 + m] for m in range(8)]
        for m in range(8):
            ps = self.P[pb % 2]
            pb += 1
            for kc in range(2):
                self.mm(ps[:, :N], wpj[:, kc, m * 128:(m + 1) * 128], PB[kc][:, :N], start=(kc == 0), stop=(kc == 1))
            self.cp("act", E[m][:, :N], ps[:, :N])
        rstd = self.bstats(c, [E[m][:, :N] for m in range(8)], False, float(D), self.P[2], self.P[3])
        for jj in range(2):
            wg = self.wnext(("gate", l, jj))
            for m4 in range(4):
                m = jj * 4 + m4
                ps = self.P[pb % 2]
                pb += 1
                for kc in range(8):
                    self.mm(ps[:, :N], wg[:, kc, m4 * 128:(m4 + 1) * 128], self.YB[kc][:, :N], start=(kc == 0), stop=(kc == 7))
                gt = A[16 + m % 2][:, :N]
                self.act(gt, ps[:, :N], AF.Sigmoid)
                t = A[4 + m % 2][:, :N]
                self.tt("pool" if m % 2 else "dve", t, E[m][:, :N], rstd[:, :N], ALU.mult)
                self.tt("dve", t, t, gt, ALU.mult)
                self.stt(self.Y[m][:, :N], t, self.vec(l, 4, m), self.Y[m][:, :N], ALU.mult, ALU.add)
        for m in range(8):
            self.cp("act" if m % 2 else "dve", self.YB[m][:, :N], self.Y[m][:, :N])

    def store(self, dram_ap, src, chan="st"):
        b = Buf()
        self.obufs.append((chan, b))
        self.dma("pool", chan, (dram_ap, [b]), src)

    def store_y(self, c):
        N = c.N
        dst = (self.o_ysT if c.sample else self.o_yT).rearrange("(c p) n -> p c n", p=128)
        dst = dst[:, :, 0:N] if c.sample else dst[:, :, c.pos0:c.pos0 + N]
        b = Buf()
        self.S.dma("pool", "sty", dst, self._ytensor[:, :, 0:N], reads=[y.b for y in self.Y], writes=[b])

    def build(self, mixers=True):
        self.mixers = mixers
        self.setup()
        self.cast_weights()
        tiles = [self.prompt_ctx(t) for t in range(self.NT)]
        if self.sample:
            tiles.append(self.sample_ctx())
        plan = []
        for c in tiles:
            for l in range(self.NLAY):
                pl = self.plan_layer(l)
                if not mixers:
                    pl = [k for k in pl if k[0] in ("up", "down", "pproj", "gate")]
                plan += pl
        if self.sample and mixers:
            pre = []
            for j in range(2):
                if 2 * j + 1 < self.NLAY:
                    pre += [("mla_uk", j), ("mla_uv", j)]
            plan = pre + plan
        self.set_plan(plan)
        if self.sample and mixers:
            self.sample_cache_kv()
        for c in tiles:
            self.load_tile(c)
            for l in range(self.NLAY):
                if mixers:
                    if l % 2 == 0:
                        self.dn_layer(c, l)
                    else:
                        self.mla_layer(c, l)
                else:
                    for k in range(8):
                        self.ts("dve", self.Y[k][:, :c.N], self.Y[k][:, :c.N], ALPHA, ALU.mult)
                self.finish(c, l)
            self.store_y(c)
        for name, ch in self.S.chan.items():
            if name.startswith("st"):
                self.nc.gpsimd.wait_ge(ch.sem, ch.n)
        self.st.close()
        return self.nc


def _consts(SEQ):
    c = {}
    c["ident"] = np.eye(128, dtype=np.float32)
    c["oh8"] = np.eye(8, dtype=np.float32)
    j = np.arange(128)[:, None]
    i = np.arange(128)[None, :]
    p_incl = np.where(i >= j, 0.0, NEG)
    p_strict = np.where(i > j, 0.0, NEG)
    same = (i // 32) == (j // 32)
    jreal = (j % 32) >= 16
    s_incl = np.where(same & jreal & (i >= j), 0.0, NEG)
    s_strict = np.where(same & jreal & (i > j), 0.0, NEG)
    c["maskbias"] = np.concatenate([p_incl, p_strict, s_incl, s_strict], 1).astype(np.float32)
    tt = np.arange(TN)
    reset_p = np.where(tt % 128 == 0, 0.0, 1.0)
    reset_s = np.where(tt % 32 == 0, 0.0, 1.0)[:SN]
    real_s = np.where((tt % 32) >= 16, 1.0, 0.0)[:SN]
    gm = np.concatenate([reset_p, reset_s, real_s], 0).reshape(1, TN + 2 * SN)
    c["gatemask"] = np.repeat(gm, 8, 0).astype(np.float32)
    tok = np.arange(128)
    tokm = np.zeros((128, 8), np.float32)
    tokm[:, 0] = (tok % 32) >= 16
    for s in range(4):
        tokm[:, 1 + s] = ((tok // 32) == s) & ((tok % 32) >= 16)
    c["tokmask"] = tokm
    am = np.zeros((128, 4 * TN + 18 * 128), np.float32)
    q = np.arange(TN)[None, :]
    for kb in range(4):
        kp = kb * 128 + np.arange(128)[:, None]
        am[:, kb * TN:(kb + 1) * TN] = (kp // 64) <= (q // 64)
    qs = np.arange(128)[None, :]
    for s in range(4):
        am[:, 4 * TN + s * 128:4 * TN + (s + 1) * 128] = np.broadcast_to((qs // 32) == s, (128, 128))
    ks = np.arange(128)[:, None]
    am[:, 4 * TN + 4 * 128:4 * TN + 5 * 128] = ((ks // 32) == (qs // 32)) & ((ks % 32) >= 16)
    o0 = 4 * TN + 5 * 128
    ri = np.arange(128)[:, None]
    ci = np.arange(128)[None, :]
    am[:, o0:o0 + 128] = (ri // 2) == (ci // 2)
    for lev, s_ in enumerate((2, 4, 8, 16, 32, 64)):
        ML = ((ri // (2 * s_)) == (ci // (2 * s_))) & ((ri % (2 * s_)) >= s_) & ((ci % (2 * s_)) < s_)
        am[:, o0 + (1 + 2 * lev) * 128:o0 + (2 + 2 * lev) * 128] = ML
        am[:, o0 + (2 + 2 * lev) * 128:o0 + (3 + 2 * lev) * 128] = ML.T
    c["attnmask"] = am
    prot = np.zeros((64, 64), np.float32)
    for m in range(32):
        prot[m + 32, m] = -1.0
        prot[m, m + 32] = 1.0
    c["prot"] = prot
    half = 32
    inv_freq = (10000.0 ** (-np.arange(half, dtype=np.float32) / half)).astype(np.float32)
    pos = np.concatenate([np.arange(SEQ), np.tile(np.concatenate([np.zeros(16), PAST + np.arange(16)]), 4)]).astype(np.float32)
    ang = pos[None, :] * inv_freq[:, None]
    c["ropeC"] = np.concatenate([np.cos(ang), np.cos(ang)], 0).astype(np.float32)
    c["ropeS"] = np.concatenate([np.sin(ang), np.sin(ang)], 0).astype(np.float32)
    return c


def _fm(v, nchunk):
    return np.ascontiguousarray(np.moveaxis(v.reshape(v.shape[:-1] + (nchunk, 128)), -1, 0))


def host_inputs(inp, NT, core_seq, core_samp):
    SEQ = NT * TN
    f = lambda a: np.ascontiguousarray(np.asarray(a, dtype=np.float32))
    cst = _consts(SEQ)
    shared = dict(cst)
    for k in ("mlp_w_up", "mlp_w_down", "ple_w_proj", "ple_w_gate", "dn_w_in", "dn_w_o", "mla_w_in", "mla_w_uq", "mla_w_o"):
        shared[k] = f(inp[k])
    shared["mla_w_uk"] = f(inp["mla_w_uk"]).reshape(2, 256, 1024)
    shared["mla_w_uv"] = f(inp["mla_w_uv"]).reshape(2, 256, 1024)
    vec = np.stack([f(inp[k]) for k in ("ln1_g", "ln1_b", "ln2_g", "ln2_b", "ple_norm")], 1)
    shared["vec1024"] = _fm(vec, 8).reshape(128, NL * 5 * 8)
    shared["convw"] = np.ascontiguousarray(np.transpose(f(inp["dn_conv_w"]).reshape(2, 4, 24, 128), (3, 0, 2, 1))).reshape(128, 2 * 24 * 4)
    shared["qnorm"] = _fm(f(inp["mla_q_norm"]), 4).reshape(128, 8)
    shared["kvnorm"] = _fm(f(inp["mla_kv_norm"]), 2).reshape(128, 4)
    shared["onorm"] = np.ascontiguousarray(f(inp["dn_o_norm"]).T)
    shared["alog"] = np.ascontiguousarray(f(inp["dn_a_log"]).T)
    shared["dtb"] = np.ascontiguousarray(f(inp["dn_dt_bias"]).T)
    maps = []
    xp = f(inp["x_prompt"])
    pp = f(inp["p_prompt"])
    xs = f(inp["x_sample"])
    ps = f(inp["p_sample"])
    for c in range(len(core_seq)):
        m = dict(shared)
        b = core_seq[c]
        if b is None:
            m["xT"] = np.zeros((D, SEQ), np.float32)
            m["pT"] = np.zeros((NL, 256, SEQ), np.float32)
        else:
            m["xT"] = np.ascontiguousarray(xp[b, :SEQ].T)
            m["pT"] = np.ascontiguousarray(np.transpose(pp[:, b, :SEQ], (0, 2, 1)))
        ss = core_samp[c]
        xsT = np.zeros((D, 4, 32), np.float32)
        psT = np.zeros((NL, 256, 4, 32), np.float32)
        for i, s in enumerate(ss):
            xsT[:, i, 16:] = xs[s].T
            psT[:, :, i, 16:] = np.transpose(ps[:, s], (0, 2, 1))
        m["xsT"] = xsT.reshape(D, SN)
        m["psT"] = psT.reshape(NL, 256, SN)
        m["sconvT"] = np.ascontiguousarray(np.transpose(f(inp["state_dn_conv"])[:, ss], (0, 1, 3, 2)))
        m["srec"] = np.ascontiguousarray(f(inp["state_dn_recurrent"])[:, ss])
        m["ckvT"] = np.ascontiguousarray(np.transpose(f(inp["cache_mla_ckv"])[:, ss], (0, 1, 3, 2)))
        m["krT"] = np.ascontiguousarray(np.transpose(f(inp["cache_mla_krope"])[:, ss], (0, 1, 3, 2)))
        maps.append(m)
    return maps


def dn_layer(self, c, l):
    j = l // 2
    N = c.N
    A = self.A
    H = self.H
    nblk = c.nblk
    U = c.U
    nun = N // U
    samp = c.sample
    mbI = self.MB[:, (2 if samp else 0) * 128:(3 if samp else 1) * 128]
    mbS = self.MB[:, (3 if samp else 1) * 128:(4 if samp else 2) * 128]
    reset = self.GM[:, TN:TN + SN] if samp else self.GM[:, 0:TN]
    realT = self.GM[:, TN + SN:TN + 2 * SN]
    wba = self.wnext(("dn_in", j, "ba"))
    pb_, pa_ = self.P[2], self.P[3]
    for kc in range(8):
        self.mm(pb_[0:8, :N], wba[:, kc, 0:8], self.YB[kc][:, :N], start=(kc == 0), stop=(kc == 7))
    for kc in range(8):
        self.mm(pa_[0:8, :N], wba[:, kc, 8:16], self.YB[kc][:, :N], start=(kc == 0), stop=(kc == 7))
    BETA, LNB, G, GC, GCB, EGC, DKD, NGC, NS1, TMP = [A[k][0:8, :N] for k in range(10)]
    self.act(BETA, pb_[0:8, :N], AF.Sigmoid)
    self.act(LNB, BETA, AF.Ln)
    self.act(G, pa_[0:8, :N], AF.Exp, bias=self.DTB[:, j:j + 1])
    self.act(G, G, AF.Ln, bias=1.0)
    self.ts("dve", G, G, self.NEGA[:, j:j + 1], ALU.mult)
    if samp:
        self.tt("dve", G, G, realT, ALU.mult)
    r, w = _rw([GC], [reset, G])
    self.S.op("dve", lambda v: v.tensor_tensor_scan(GC.ap, reset.ap, G.ap, 0.0, ALU.mult, ALU.add), r, w)
    self.tt("dve", GCB, GC, LNB, ALU.add)
    self.act(EGC, GC, AF.Exp)
    gc3 = GC.re("p (u k) -> p u k", k=U)
    gl = gc3[:, :, U - 1:U]
    self.tt("dve", TMP.re("p (u k) -> p u k", k=U), gl.bc([8, nun, U]), gc3, ALU.subtract)
    self.act(DKD, TMP, AF.Exp)
    if samp:
        self.tt("dve", DKD, DKD, realT, ALU.mult)
    EGL = self.EGLT[:, 0:nun]
    self.act(EGL.re("p (u k) -> p u k", k=1), gl, AF.Exp)
    self.ts("dve", NGC, GC, -1.0, ALU.mult)
    self.act(NS1, GCB, AF.Exp)
    self.ts("dve", NS1, NS1, -1.0, ALU.mult)
    for b in range(nblk):
        cb = slice(b * 128, (b + 1) * 128)
        ps = self.P[6 + b % 2]
        for q, src in enumerate((NGC, NS1, BETA, DKD)):
            self.tr(ps[:, q * 8:(q + 1) * 8], src[:, cb], self.IDENT[0:8, 0:8])
        self.cp("dve", self.TS[b][:, :], ps[:, 0:32])
    if samp:
        for s in range(4):
            self.ts("dve", self.DKS[:, s * 8:(s + 1) * 8], self.TS[0][:, 24:32], self.TOKM[:, 1 + s:2 + s], ALU.mult)
    if samp:
        segs = [(s * 32, 32, s) for s in range(4)]
    else:
        segs = [(0, 128, 0)]
    sets = [dict(VSb=H[1], QHb=H[2], KHb=H[3], QGb=H[4], ZSb=H[5], DI=A[15], DS=A[16], EGLB=self.EGLB2[0]),
            dict(VSb=H[18], QHb=H[19], KHb=H[20], QGb=H[21], ZSb=H[22], DI=A[0], DS=A[1], EGLB=self.EGLB2[1])]
    blks = range(nblk)
    cbs = [slice(b * 128, (b + 1) * 128) for b in blks]
    MF = self.MF
    o0 = 4 * TN + 5 * 128
    msk = lambda k: self.AM[:, o0 + k * 128:o0 + (k + 1) * 128]
    PO = self.P[5]
    st = {"pbi": 0}

    def prep(h):
        S_ = sets[h % 2]
        wp = self.wnext(("dn_in", j, h))
        QKV = [A[10], A[11], A[12]]
        VSb = S_["VSb"][:, :N]
        for g in range(4):
            ps = self.P[st["pbi"] % 2]
            st["pbi"] += 1
            for kc in range(8):
                self.mm(ps[:, :N], wp[:, kc, g, :], self.YB[kc][:, :N], start=(kc == 0), stop=(kc == 7))
            if g == 3:
                sg = A[17][:, :N]
                self.act(sg, ps[:, :N], AF.Exp, scale=-1.0)
                self.act(sg, sg, AF.Ln, bias=1.0)
                self.act(sg, sg, AF.Exp, scale=-1.0)
                self.tt("dve", S_["ZSb"][:, :N], ps[:, :N], sg, ALU.mult)
                yield
                continue
            xs = self.XS[g]
            self.cp("act", xs[:, 3:3 + N], ps[:, :N])
            ch = g * 8 + h
            if samp:
                for s in range(4):
                    self.dma("sp", "ldcs", xs[:, 3 + 32 * s + 13:3 + 32 * s + 16],
                             (self.sconvT[j, s, ch * 128:(ch + 1) * 128, :], []))
                    self.store(self.o_sconv[j, s, ch * 128:(ch + 1) * 128, :], xs[:, 3 + 32 * s + 29:3 + 32 * s + 32], "stc")
            else:
                self.cp("dve", xs[:, 0:3], self.HALO[j][ch][:, :])
                self.cp("dve", self.HALO[j][ch][:, :], xs[:, N:N + 3])
                if c.last:
                    self.store(self.o_pconv[j, ch * 128:(ch + 1) * 128, :], self.HALO[j][ch][:, :], "stc")
            yield
            cw = lambda i: self.CONVW[:, ((j * 24 + ch) * 4 + i):((j * 24 + ch) * 4 + i + 1)]
            acc = QKV[g][:, :N]
            self.ts("dve", acc, xs[:, 0:N], cw(0), ALU.mult)
            self.stt(acc, xs[:, 1:1 + N], cw(1), acc, ALU.mult, ALU.add)
            yield
            self.stt(acc, xs[:, 2:2 + N], cw(2), acc, ALU.mult, ALU.add)
            self.stt(acc, xs[:, 3:3 + N], cw(3), acc, ALU.mult, ALU.add)
            sg = A[17][:, :N]
            self.act(sg, acc, AF.Exp, scale=-1.0)
            self.act(sg, sg, AF.Ln, bias=1.0)
            self.act(sg, sg, AF.Exp, scale=-1.0)
            yield
            if g == 2:
                self.tt("dve", VSb, acc, sg, ALU.mult)
            else:
                self.tt("dve", acc, acc, sg, ALU.mult)
            yield
        QS, KS = QKV[0][:, :N], QKV[1][:, :N]
        QHb = S_["QHb"][:, :N]
        KHb = S_["KHb"][:, :N]
        nrm = ((QS, QHb, 128.0 ** -0.5, A[13], H[0]), (KS, KHb, 1.0, A[14], H[23]))
        pss = []
        for src, dst, sc, rt, sqt in nrm:
            self.act(sqt[:, :N], src, AF.Square)
        yield
        for src, dst, sc, rt, sqt in nrm:
            ps = self.P[st["pbi"] % 2]
            st["pbi"] += 1
            pss.append(ps)
            self.mm(ps[:, :N], self.ONESB[:], sqt[:, :N])
        yield
        for (src, dst, sc, rt, sqt), ps in zip(nrm, pss):
            self.act(rt[:, :N], ps[:, :N], AF.Ln, bias=EPS)
            self.act(rt[:, :N], rt[:, :N], AF.Exp, scale=-0.5)
        yield
        for src, dst, sc, rt, sqt in nrm:
            self.stt(dst, src, sc, rt[:, :N], ALU.mult, ALU.mult)
        oh = self.OH8[:, h:h + 1]
        DI = S_["DI"][:, :N]
        DS = S_["DS"][:, :N]
        QGb = S_["QGb"][:, :N]
        PB = self.P[2]
        tmf = [A[2][:, :N], A[6][:, :N], A[7][:, :N], A[8][:, :N]]
        tm = [t[0:8, :] for t in tmf]
        self.ts("dve", tm[0], GC, oh, ALU.mult)
        self.ts("dve", tm[1], GCB, oh, ALU.mult)
        self.ts("dve", tm[2], EGC, oh, ALU.mult)
        self.ts("dve", tm[3][:, 0:nun], EGL, oh, ALU.mult)
        yield
        self.mm(PB[:, :N], self.ONES8[:, :], tmf[0])
        yield
        for b in range(nblk):
            self.tt("dve", DI[:, cbs[b]], PB[:, cbs[b]], mbI, ALU.add)
        self.mm(PB[:, :N], self.ONES8[:, :], tmf[1])
        yield
        for b in range(nblk):
            self.tt("dve", DS[:, cbs[b]], PB[:, cbs[b]], mbS, ALU.add)
        self.mm(PB[:, :N], self.ONES8[:, :], tmf[2])
        yield
        self.tt("dve", QGb, QHb, PB[:, :N], ALU.mult)
        self.mm(PB[:, 0:nun], self.ONES8[:, :], tmf[3][:, 0:nun])
        yield
        self.cp("dve", S_["EGLB"][:, 0:nun], PB[:, 0:nun])
        yield

    def solve(h):
        S_ = sets[h % 2]
        VSb = S_["VSb"][:, :N]
        QHb = S_["QHb"][:, :N]
        KHb = S_["KHb"][:, :N]
        QGb = S_["QGb"][:, :N]
        DI = S_["DI"][:, :N]
        DS = S_["DS"][:, :N]
        EGLB = S_["EGLB"]
        attnT, TT, T_, vb, kds, n0, n0t = ({} for _ in range(7))
        DTs, DTi, r0s, r1s, r2s, r3s, r4s = ({} for _ in range(7))
        for b in blks:
            ngc = self.TS[b][:, h:h + 1]
            DTs[b] = self.newM()
            DTi[b] = self.newM()
            self.act(DTs[b][:, :], DS[:, cbs[b]], AF.Exp, bias=ngc)
            self.act(DTi[b][:, :], DI[:, cbs[b]], AF.Exp, bias=ngc)
            r0s[b] = self.R()
            self.mm(r0s[b], KHb[:, cbs[b]], KHb[:, cbs[b]])
        yield
        for b in blks:
            r2s[b] = self.R()
            self.mm(r2s[b], KHb[:, cbs[b]], QHb[:, cbs[b]])
            n0t[b] = MF[b * 6 + 5]
            self.stt(n0t[b][:, :], r0s[b], -1.0, DTs[b][:, :], ALU.mult, ALU.mult)
        yield
        for b in blks:
            r3s[b] = self.Rb()
            self.tr(r3s[b], VSb[:, cbs[b]], self.IDENTB[:])
            attnT[b] = MF[b * 6 + 0]
            self.tt("dve", attnT[b][:, :], r2s[b], DTi[b][:, :], ALU.mult)
            TT[b] = self.newM()
            self.tt("pool", TT[b][:, :], n0t[b][:, :], msk(0), ALU.mult)
            self.tt("pool", TT[b][:, :], TT[b][:, :], self.IDENTB[:], ALU.add)
        yield
        for b in blks:
            r1s[b] = self.Rb()
            self.tr(r1s[b], n0t[b][:, :], self.IDENTB[:])
            vb[b] = MF[b * 6 + 1]
            self.ts("dve", vb[b][:, :], r3s[b], self.TS[b][:, 16 + h:17 + h], ALU.mult)
        yield
        for b in blks:
            r4s[b] = self.Rb()
            self.tr(r4s[b], KHb[:, cbs[b]], self.IDENTB[:])
            n0[b] = MF[b * 6 + 4]
            T_[b] = self.newM()
            self.tt("dve", T_[b][:, :], r1s[b], msk(0), ALU.mult)
            self.cp("act", n0[b][:, :], r1s[b])
            self.tt("pool", T_[b][:, :], T_[b][:, :], self.IDENTB[:], ALU.add)
        yield
        for b in blks:
            kds[b] = []
            for si, (p0, pl_, s) in enumerate(segs):
                kd = MF[b * 6 + 2] if si == 0 else MF[23 + si]
                if samp:
                    self.ts("dve", kd[:, :], r4s[b], self.DKS[:, s * 8 + h:s * 8 + h + 1], ALU.mult)
                else:
                    self.ts("dve", kd[:, :], r4s[b], self.TS[b][:, 24 + h:25 + h], ALU.mult)
                kds[b].append(kd)
        yield
        for lev in range(c.L):
            last = (lev == c.L - 1)
            X = {}
            for b in blks:
                rx = self.R()
                self.mm(rx, n0t[b][:, :], T_[b][:, :])
                X[b] = self.newM()
                self.tt("dve", X[b][:, :], rx, msk(1 + 2 * lev), ALU.mult)
            yield
            for b in blks:
                rp = self.R()
                self.mm(rp, X[b][:, :], TT[b][:, :])
                TT2 = MF[b * 6 + 3] if last else self.newM()
                self.tt("dve", TT2[:, :], rp, TT[b][:, :], ALU.add)
                if not last:
                    rq = self.R()
                    self.mm(rq, TT[b][:, :], X[b][:, :])
                    T2 = self.newM()
                    self.tt("dve", T2[:, :], rq, T_[b][:, :], ALU.add)
                    T_[b] = T2
                TT[b] = TT2
                if b % 2 == 1:
                    yield
        if not samp:
            Sbh = self.SSB[h % 2]
            self.cp("act", Sbh[:, :], self.SST[j][h][:, :])
        for b in blks:
            cb = cbs[b]
            ns1 = self.TS[b][:, 8 + h:9 + h]
            Ss = []
            r5 = self.R()
            for si, (p0, pl_, s) in enumerate(segs):
                if samp:
                    Sf = self.S0[s]
                    self.dma("sp", "lds0", Sf[:, :], (self.srec[j, s, h], []))
                    Sb_ = self.SB[s]
                    self.cp("act", Sb_[:, :], Sf[:, :])
                else:
                    Sf = self.SST[j][h]
                    Sb_ = Sbh
                Ss.append((Sf, Sb_))
                kw = {} if pl_ == 128 else {"tile_position": (0, p0)}
                self.mm(r5[p0:p0 + pl_, :], KHb[:, b * 128 + p0:b * 128 + p0 + pl_], Sb_[:, :], **kw)
            rr_ = MF[27 + b % 2]
            self.stt(rr_[:, :], r5, ns1, vb[b][:, :], ALU.mult, ALU.add)
            yield
            r6 = self.R()
            self.mm(r6, TT[b][:, :], rr_[:, :])
            vn = MF[29 + b % 2]
            self.cp("act", vn[:, :], r6)
            yield
            for si, (p0, pl_, s) in enumerate(segs):
                cs = slice(b * 128 + p0, b * 128 + p0 + pl_)
                self.mm(PO[:, cs], Ss[si][1][:, :], QGb[:, cs], start=(si == 0), stop=False)
            self.mm(PO[:, cb], vn[:, :], attnT[b][:, :], start=False, stop=True)
            for si, (p0, pl_, s) in enumerate(segs):
                Sf, Sb_ = Ss[si]
                r7 = self.R()
                self.mm(r7, kds[b][si][:, :], vn[:, :])
                u = b * (128 // U) + (si if samp else 0)
                if samp:
                    So = self.S1[self.s0i % 2]
                    self.s0i += 1
                    self.stt(So[:, :], Sf[:, :], EGLB[:, u:u + 1], r7, ALU.mult, ALU.add)
                    self.store(self.o_srec[j, s, h], So[:, :], "sts")
                else:
                    self.stt(Sf[:, :], Sf[:, :], EGLB[:, u:u + 1], r7, ALU.mult, ALU.add)
                    self.cp("act", Sb_[:, :], Sf[:, :])
            yield
        sq = H[24][:, :N]
        self.act(sq, PO[:, :N], AF.Square)
        self.mm(self.P[2][:, :N], self.ONESB[:], sq)
        rn = A[18][:, :N]
        self.act(rn, self.P[2][:, :N], AF.Ln, scale=1.0 / 128.0, bias=EPS)
        self.act(rn, rn, AF.Exp, scale=-0.5)
        self.stt(rn, PO[:, :N], self.ON[:, j:j + 1], rn, ALU.mult, ALU.mult)
        self.tt("dve", H[8 + h][:, :N], rn, S_["ZSb"][:, :N], ALU.mult)
        if (not samp) and c.last:
            self.store(self.o_prec[j, h], self.SST[j][h][:, :], "sts")
        yield

    def drive(gens):
        gens = [g for g in gens if g is not None]
        while gens:
            for g in list(gens):
                try:
                    next(g)
                except StopIteration:
                    gens.remove(g)

    drive([prep(0)])
    for h in range(8):
        drive([solve(h), prep(h + 1) if h < 7 else None])
    self.out_proj(c, ("dn_o", j))


def out_proj(self, c, key):
    N = c.N
    pb = 0
    for half in range(2):
        wo = self.wnext(key + (half,))
        for m4 in range(4):
            m = half * 4 + m4
            ps = self.P[pb % 2]
            pb += 1
            for hh in range(8):
                self.mm(ps[:, :N], wo[:, hh, m4 * 128:(m4 + 1) * 128], self.H[8 + hh][:, :N], start=(hh == 0), stop=(hh == 7))
            self.residual(c, m, ps[:, :N])


MK.dn_layer = dn_layer
MK.out_proj = out_proj


def rope(self, c, src, dst, tmp):
    N = c.N
    ps = self.P[6]
    self.mm(ps[0:64, :N], self.PROT[:, :], src)
    self.tt("dve", tmp, src, self.COS[:, :N], ALU.mult)
    self.tt("dve", dst, ps[0:64, :N], self.SIN[:, :N], ALU.mult)
    self.tt("pool", dst, dst, tmp, ALU.add)


def make_k(self, wuk, CKB, N):
    for h in range(8):
        ps = self.P[h % 2]
        for kc in range(2):
            self.mm(ps[:, :N], wuk[:, kc, h * 128:(h + 1) * 128], CKB[kc][:, :N], start=(kc == 0), stop=(kc == 1))
        self.evac(self.H[19 + h][:, :N], ps[:, :N])


def make_v(self, wuv, CKB, nblk):
    for kb in range(nblk):
        for half in range(2):
            ps = self.P[half]
            for kc in range(2):
                self.mm(ps[:, :], CKB[kc][:, kb * 128:(kb + 1) * 128], wuv[:, kc, half * 512:(half + 1) * 512], start=(kc == 0), stop=(kc == 1))
            self.evac(self.VT[kb][:, half * 512:(half + 1) * 512], ps[:, :])


def spill_k(self, j, key0, N):
    b = Buf()
    dst = self.Kscr[j].rearrange("h p n -> p h n")[:, :, key0:key0 + N]
    self.S.dma("pool", "spk", dst, self.Ht[:, 19:27, 0:N], reads=[self.H[19 + h].b for h in range(8)], writes=[b])
    return b


def spill_v(self, j, key0, nblk):
    b = Buf()
    dst = self.Vscr[j][:, key0 // 128:key0 // 128 + nblk, :]
    src = self.BXt[:, 0:2 * nblk, :].rearrange("p (b a) n -> p b (a n)", a=2)
    self.S.dma("pool", "spv", dst, src, reads=[self.VT[kb].b for kb in range(nblk)], writes=[b])
    return b


def sample_cache_kv(self):
    A = self.A
    H = self.H
    for j in range(2):
        if 2 * j + 1 >= self.NLAY:
            continue
        bl = []
        grp = [(s, g) for s in range(4) for g in range(2)]
        for gi, (s, g) in enumerate(grp):
            src = self.ckvT[j, s].rearrange("(c p) n -> p c n", p=128)[:, :, g * 512:(g + 1) * 512]
            self.S.dma("sp", "ldck", self.Awide[:, 12:14, :], src, reads=[], writes=[A[12].b, A[13].b])
            for kc in range(2):
                self.cp("pool", H[2 * gi + kc][:, :], A[12 + kc][:, :])
        for s in range(4):
            b = Buf()
            self.S.dma("pool", "spr", self.KRscr[j][:, self.SEQ + s * PAST:self.SEQ + (s + 1) * PAST], self.krT[j, s], reads=[], writes=[b])
            bl.append(b)
        wuk = self.wnext(("mla_uk", j))
        for gi, (s, g) in enumerate(grp):
            self.make_k(wuk, [H[2 * gi], H[2 * gi + 1]], TN)
            bl.append(self.spill_k(j, self.SEQ + s * PAST + g * 512, TN))
        wuv = self.wnext(("mla_uv", j))
        for gi, (s, g) in enumerate(grp):
            self.make_v(wuv, [H[2 * gi], H[2 * gi + 1]], 4)
            bl.append(self.spill_v(j, self.SEQ + s * PAST + g * 512, 4))
        self.kvbuf[(j, "cache")] = bl


def mla_layer(self, c, l):
    j = l // 2
    N = c.N
    A = self.A
    H = self.H
    samp = c.sample
    nblk = c.nblk
    pb = 0
    wi0 = self.wnext(("mla_in", j, 0))
    CQ = [A[8 + m] for m in range(4)]
    for m in range(4):
        ps = self.P[pb % 2]
        pb += 1
        for kc in range(8):
            self.mm(ps[:, :N], wi0[:, kc, m * 128:(m + 1) * 128], self.YB[kc][:, :N], start=(kc == 0), stop=(kc == 7))
        self.evac(CQ[m][:, :N], ps[:, :N])
    rstd = self.bstats(c, [CQ[m][:, :N] for m in range(4)], False, 512.0, self.P[2], self.P[3])
    for m in range(4):
        self.stt(H[4 + m][:, :N], CQ[m][:, :N], self.QN[:, j * 4 + m:j * 4 + m + 1], rstd[:, :N], ALU.mult, ALU.mult)
    wi1 = self.wnext(("mla_in", j, 1))
    CKV = [A[12], A[13]]
    CKVN = [A[14], A[15]]
    for m in range(2):
        ps = self.P[pb % 2]
        pb += 1
        for kc in range(8):
            self.mm(ps[:, :N], wi1[:, kc, m * 128:(m + 1) * 128], self.YB[kc][:, :N], start=(kc == 0), stop=(kc == 7))
        self.evac(CKV[m][:, :N], ps[:, :N])
    ps = self.P[pb % 2]
    pb += 1
    for kc in range(8):
        self.mm(ps[0:64, :N], wi1[:, kc, 256:320], self.YB[kc][:, :N], start=(kc == 0), stop=(kc == 7))
    KR = A[16][0:64, :N]
    self.evac(KR, ps[0:64, :N])
    rstd = self.bstats(c, [CKV[m][:, :N] for m in range(2)], False, 256.0, self.P[2], self.P[3])
    CKB = [H[16], H[17]]
    for m in range(2):
        self.stt(CKVN[m][:, :N], CKV[m][:, :N], self.KVN[:, j * 2 + m:j * 2 + m + 1], rstd[:, :N], ALU.mult, ALU.mult)
        if samp:
            self.store(self.o_sckv[j, m * 128:(m + 1) * 128, 0:N], CKVN[m][:, :N], "stk")
        else:
            self.store(self.o_pckv[j, m * 128:(m + 1) * 128, c.pos0:c.pos0 + N], CKVN[m][:, :N], "stk")
        self.cp("pool", CKB[m][:, :N], CKVN[m][:, :N])
    KRR = A[17][0:64, :N]
    self.rope(c, KR, KRR, A[18][0:64, :N])
    if samp:
        self.store(self.o_skr[j, :, 0:N], KRR, "stk")
    else:
        self.store(self.o_pkr[j, :, c.pos0:c.pos0 + N], KRR, "stk")
    KRB = H[18][0:64, :N]
    self.memset("pool", H[18][64:128, :N], 0.0)
    self.memset("pool", H[28][64:128, :N], 0.0)
    self.cp("pool", KRB, KRR)
    wuk = self.wnext(("mla_uk", j))
    self.make_k(wuk, CKB, N)
    wuv = self.wnext(("mla_uv", j))
    self.make_v(wuv, CKB, nblk)
    if not samp:
        bl = [self.spill_k(j, c.pos0, N), self.spill_v(j, c.pos0, nblk)]
        b = Buf()
        self.dma("pool", "spr", (self.KRscr[j][:, c.pos0:c.pos0 + N], [b]), KRB)
        bl.append(b)
        self.kvbuf[(j, c.t)] = bl
    srcs = []
    if samp:
        for s in range(4):
            msk = self.AM[:, 4 * TN + s * 128:4 * TN + (s + 1) * 128]
            srcs.append((self.SEQ + s * PAST, PAST, self.kvbuf[(j, "cache")], msk))
    else:
        t0 = 0
        while t0 < c.t:
            nt = min(2, c.t - t0)
            bl = []
            for tt_ in range(t0, t0 + nt):
                bl += self.kvbuf[(j, tt_)]
            srcs.append((t0 * TN, nt * TN, bl, None))
            t0 += nt
    PO = self.P[5]
    PD = self.P[4]
    for h in range(8):
        wq = self.wnext(("mla_uq", j, h))
        ps = self.P[pb % 2]
        pb += 1
        for kc in range(4):
            self.mm(ps[:, :N], wq[:, kc, 0:128], H[4 + kc][:, :N], start=(kc == 0), stop=(kc == 3))
        QT = H[27][:, :N]
        self.evac(QT, ps[:, :N])
        ps = self.P[pb % 2]
        pb += 1
        for kc in range(4):
            self.mm(ps[0:64, :N], wq[:, kc, 128:192], H[4 + kc][:, :N], start=(kc == 0), stop=(kc == 3))
        QR = A[10][0:64, :N]
        self.evac(QR, ps[0:64, :N])
        QRR = A[11][0:64, :N]
        self.rope(c, QR, QRR, A[18][0:64, :N])
        QRB = H[28][0:64, :N]
        self.cp("pool", QRB, QRR)
        blist = []
        for (key0, nk, bl, msk) in srcs:
            for kb in range(nk // 128):
                blist.append(("hist", key0, nk, bl, msk, kb))
        for kb in range(nblk):
            blist.append(("own", kb))
        nb_tot = len(blist)
        DEN = A[9][:, :N]
        cur = {}

        def emit_st(i):
            d = blist[i]
            if d[0] == "hist":
                _, key0, nk, bl, msk, kb = d
                if kb == 0:
                    sl = self.hslot % 2
                    self.hslot += 1
                    nb = nk // 128
                    self.dma("sp", f"hk{sl}", self.HK[sl][:, 0:nk], (self.Kscr[j, h][:, key0:key0 + nk], bl))
                    self.dma("sp", f"hv{sl}", self.HV[sl][:, 0:nb, :], (self.Vscr[j][:, key0 // 128:key0 // 128 + nb, h * 128:(h + 1) * 128], bl))
                    self.dma("sp", f"hr{sl}", self.HKR[sl][0:64, 0:nk], (self.KRscr[j][:, key0:key0 + nk], bl))
                    cur["sl"] = sl
                sl = cur["sl"]
                Kv = self.HK[sl][:, kb * 128:(kb + 1) * 128]
                KRv = self.HKR[sl][:, kb * 128:(kb + 1) * 128]
                Vv = self.HV[sl][:, kb, :]
                c0 = 0
            else:
                kb = d[1]
                Kv = H[19 + h][:, kb * 128:(kb + 1) * 128]
                KRv = H[18][:, kb * 128:(kb + 1) * 128]
                Vv = self.VT[kb][:, h * 128:(h + 1) * 128]
                if samp:
                    msk = self.AM[:, 4 * TN + 4 * 128:4 * TN + 5 * 128]
                    c0 = 0
                else:
                    c0 = kb * 128
                    msk = self.AM[:, kb * TN + c0:(kb + 1) * TN]
            ps_ = self.P[(2, 3, 6, 7)[i % 4]]
            self.mm(ps_[:, c0:N], Kv, QT[:, c0:N], start=True, stop=False)
            self.mm(ps_[:, c0:N], KRv, H[28][:, c0:N], start=False, stop=True)
            PT = (H[29], H[30], H[31], H[0], H[1], H[2])[i % 6][:, c0:N]
            self.act(PT, ps_[:, c0:N], AF.Exp, scale=MLA_SCALE)
            if msk is not None:
                self.tt("dve", PT, PT, msk, ALU.mult)
            return (PT, Vv, c0)

        GB = 2
        groups = [list(range(g0, min(g0 + GB, nb_tot))) for g0 in range(0, nb_tot, GB)]
        pend = [emit_st(i) for i in groups[0]]
        for gi, grp in enumerate(groups):
            cur_items = pend
            if gi + 1 < len(groups):
                pend = [emit_st(i) for i in groups[gi + 1]]
            for i, (PT, Vv, c0) in zip(grp, cur_items):
                self.mm(PO[:, c0:N], Vv, PT, start=(i == 0), stop=(i == nb_tot - 1))
            for i, (PT, Vv, c0) in zip(grp, cur_items):
                if i == 0:
                    self.cp("dve", DEN[:, c0:N], PT)
                else:
                    self.tt("dve", DEN[:, c0:N], DEN[:, c0:N], PT, ALU.add)
        self.mm(PD[:, :N], self.ONESF[:], DEN)
        rden = A[8][:, :N]
        self.recip(rden, PD[:, :N])
        self.tt("dve", H[8 + h][:, :N], PO[:, :N], rden, ALU.mult)
    self.out_proj(c, ("mla_o", j))


MK.mla_layer = mla_layer
MK.sample_cache_kv = sample_cache_kv
MK.rope = rope
MK.make_k = make_k
MK.make_v = make_v
MK.spill_k = spill_k
MK.spill_v = spill_v


PROMPT_CORES = [0, 1, 4, 5]
_NC_CACHE = {}


def kernel(**inputs):
    NT = SEQ_FULL // TN
    if "nc" not in _NC_CACHE:
        mk = MK(NT=NT, NLAY=4, sample=True)
        _NC_CACHE["nc"] = mk.build(mixers=True)
    nc = _NC_CACHE["nc"]
    core_seq = [None] * 8
    for b, cidx in enumerate(PROMPT_CORES):
        core_seq[cidx] = b
    core_samp = [[4 * c + i for i in range(4)] for c in range(8)]
    maps = host_inputs(inputs, NT, core_seq, core_samp)
    res = run_bass_kernel_spmd(nc, maps, core_ids=list(range(8)))
    R = res.results
    f32 = np.float32
    y_prompt = np.empty((4, SEQ_FULL, D), f32)
    y_sample = np.empty((32, 16, D), f32)
    p_conv = np.empty((2, 4, 3, 3072), f32)
    p_rec = np.empty((2, 4, 8, 128, 128), f32)
    p_ckv = np.empty((2, 4, SEQ_FULL, 256), f32)
    p_kr = np.empty((2, 4, SEQ_FULL, 64), f32)
    s_conv = np.empty((2, 32, 3, 3072), f32)
    s_rec = np.empty((2, 32, 8, 128, 128), f32)
    s_ckv = np.empty((2, 32, 16, 256), f32)
    s_kr = np.empty((2, 32, 16, 64), f32)
    for b, cidx in enumerate(PROMPT_CORES):
        r = R[cidx]
        y_prompt[b] = r["o_yT"].T
        p_conv[:, b] = np.transpose(r["o_pconv"], (0, 2, 1))
        p_rec[:, b] = r["o_prec"]
        p_ckv[:, b] = np.transpose(r["o_pckv"], (0, 2, 1))
        p_kr[:, b] = np.transpose(r["o_pkr"], (0, 2, 1))
    for c in range(8):
        r = R[c]
        sl = slice(4 * c, 4 * c + 4)
        y_sample[sl] = np.transpose(r["o_ysT"].reshape(D, 4, 32)[:, :, 16:], (1, 2, 0))
        s_conv[:, sl] = np.transpose(r["o_sconv"], (0, 1, 3, 2))
        s_rec[:, sl] = r["o_srec"]
        s_ckv[:, sl] = np.transpose(r["o_sckv"].reshape(2, 256, 4, 32)[..., 16:], (0, 2, 3, 1))
        s_kr[:, sl] = np.transpose(r["o_skr"].reshape(2, 64, 4, 32)[..., 16:], (0, 2, 3, 1))
    return (y_prompt, y_sample, p_conv, p_rec, p_ckv, p_kr, s_conv, s_rec, s_ckv, s_kr)
```

```python
import numpy as np
from contextlib import ExitStack
import concourse.bass as bass
import concourse.mybir as mybir
from concourse.bass_utils import run_bass_kernel_spmd

F32 = mybir.dt.float32
BF16 = mybir.dt.bfloat16
AF = mybir.ActivationFunctionType
ALU = mybir.AluOpType

D = 1024
DFF = 4096
NL = 4
ALPHA = 8.0 ** 0.25
EPS = 1e-6
MLA_SCALE = 192.0 ** -0.5
NEG = -30000.0
SEQ_FULL = 8192
TN = 512
SN = 128
PAST = 1024


class Buf:
    __slots__ = ("lw", "rd")

    def __init__(self):
        self.lw = None
        self.rd = {}


class Prod:
    def __init__(self, name, sem, step):
        self.name = name
        self.sem = sem
        self.step = step
        self.n = 0


class Sched:
    def __init__(self, nc, stack):
        self.nc = nc
        self.stack = stack
        self.h = {"pe": nc.tensor, "act": nc.scalar, "dve": nc.vector, "pool": nc.gpsimd, "sp": nc.sync}
        self.eng = {}
        for k in self.h:
            self.eng[k] = Prod(k, stack.enter_context(nc.semaphore("s_" + k)), 1)
        self.waited = {k: {} for k in self.h}
        self.chan = {}
        self.ninstr = 0

    def channel(self, name):
        if name not in self.chan:
            self.chan[name] = Prod(name, self.stack.enter_context(self.nc.semaphore("c_" + name)), 16)
        return self.chan[name]

    def _need(self, e, reads, writes):
        need = {}
        me = self.eng.get(e)
        for b in reads:
            if b.lw is not None:
                p, i = b.lw
                if need.get(p, 0) < i:
                    need[p] = i
        for b in writes:
            if b.lw is not None:
                p, i = b.lw
                if p is not me and need.get(p, 0) < i:
                    need[p] = i
            for p, i in b.rd.items():
                if p is not me and need.get(p, 0) < i:
                    need[p] = i
        return need

    def _waits(self, e, need):
        w = self.waited[e]
        h = self.h[e]
        for p, i in need.items():
            if w.get(p, 0) < i:
                h.wait_ge(p.sem, i)
                w[p] = i
                self.ninstr += 1

    def op(self, e, fn, reads=(), writes=()):
        need = self._need(e, reads, writes)
        if e == "pe":
            need.pop(self.eng["pe"], None)
        self._waits(e, need)
        prod = self.eng[e]
        inst = fn(self.h[e])
        prod.n += 1
        inst.then_inc(prod.sem, 1)
        self.ninstr += 1
        for b in reads:
            b.rd[prod] = prod.n
        for b in writes:
            b.lw = (prod, prod.n)
            b.rd = {}

    def dma(self, q, chan, out, in_, reads=(), writes=(), **kw):
        c = self.channel(chan)
        need = self._need(q, reads, writes)
        if c.n > 0 and need.get(c, 0) < c.n:
            need[c] = c.n
        self._waits(q, need)
        inst = self.h[q].dma_start(out=out, in_=in_, **kw)
        c.n += 16
        inst.then_inc(c.sem, 16)
        self.ninstr += 1
        for b in reads:
            b.rd[c] = c.n
        for b in writes:
            b.lw = (c, c.n)
            b.rd = {}


class Tl:
    def __init__(self, ap, buf=None, excl=False):
        self.ap = ap
        self.b = buf if buf is not None else Buf()
        self.excl = excl

    def __getitem__(self, idx):
        return Vw(self, self.ap[idx])

    def v(self, ap):
        return Vw(self, ap)


class Vw:
    def __init__(self, tl, ap):
        self.tl = tl
        self.ap = ap

    def __getitem__(self, idx):
        return Vw(self.tl, self.ap[idx])

    def re(self, s, **kw):
        return Vw(self.tl, self.ap.rearrange(s, **kw))

    def bc(self, shape):
        return Vw(self.tl, self.ap.to_broadcast(shape))


def _rw(outs, ins):
    r, w = [], []
    for v in ins:
        if isinstance(v, Vw):
            (w if v.tl.excl else r).append(v.tl.b)
    for v in outs:
        w.append(v.tl.b)
    return r, w


def _a(x):
    return x.ap if isinstance(x, Vw) else x


class TileCtx:
    pass


class MK:
    def __init__(self, NT=16, NLAY=4, sample=True, solve_dt=F32):
        self.NT = NT
        self.NLAY = NLAY
        self.sample = sample
        self.SEQ = NT * TN
        self.SD = solve_dt
        self.nc = bass.Bass("TRN2", target_bir_lowering=False)
        self.st = ExitStack()
        self.S = Sched(self.nc, self.st)
        self.rr = 0
        self.ev = 0
        self.castn = 0
        self._decl()
        self._alloc()

    def din(self, name, shape, dt=F32):
        return self.nc.dram_tensor(name, list(shape), dt, kind="ExternalInput").ap()

    def dout(self, name, shape, dt=F32):
        return self.nc.dram_tensor(name, list(shape), dt, kind="ExternalOutput").ap()

    def dint(self, name, shape, dt=BF16):
        return self.nc.dram_tensor(name, list(shape), dt, kind="Internal").ap()

    def _decl(self):
        SEQ = self.SEQ
        i = self.din
        self.xT = i("xT", [D, SEQ])
        self.pT = i("pT", [NL, 256, SEQ])
        self.xsT = i("xsT", [D, SN])
        self.psT = i("psT", [NL, 256, SN])
        self.sconvT = i("sconvT", [2, 4, 3072, 3])
        self.srec = i("srec", [2, 4, 8, 128, 128])
        self.ckvT = i("ckvT", [2, 4, 256, PAST])
        self.krT = i("krT", [2, 4, 64, PAST])
        self.w = {
            "up": i("mlp_w_up", [NL, D, DFF]), "down": i("mlp_w_down", [NL, DFF, D]),
            "pproj": i("ple_w_proj", [NL, 256, D]), "gate": i("ple_w_gate", [NL, D, D]),
            "dn_in": i("dn_w_in", [2, D, 4112]), "dn_o": i("dn_w_o", [2, D, D]),
            "mla_in": i("mla_w_in", [2, D, 832]), "mla_uq": i("mla_w_uq", [2, 512, 1536]),
            "mla_uk": i("mla_w_uk", [2, 256, 1024]), "mla_uv": i("mla_w_uv", [2, 256, 1024]),
            "mla_o": i("mla_w_o", [2, D, D]),
        }
        self.wb = {k: self.dint("wb_" + k, v.shape) for k, v in self.w.items()}
        self.wbuf = {k: [] for k in self.w}
        self.vec_d = i("vec1024", [128, NL * 5 * 8])
        self.convw_d = i("convw", [128, 2 * 24 * 4])
        self.qn_d = i("qnorm", [128, 2 * 4])
        self.kvn_d = i("kvnorm", [128, 2 * 2])
        self.on_d = i("onorm", [128, 2])
        self.alog_d = i("alog", [8, 2])
        self.dtb_d = i("dtb", [8, 2])
        self.ident_d = i("ident", [128, 128])
        self.oh8_d = i("oh8", [8, 8])
        self.mb_d = i("maskbias", [128, 4 * 128])
        self.gm_d = i("gatemask", [8, TN + 2 * SN])
        self.tokm_d = i("tokmask", [128, 8])
        self.am_d = i("attnmask", [128, 4 * TN + 18 * 128])
        self.prot_d = i("prot", [64, 64])
        self.ropeC = i("ropeC", [64, SEQ + SN])
        self.ropeS = i("ropeS", [64, SEQ + SN])
        o = self.dout
        self.o_yT = o("o_yT", [D, SEQ])
        self.o_ysT = o("o_ysT", [D, SN])
        self.o_pconv = o("o_pconv", [2, 3072, 3])
        self.o_prec = o("o_prec", [2, 8, 128, 128])
        self.o_pckv = o("o_pckv", [2, 256, SEQ])
        self.o_pkr = o("o_pkr", [2, 64, SEQ])
        self.o_sconv = o("o_sconv", [2, 4, 3072, 3])
        self.o_srec = o("o_srec", [2, 4, 8, 128, 128])
        self.o_sckv = o("o_sckv", [2, 256, SN])
        self.o_skr = o("o_skr", [2, 64, SN])
        self.obufs = []
        KS = SEQ + 4 * PAST
        self.KS = KS
        self.Kscr = self.dint("Kscr", [2, 8, 128, KS])
        self.Vscr = self.dint("Vscr", [2, 128, KS // 128, 1024])
        self.KRscr = self.dint("KRscr", [2, 64, KS])
        self.kvbuf = {}

    def sb(self, name, shape, dt):
        return self.st.enter_context(self.nc.sbuf_tensor(name, list(shape), dt))

    def _alloc(self):
        nc = self.nc
        sb = self.sb
        yt = sb("Y", [128, 8, TN], F32)
        self._ytensor = yt
        self.Y = [Tl(yt[:, c]) for c in range(8)]
        ybt = sb("YB", [128, 8, TN], BF16)
        self.YB = [Tl(ybt[:, c]) for c in range(8)]
        NA = 20
        at = sb("A", [128, NA, TN], F32)
        self.A = [Tl(at[:, c]) for c in range(NA)]
        self.Awide = at
        ht = sb("H", [128, 32, TN], BF16)
        self.H = [Tl(ht[:, c]) for c in range(32)]
        self.Ht = ht
        NBX = 10
        bt = sb("BX", [128, NBX, TN], BF16)
        self.BX = [Tl(bt[:, c]) for c in range(NBX)]
        self.BXt = bt
        self.VT = [Tl(bt[:, 2 * kb:2 * kb + 2, :].rearrange("p a n -> p (a n)")) for kb in range(4)]
        NM = 34
        mt = sb("M", [128, NM, 128], BF16)
        self.M = [Tl(mt[:, c]) for c in range(NM)]
        self.mi = 0
        mf = sb("MF", [128, 31, 128], BF16)
        self.MF = [Tl(mf[:, c]) for c in range(31)]
        xs = sb("XS", [128, 3, TN + 8], F32)
        self.XS = [Tl(xs[:, c]) for c in range(3)]
        self.NSLOT = 4
        rt = sb("RING", [128, self.NSLOT, 4096], BF16)
        self.ring = [Tl(rt[:, c]) for c in range(self.NSLOT)]
        hk = sb("HK", [128, 2, 1024], BF16)
        hv = sb("HV", [128, 2, 8, 128], BF16)
        hr = sb("HKR", [128, 2, 1024], BF16)
        self.HKRt = hr
        self.HK = [Tl(hk[:, c]) for c in range(2)]
        self.HV = [Tl(hv[:, c]) for c in range(2)]
        self.HKR = [Tl(hr[:, c]) for c in range(2)]
        self.hslot = 0
        self.P = [Tl(self.st.enter_context(nc.psum_tensor(f"P{i}", [128, TN], F32))[:], excl=True) for i in range(8)]
        self.IDENT = Tl(sb("IDENT", [128, 128], F32)[:])
        self.IDENTB = Tl(sb("IDENTB", [128, 128], BF16)[:])
        ssb = sb("SSB", [128, 2, 128], BF16)
        self.SSB = [Tl(ssb[:, k]) for k in range(2)]
        sbb = sb("SBs", [128, 4, 128], BF16)
        self.SB = [Tl(sbb[:, c]) for c in range(4)]
        self.ONESB = Tl(sb("ONESB", [128, 128], BF16)[:])
        self.ONESF = Tl(sb("ONESF", [128, 128], F32)[:])
        self.OH8 = Tl(sb("OH8", [8, 8], F32)[:])
        self.ONES8 = Tl(sb("ONES8", [128, 128], F32)[:])
        self.MB = Tl(sb("MB", [128, 4 * 128], F32)[:])
        self.GM = Tl(sb("GM", [8, TN + 2 * SN], F32)[:])
        self.TOKM = Tl(sb("TOKM", [128, 8], F32)[:])
        self.AM = Tl(sb("AM", [128, 4 * TN + 18 * 128], BF16)[:])
        self.PROT = Tl(sb("PROT", [64, 64], F32)[:])
        self.COS = Tl(sb("COS", [64, TN], F32)[:])
        self.SIN = Tl(sb("SIN", [64, TN], F32)[:])
        self.VEC = Tl(sb("VEC", [128, NL * 5 * 8], F32)[:])
        self.CONVW = Tl(sb("CONVW", [128, 2 * 24 * 4], F32)[:])
        self.QN = Tl(sb("QN", [128, 8], F32)[:])
        self.KVN = Tl(sb("KVN", [128, 4], F32)[:])
        self.ON = Tl(sb("ON", [128, 2], F32)[:])
        self.ALOG = Tl(sb("ALOG", [8, 2], F32)[:])
        self.NEGA = Tl(sb("NEGA", [8, 2], F32)[:])
        self.DTB = Tl(sb("DTB", [8, 2], F32)[:])
        sst = sb("SST", [128, 2, 8, 128], F32)
        self.SST = [[Tl(sst[:, j, h]) for h in range(8)] for j in range(2)]
        self.SSTt = sst
        halo = sb("HALO", [128, 2, 24, 3], F32)
        self.HALO = [[Tl(halo[:, j, c]) for c in range(24)] for j in range(2)]
        self.HALOt = halo
        ts = sb("TS", [128, 4, 32], F32)
        self.TS = [Tl(ts[:, b]) for b in range(4)]
        self.DKS = Tl(sb("DKS", [128, 32], F32)[:])
        eg = sb("EGLB", [128, 2, 16], F32)
        self.EGLB2 = [Tl(eg[:, k]) for k in range(2)]
        s0 = sb("S0", [128, 4, 128], F32)
        self.S0 = [Tl(s0[:, c]) for c in range(4)]
        self.EGLT = Tl(sb("EGLT", [8, 16], F32)[:])
        s1 = sb("S1", [128, 2, 128], F32)
        self.S1 = [Tl(s1[:, c]) for c in range(2)]
        self.s0i = 0

    def mm(self, out, lhsT, rhs, start=True, stop=True, **kw):
        r, w = _rw([out], [lhsT, rhs])
        self.S.op("pe", lambda t: t.matmul(out.ap, lhsT.ap, rhs.ap, start=start, stop=stop, **kw), r, w)

    def tr(self, out, in_, ident):
        r, w = _rw([out], [in_, ident])
        self.S.op("pe", lambda t: t.transpose(out.ap, in_.ap, ident.ap), r, w)

    def act(self, out, in_, func, scale=1.0, bias=None):
        r, w = _rw([out], [in_, scale, bias])
        kw = {}
        if bias is not None:
            kw["bias"] = _a(bias)
        self.S.op("act", lambda a: a.activation(out=out.ap, in_=in_.ap, func=func, scale=_a(scale), **kw), r, w)

    def tt(self, e, out, in0, in1, op):
        r, w = _rw([out], [in0, in1])
        self.S.op(e, lambda v: v.tensor_tensor(out.ap, in0.ap, in1.ap, op), r, w)

    def ts(self, e, out, in0, s1, op0, s2=None, op1=None):
        r, w = _rw([out], [in0, s1, s2])
        if op1 is None:
            self.S.op(e, lambda v: v.tensor_scalar(out.ap, in0.ap, _a(s1), None, op0), r, w)
        else:
            self.S.op(e, lambda v: v.tensor_scalar(out.ap, in0.ap, _a(s1), _a(s2), op0, op1), r, w)

    def stt(self, out, in0, scalar, in1, op0, op1):
        r, w = _rw([out], [in0, scalar, in1])
        self.S.op("dve", lambda v: v.scalar_tensor_tensor(out.ap, in0.ap, _a(scalar), in1.ap, op0, op1), r, w)

    def cp(self, e, out, in_):
        r, w = _rw([out], [in_])
        if e == "act":
            self.S.op("act", lambda a: a.activation(out=out.ap, in_=in_.ap, func=AF.Copy), r, w)
        else:
            self.S.op(e, lambda v: v.tensor_copy(out.ap, in_.ap), r, w)

    def evac(self, out, in_):
        self.ev += 1
        self.cp("act" if self.ev % 2 else "dve", out, in_)

    def recip(self, out, in_):
        r, w = _rw([out], [in_])
        self.S.op("dve", lambda v: v.reciprocal(out.ap, in_.ap), r, w)

    def memset(self, e, out, val):
        r, w = _rw([out], [])
        self.S.op(e, lambda v: v.memset(out.ap, val), r, w)

    def dma(self, q, chan, out, in_, **kw):
        reads, writes = [], []
        if isinstance(in_, Vw):
            reads.append(in_.tl.b)
            ia = in_.ap
        else:
            ia, bl = in_
            reads += bl
        if isinstance(out, Vw):
            writes.append(out.tl.b)
            oa = out.ap
        else:
            oa, bl = out
            writes += bl
        self.S.dma(q, chan, oa, ia, reads=reads, writes=writes, **kw)

    def newM(self):
        m = self.M[self.mi % len(self.M)]
        self.mi += 1
        return m

    def R(self):
        i = self.rr
        self.rr += 1
        bank = self.P[(3, 4, 6, 7)[i % 4]]
        c = ((i // 4) % 4) * 128
        return bank[:, c:c + 128]

    def Rb(self):
        r = self.R()
        return Vw(r.tl, r.ap[:, 0:64].bitcast(BF16))

    def cast_weights(self):
        order = ["dn_in", "dn_o", "up", "down", "pproj", "gate", "mla_in", "mla_uk", "mla_uv", "mla_uq", "mla_o"]
        for k in order:
            src = self.w[k]
            dst = self.wb[k]
            L = src.shape[0]
            R_ = src.shape[1]
            if k == "dn_in":
                for l in range(L):
                    for h in range(8):
                        b = Buf()
                        self.wbuf[k].append((l, b))
                        sv = src[l, :, 0:4096].rearrange("r (g h n) -> r g h n", g=4, h=8)[:, :, h, :]
                        dv = dst[l, :, h * 512:(h + 1) * 512].rearrange("r (g n) -> r g n", g=4)
                        self.S.dma("pool", f"cast{self.castn % 4}", dv, sv, writes=[b])
                        self.castn += 1
                    b = Buf()
                    self.wbuf[k].append((l, b))
                    self.S.dma("pool", f"cast{self.castn % 4}", dst[l, :, 4096:4112], src[l, :, 4096:4112], writes=[b])
                    self.castn += 1
                continue
            for l in range(L):
                nsplit = max(1, (R_ * src.shape[2]) // (1 << 20))
                rs = R_ // nsplit
                for s in range(nsplit):
                    b = Buf()
                    self.wbuf[k].append((l, b))
                    self.S.dma("pool", f"cast{self.castn % 4}", dst[l, s * rs:(s + 1) * rs, :], src[l, s * rs:(s + 1) * rs, :], writes=[b])
                    self.castn += 1

    def piece_src(self, key):
        k = key[0]
        l = key[1]
        wbv = self.wb[k][l]
        if k in ("up", "gate", "dn_o", "mla_o"):
            j = key[2]
            v = wbv.rearrange("(c p) n -> p c n", p=128)[:, :, j * 512:(j + 1) * 512]
            return v, [8, 512]
        if k == "down":
            m = key[2]
            v = wbv.rearrange("(c p) n -> p c n", p=128)[:, :, m * 128:(m + 1) * 128]
            return v, [32, 128]
        if k == "pproj":
            return wbv.rearrange("(c p) n -> p c n", p=128), [2, 1024]
        if k == "dn_in":
            h = key[2]
            if h == "ba":
                v = wbv.rearrange("(c p) n -> p c n", p=128)[:, :, 4096:4112]
                return v, [8, 16]
            v = wbv.rearrange("(c p) n -> p c n", p=128)[:, :, h * 512:(h + 1) * 512]
            return v, [8, 4, 128]
        if k == "mla_in":
            if key[2] == 0:
                return wbv.rearrange("(c p) n -> p c n", p=128)[:, :, 0:512], [8, 512]
            return wbv.rearrange("(c p) n -> p c n", p=128)[:, :, 512:832], [8, 320]
        if k == "mla_uq":
            h = key[2]
            return wbv.rearrange("(c p) n -> p c n", p=128)[:, :, h * 192:(h + 1) * 192], [4, 192]
        if k in ("mla_uk", "mla_uv"):
            return wbv.rearrange("(c p) n -> p c n", p=128), [2, 1024]
        raise KeyError(key)

    def plan_layer(self, l):
        j = l // 2
        pl = []
        if l % 2 == 0:
            pl.append(("dn_in", j, "ba"))
            pl += [("dn_in", j, h) for h in range(8)]
            pl += [("dn_o", j, 0), ("dn_o", j, 1)]
        else:
            pl += [("mla_in", j, 0), ("mla_in", j, 1), ("mla_uk", j), ("mla_uv", j)]
            pl += [("mla_uq", j, h) for h in range(8)]
            pl += [("mla_o", j, 0), ("mla_o", j, 1)]
        pl += [("up", l, jj) for jj in range(8)]
        pl += [("down", l, m) for m in range(8)]
        pl += [("pproj", l), ("gate", l, 0), ("gate", l, 1)]
        return pl

    def set_plan(self, plan):
        self.plan = plan
        self.pi = 0
        self.pl = 0

    def _load_piece(self, idx):
        key = self.plan[idx]
        src, shp = self.piece_src(key)
        slot = self.ring[idx % self.NSLOT]
        n = int(np.prod(shp))
        dst = slot[:, 0:n].re("p (c n) -> p c n", c=shp[0])
        bl = [b for (l, b) in self.wbuf[key[0]] if l == key[1]]
        self.dma("sp", f"w{idx % self.NSLOT}", dst, (src, bl))

    def wnext(self, key):
        assert self.plan[self.pi] == key, (self.plan[self.pi], key)
        while self.pl < len(self.plan) and self.pl < self.pi + self.NSLOT:
            self._load_piece(self.pl)
            self.pl += 1
        idx = self.pi
        self.pi += 1
        _, shp = self.piece_src(key)
        slot = self.ring[idx % self.NSLOT]
        n = int(np.prod(shp))
        v = slot[:, 0:n]
        if len(shp) == 2:
            return v.re("p (c n) -> p c n", c=shp[0])
        return v.re("p (c g n) -> p c g n", c=shp[0], g=shp[1])

    def setup(self):
        ld = lambda tl, d: self.dma("sp", "const", tl[:], (d, []))
        ld(self.IDENT, self.ident_d[:, :])
        ld(self.OH8, self.oh8_d[:, :])
        self.memset("dve", self.ONES8[:], 0.0)
        self.memset("pool", self.ONES8[0:8, :], 1.0)
        ld(self.MB, self.mb_d[:, :])
        ld(self.GM, self.gm_d[:, :])
        ld(self.TOKM, self.tokm_d[:, :])
        AMW = 4 * TN + 18 * 128
        amf = self.Awide[:, 0:9, :].rearrange("p a n -> p (a n)")[:, 0:AMW]
        self.S.dma("sp", "const", amf, self.am_d[:, :], reads=[], writes=[self.A[k].b for k in range(9)])
        ld(self.PROT, self.prot_d[:, :])
        ld(self.VEC, self.vec_d[:, :])
        ld(self.CONVW, self.convw_d[:, :])
        ld(self.QN, self.qn_d[:, :])
        ld(self.KVN, self.kvn_d[:, :])
        ld(self.ON, self.on_d[:, :])
        ld(self.ALOG, self.alog_d[:, :])
        ld(self.DTB, self.dtb_d[:, :])
        self.S.op("dve", lambda v: v.tensor_copy(self.AM.ap[:, :], amf), reads=[self.A[k].b for k in range(9)], writes=[self.AM.b])
        self.cp("dve", self.IDENTB[:], self.IDENT[:])
        self.memset("dve", self.ONESB[:], 1.0)
        self.memset("dve", self.ONESF[:], 1.0)
        for k in range(2):
            self.memset("pool", self.HKR[k][64:128, :], 0.0)
        for g in range(3):
            self.memset("pool", self.XS[g][:, :], 0.0)
        self.act(self.NEGA[:], self.ALOG[:], AF.Exp)
        self.ts("dve", self.NEGA[:], self.NEGA[:], -1.0, ALU.mult)
        for j in range(2):
            for h in range(8):
                self.memset("dve", self.SST[j][h][:], 0.0)
            for cc in range(24):
                self.memset("pool", self.HALO[j][cc][:], 0.0)

    def vec(self, l, kind, c):
        i = (l * 5 + kind) * 8 + c
        return self.VEC[:, i:i + 1]

    def prompt_ctx(self, t):
        c = TileCtx()
        c.sample = False
        c.t = t
        c.N = TN
        c.pos0 = t * TN
        c.nblk = 4
        c.U = 128
        c.nunits = 4
        c.last = (t == self.NT - 1)
        c.L = 6
        return c

    def sample_ctx(self):
        c = TileCtx()
        c.sample = True
        c.t = self.NT
        c.N = SN
        c.pos0 = self.SEQ
        c.nblk = 1
        c.U = 32
        c.nunits = 4
        c.last = True
        c.L = 4
        return c

    def load_tile(self, c):
        N = c.N
        src = (self.xsT if c.sample else self.xT).rearrange("(c p) n -> p c n", p=128)
        src = src[:, :, 0:N] if c.sample else src[:, :, c.pos0:c.pos0 + N]
        self.S.dma("sp", "ldx", self._ytensor[:, :, 0:N], src, reads=[], writes=[y.b for y in self.Y])
        for k in range(8):
            self.cp("pool", self.YB[k][:, :N], self.Y[k][:, :N])
        self.dma("sp", "ldrope", self.COS[:, :N], (self.ropeC[:, c.pos0:c.pos0 + N], []))
        self.dma("sp", "ldrope", self.SIN[:, :N], (self.ropeS[:, c.pos0:c.pos0 + N], []))

    def bstats(self, c, srcs, want_mean, nfeat, PSs, PSq):
        N = c.N
        n = len(srcs)
        for k, s in enumerate(srcs):
            sq = self.H[k % 4][:, :N]
            self.act(sq, s, AF.Square)
            self.mm(PSq[:, :N], self.ONESB[:], sq, start=(k == 0), stop=(k == n - 1))
            if want_mean:
                rb = self.H[4 + k % 4][:, :N]
                self.cp("dve", rb, s)
                self.mm(PSs[:, :N], self.ONESB[:], rb, start=(k == 0), stop=(k == n - 1))
        A = self.A
        inv = 1.0 / nfeat
        if want_mean:
            self.act(A[0][:, :N], PSs[:, :N], AF.Copy, scale=inv)
            self.tt("dve", A[1][:, :N], A[0][:, :N], A[0][:, :N], ALU.mult)
            self.stt(A[2][:, :N], PSq[:, :N], inv, A[1][:, :N], ALU.mult, ALU.subtract)
            self.act(A[2][:, :N], A[2][:, :N], AF.Ln, bias=EPS)
        else:
            self.act(A[2][:, :N], PSq[:, :N], AF.Ln, scale=inv, bias=EPS)
        self.act(A[3][:, :N], A[2][:, :N], AF.Exp, scale=-0.5)
        return A[3]

    def layer_norm(self, c, l, kg, kb):
        N = c.N
        A = self.A
        rstd = self.bstats(c, [self.Y[k][:, :N] for k in range(8)], True, float(D), self.P[2], self.P[3])
        for k in range(8):
            t = A[4 + k % 2][:, :N]
            self.tt("dve", t, self.Y[k][:, :N], A[0][:, :N], ALU.subtract)
            self.tt("dve", t, t, rstd[:, :N], ALU.mult)
            self.act(self.Y[k][:, :N], t, AF.Identity, scale=self.vec(l, kg, k), bias=self.vec(l, kb, k))
            self.act(self.YB[k][:, :N], t, AF.Identity, scale=self.vec(l, kg, k), bias=self.vec(l, kb, k))

    def residual(self, c, k, ps):
        N = c.N
        self.stt(self.Y[k][:, :N], self.Y[k][:, :N], ALPHA, ps, ALU.mult, ALU.add)

    def finish(self, c, l):
        N = c.N
        A = self.A
        self.layer_norm(c, l, 0, 1)
        pb = 0
        for jj in range(8):
            wu = self.wnext(("up", l, jj))
            for m in range(4):
                ff = jj * 4 + m
                ps = self.P[pb % 2]
                pb += 1
                for kc in range(8):
                    self.mm(ps[:, :N], wu[:, kc, m * 128:(m + 1) * 128], self.YB[kc][:, :N], start=(kc == 0), stop=(kc == 7))
                r = A[6 + ff % 2][:, :N]
                self.act(r, ps[:, :N], AF.Relu)
                self.tt("dve" if ff % 2 else "pool", self.H[ff][:, :N], r, r, ALU.mult)
        for m in range(8):
            wd = self.wnext(("down", l, m))
            ps = self.P[pb % 2]
            pb += 1
            for kc in range(32):
                self.mm(ps[:, :N], wd[:, kc, :], self.H[kc][:, :N], start=(kc == 0), stop=(kc == 31))
            self.residual(c, m, ps[:, :N])
        self.layer_norm(c, l, 2, 3)
        psrc = (self.psT if c.sample else self.pT)[l].rearrange("(c p) n -> p c n", p=128)
        psrc = psrc[:, :, 0:N] if c.sample else psrc[:, :, c.pos0:c.pos0 + N]
        PT = [A[18], A[19]]
        self.S.dma("sp", "ldp", self.Awide[:, 18:20, 0:N], psrc, reads=[], writes=[PT[0].b, PT[1].b])
        PB = [self.BX[8], self.BX[9]]
        for kc in range(2):
            self.cp("pool", PB[kc][:, :N], PT[kc][:, :N])
        wpj = self.wnext(("pproj", l))
        E = [A[8 + m] for m in range(8)]
        for m in range(8):
            ps = self.P[pb % 2]
            pb += 1
            for kc in range(2):
                self.mm(ps[:, :N], wpj[:, kc, m * 128:(m + 1) * 128], PB[kc][:, :N], start=(kc == 0), stop=(kc == 1))
            self.cp("act", E[m][:, :N], ps[:, :N])
        rstd = self.bstats(c, [E[m][:, :N] for m in range(8)], False, float(D), self.P[2], self.P[3])
        for jj in range(2):
            wg = self.wnext(("gate", l, jj))
            for m4 in range(4):
                m = jj * 4 + m4
                ps = self.P[pb % 2]
                pb += 1
                for kc in range(8):
                    self.mm(ps[:, :N], wg[:, kc, m4 * 128:(m4 + 1) * 128], self.YB[kc][:, :N], start=(kc == 0), stop=(kc == 7))
                gt = A[16 + m % 2][:, :N]
                self.act(gt, ps[:, :N], AF.Sigmoid)
                t = A[4 + m % 2][:, :N]
                self.tt("pool" if m % 2 else "dve", t, E[m][:, :N], rstd[:, :N], ALU.mult)
                self.tt("dve", t, t, gt, ALU.mult)
                self.stt(self.Y[m][:, :N], t, self.vec(l, 4, m), self.Y[m][:, :N], ALU.mult, ALU.add)
        for m in range(8):
            self.cp("act" if m % 2 else "dve", self.YB[m][:, :N], self.Y[m][:, :N])

    def store(self, dram_ap, src, chan="st"):
        b = Buf()
        self.obufs.append((chan, b))
        self.dma("pool", chan, (dram_ap, [b]), src)

    def store_y(self, c):
        N = c.N
        dst = (self.o_ysT if c.sample else self.o_yT).rearrange("(c p) n -> p c n", p=128)
        dst = dst[:, :, 0:N] if c.sample else dst[:, :, c.pos0:c.pos0 + N]
        b = Buf()
        self.S.dma("pool", "sty", dst, self._ytensor[:, :, 0:N], reads=[y.b for y in self.Y], writes=[b])

    def build(self, mixers=True):
        self.mixers = mixers
        self.setup()
        self.cast_weights()
        tiles = [self.prompt_ctx(t) for t in range(self.NT)]
        if self.sample:
            tiles.append(self.sample_ctx())
        plan = []
        for c in tiles:
            for l in range(self.NLAY):
                pl = self.plan_layer(l)
                if not mixers:
                    pl = [k for k in pl if k[0] in ("up", "down", "pproj", "gate")]
                plan += pl
        if self.sample and mixers:
            pre = []
            for j in range(2):
                if 2 * j + 1 < self.NLAY:
                    pre += [("mla_uk", j), ("mla_uv", j)]
            plan = pre + plan
        self.set_plan(plan)
        if self.sample and mixers:
            self.sample_cache_kv()
        for c in tiles:
            self.load_tile(c)
            for l in range(self.NLAY):
                if mixers:
                    if l % 2 == 0:
                        self.dn_layer(c, l)
                    else:
                        self.mla_layer(c, l)
                else:
                    for k in range(8):
                        self.ts("dve", self.Y[k][:, :c.N], self.Y[k][:, :c.N], ALPHA, ALU.mult)
                self.finish(c, l)
            self.store_y(c)
        for name, ch in self.S.chan.items():
            if name.startswith("st"):
                self.nc.gpsimd.wait_ge(ch.sem, ch.n)
        self.st.close()
        return self.nc


def _consts(SEQ):
    c = {}
    c["ident"] = np.eye(128, dtype=np.float32)
    c["oh8"] = np.eye(8, dtype=np.float32)
    j = np.arange(128)[:, None]
    i = np.arange(128)[None, :]
    p_incl = np.where(i >= j, 0.0, NEG)
    p_strict = np.where(i > j, 0.0, NEG)
    same = (i // 32) == (j // 32)
    jreal = (j % 32) >= 16
    s_incl = np.where(same & jreal & (i >= j), 0.0, NEG)
    s_strict = np.where(same & jreal & (i > j), 0.0, NEG)
    c["maskbias"] = np.concatenate([p_incl, p_strict, s_incl, s_strict], 1).astype(np.float32)
    tt = np.arange(TN)
    reset_p = np.where(tt % 128 == 0, 0.0, 1.0)
    reset_s = np.where(tt % 32 == 0, 0.0, 1.0)[:SN]
    real_s = np.where((tt % 32) >= 16, 1.0, 0.0)[:SN]
    gm = np.concatenate([reset_p, reset_s, real_s], 0).reshape(1, TN + 2 * SN)
    c["gatemask"] = np.repeat(gm, 8, 0).astype(np.float32)
    tok = np.arange(128)
    tokm = np.zeros((128, 8), np.float32)
    tokm[:, 0] = (tok % 32) >= 16
    for s in range(4):
        tokm[:, 1 + s] = ((tok // 32) == s) & ((tok % 32) >= 16)
    c["tokmask"] = tokm
    am = np.zeros((128, 4 * TN + 18 * 128), np.float32)
    q = np.arange(TN)[None, :]
    for kb in range(4):
        kp = kb * 128 + np.arange(128)[:, None]
        am[:, kb * TN:(kb + 1) * TN] = (kp // 64) <= (q // 64)
    qs = np.arange(128)[None, :]
    for s in range(4):
        am[:, 4 * TN + s * 128:4 * TN + (s + 1) * 128] = np.broadcast_to((qs // 32) == s, (128, 128))
    ks = np.arange(128)[:, None]
    am[:, 4 * TN + 4 * 128:4 * TN + 5 * 128] = ((ks // 32) == (qs // 32)) & ((ks % 32) >= 16)
    o0 = 4 * TN + 5 * 128
    ri = np.arange(128)[:, None]
    ci = np.arange(128)[None, :]
    am[:, o0:o0 + 128] = (ri // 2) == (ci // 2)
    for lev, s_ in enumerate((2, 4, 8, 16, 32, 64)):
        ML = ((ri // (2 * s_)) == (ci // (2 * s_))) & ((ri % (2 * s_)) >= s_) & ((ci % (2 * s_)) < s_)
        am[:, o0 + (1 + 2 * lev) * 128:o0 + (2 + 2 * lev) * 128] = ML
        am[:, o0 + (2 + 2 * lev) * 128:o0 + (3 + 2 * lev) * 128] = ML.T
    c["attnmask"] = am
    prot = np.zeros((64, 64), np.float32)
    for m in range(32):
        prot[m + 32, m] = -1.0
        prot[m, m + 32] = 1.0
    c["prot"] = prot
    half = 32
    inv_freq = (10000.0 ** (-np.arange(half, dtype=np.float32) / half)).astype(np.float32)
    pos = np.concatenate([np.arange(SEQ), np.tile(np.concatenate([np.zeros(16), PAST + np.arange(16)]), 4)]).astype(np.float32)
    ang = pos[None, :] * inv_freq[:, None]
    c["ropeC"] = np.concatenate([np.cos(ang), np.cos(ang)], 0).astype(np.float32)
    c["ropeS"] = np.concatenate([np.sin(ang), np.sin(ang)], 0).astype(np.float32)
    return c


def _fm(v, nchunk):
    return np.ascontiguousarray(np.moveaxis(v.reshape(v.shape[:-1] + (nchunk, 128)), -1, 0))


def host_inputs(inp, NT, core_seq, core_samp):
    SEQ = NT * TN
    f = lambda a: np.ascontiguousarray(np.asarray(a, dtype=np.float32))
    cst = _consts(SEQ)
    shared = dict(cst)
    for k in ("mlp_w_up", "mlp_w_down", "ple_w_proj", "ple_w_gate", "dn_w_in", "dn_w_o", "mla_w_in", "mla_w_uq", "mla_w_o"):
        shared[k] = f(inp[k])
    shared["mla_w_uk"] = f(inp["mla_w_uk"]).reshape(2, 256, 1024)
    shared["mla_w_uv"] = f(inp["mla_w_uv"]).reshape(2, 256, 1024)
    vec = np.stack([f(inp[k]) for k in ("ln1_g", "ln1_b", "ln2_g", "ln2_b", "ple_norm")], 1)
    shared["vec1024"] = _fm(vec, 8).reshape(128, NL * 5 * 8)
    shared["convw"] = np.ascontiguousarray(np.transpose(f(inp["dn_conv_w"]).reshape(2, 4, 24, 128), (3, 0, 2, 1))).reshape(128, 2 * 24 * 4)
    shared["qnorm"] = _fm(f(inp["mla_q_norm"]), 4).reshape(128, 8)
    shared["kvnorm"] = _fm(f(inp["mla_kv_norm"]), 2).reshape(128, 4)
    shared["onorm"] = np.ascontiguousarray(f(inp["dn_o_norm"]).T)
    shared["alog"] = np.ascontiguousarray(f(inp["dn_a_log"]).T)
    shared["dtb"] = np.ascontiguousarray(f(inp["dn_dt_bias"]).T)
    maps = []
    xp = f(inp["x_prompt"])
    pp = f(inp["p_prompt"])
    xs = f(inp["x_sample"])
    ps = f(inp["p_sample"])
    for c in range(len(core_seq)):
        m = dict(shared)
        b = core_seq[c]
        if b is None:
            m["xT"] = np.zeros((D, SEQ), np.float32)
            m["pT"] = np.zeros((NL, 256, SEQ), np.float32)
        else:
            m["xT"] = np.ascontiguousarray(xp[b, :SEQ].T)
            m["pT"] = np.ascontiguousarray(np.transpose(pp[:, b, :SEQ], (0, 2, 1)))
        ss = core_samp[c]
        xsT = np.zeros((D, 4, 32), np.float32)
        psT = np.zeros((NL, 256, 4, 32), np.float32)
        for i, s in enumerate(ss):
            xsT[:, i, 16:] = xs[s].T
            psT[:, :, i, 16:] = np.transpose(ps[:, s], (0, 2, 1))
        m["xsT"] = xsT.reshape(D, SN)
        m["psT"] = psT.reshape(NL, 256, SN)
        m["sconvT"] = np.ascontiguousarray(np.transpose(f(inp["state_dn_conv"])[:, ss], (0, 1, 3, 2)))
        m["srec"] = np.ascontiguousarray(f(inp["state_dn_recurrent"])[:, ss])
        m["ckvT"] = np.ascontiguousarray(np.transpose(f(inp["cache_mla_ckv"])[:, ss], (0, 1, 3, 2)))
        m["krT"] = np.ascontiguousarray(np.transpose(f(inp["cache_mla_krope"])[:, ss], (0, 1, 3, 2)))
        maps.append(m)
    return maps


def dn_layer(self, c, l):
    j = l // 2
    N = c.N
    A = self.A
    H = self.H
    nblk = c.nblk
    U = c.U
    nun = N // U
    samp = c.sample
    mbI = self.MB[:, (2 if samp else 0) * 128:(3 if samp else 1) * 128]
    mbS = self.MB[:, (3 if samp else 1) * 128:(4 if samp else 2) * 128]
    reset = self.GM[:, TN:TN + SN] if samp else self.GM[:, 0:TN]
    realT = self.GM[:, TN + SN:TN + 2 * SN]
    wba = self.wnext(("dn_in", j, "ba"))
    pb_, pa_ = self.P[2], self.P[3]
    for kc in range(8):
        self.mm(pb_[0:8, :N], wba[:, kc, 0:8], self.YB[kc][:, :N], start=(kc == 0), stop=(kc == 7))
    for kc in range(8):
        self.mm(pa_[0:8, :N], wba[:, kc, 8:16], self.YB[kc][:, :N], start=(kc == 0), stop=(kc == 7))
    BETA, LNB, G, GC, GCB, EGC, DKD, NGC, NS1, TMP = [A[k][0:8, :N] for k in range(10)]
    self.act(BETA, pb_[0:8, :N], AF.Sigmoid)
    self.act(LNB, BETA, AF.Ln)
    self.act(G, pa_[0:8, :N], AF.Exp, bias=self.DTB[:, j:j + 1])
    self.act(G, G, AF.Ln, bias=1.0)
    self.ts("dve", G, G, self.NEGA[:, j:j + 1], ALU.mult)
    if samp:
        self.tt("dve", G, G, realT, ALU.mult)
    r, w = _rw([GC], [reset, G])
    self.S.op("dve", lambda v: v.tensor_tensor_scan(GC.ap, reset.ap, G.ap, 0.0, ALU.mult, ALU.add), r, w)
    self.tt("dve", GCB, GC, LNB, ALU.add)
    self.act(EGC, GC, AF.Exp)
    gc3 = GC.re("p (u k) -> p u k", k=U)
    gl = gc3[:, :, U - 1:U]
    self.tt("dve", TMP.re("p (u k) -> p u k", k=U), gl.bc([8, nun, U]), gc3, ALU.subtract)
    self.act(DKD, TMP, AF.Exp)
    if samp:
        self.tt("dve", DKD, DKD, realT, ALU.mult)
    EGL = self.EGLT[:, 0:nun]
    self.act(EGL.re("p (u k) -> p u k", k=1), gl, AF.Exp)
    self.ts("dve", NGC, GC, -1.0, ALU.mult)
    self.act(NS1, GCB, AF.Exp)
    self.ts("dve", NS1, NS1, -1.0, ALU.mult)
    for b in range(nblk):
        cb = slice(b * 128, (b + 1) * 128)
        ps = self.P[6 + b % 2]
        for q, src in enumerate((NGC, NS1, BETA, DKD)):
            self.tr(ps[:, q * 8:(q + 1) * 8], src[:, cb], self.IDENT[0:8, 0:8])
        self.cp("dve", self.TS[b][:, :], ps[:, 0:32])
    if samp:
        for s in range(4):
            self.ts("dve", self.DKS[:, s * 8:(s + 1) * 8], self.TS[0][:, 24:32], self.TOKM[:, 1 + s:2 + s], ALU.mult)
    if samp:
        segs = [(s * 32, 32, s) for s in range(4)]
    else:
        segs = [(0, 128, 0)]
    sets = [dict(VSb=H[1], QHb=H[2], KHb=H[3], QGb=H[4], ZSb=H[5], DI=A[15], DS=A[16], EGLB=self.EGLB2[0]),
            dict(VSb=H[18], QHb=H[19], KHb=H[20], QGb=H[21], ZSb=H[22], DI=A[0], DS=A[1], EGLB=self.EGLB2[1])]
    blks = range(nblk)
    cbs = [slice(b * 128, (b + 1) * 128) for b in blks]
    MF = self.MF
    o0 = 4 * TN + 5 * 128
    msk = lambda k: self.AM[:, o0 + k * 128:o0 + (k + 1) * 128]
    PO = self.P[5]
    st = {"pbi": 0}

    def prep(h):
        S_ = sets[h % 2]
        wp = self.wnext(("dn_in", j, h))
        QKV = [A[10], A[11], A[12]]
        VSb = S_["VSb"][:, :N]
        for g in range(4):
            ps = self.P[st["pbi"] % 2]
            st["pbi"] += 1
            for kc in range(8):
                self.mm(ps[:, :N], wp[:, kc, g, :], self.YB[kc][:, :N], start=(kc == 0), stop=(kc == 7))
            if g == 3:
                sg = A[17][:, :N]
                self.act(sg, ps[:, :N], AF.Exp, scale=-1.0)
                self.act(sg, sg, AF.Ln, bias=1.0)
                self.act(sg, sg, AF.Exp, scale=-1.0)
                self.tt("dve", S_["ZSb"][:, :N], ps[:, :N], sg, ALU.mult)
                yield
                continue
            xs = self.XS[g]
            self.cp("act", xs[:, 3:3 + N], ps[:, :N])
            ch = g * 8 + h
            if samp:
                for s in range(4):
                    self.dma("sp", "ldcs", xs[:, 3 + 32 * s + 13:3 + 32 * s + 16],
                             (self.sconvT[j, s, ch * 128:(ch + 1) * 128, :], []))
                    self.store(self.o_sconv[j, s, ch * 128:(ch + 1) * 128, :], xs[:, 3 + 32 * s + 29:3 + 32 * s + 32], "stc")
            else:
                self.cp("dve", xs[:, 0:3], self.HALO[j][ch][:, :])
                self.cp("dve", self.HALO[j][ch][:, :], xs[:, N:N + 3])
                if c.last:
                    self.store(self.o_pconv[j, ch * 128:(ch + 1) * 128, :], self.HALO[j][ch][:, :], "stc")
            yield
            cw = lambda i: self.CONVW[:, ((j * 24 + ch) * 4 + i):((j * 24 + ch) * 4 + i + 1)]
            acc = QKV[g][:, :N]
            self.ts("dve", acc, xs[:, 0:N], cw(0), ALU.mult)
            self.stt(acc, xs[:, 1:1 + N], cw(1), acc, ALU.mult, ALU.add)
            yield
            self.stt(acc, xs[:, 2:2 + N], cw(2), acc, ALU.mult, ALU.add)
            self.stt(acc, xs[:, 3:3 + N], cw(3), acc, ALU.mult, ALU.add)
            sg = A[17][:, :N]
            self.act(sg, acc, AF.Exp, scale=-1.0)
            self.act(sg, sg, AF.Ln, bias=1.0)
            self.act(sg, sg, AF.Exp, scale=-1.0)
            yield
            if g == 2:
                self.tt("dve", VSb, acc, sg, ALU.mult)
            else:
                self.tt("dve", acc, acc, sg, ALU.mult)
            yield
        QS, KS = QKV[0][:, :N], QKV[1][:, :N]
        QHb = S_["QHb"][:, :N]
        KHb = S_["KHb"][:, :N]
        nrm = ((QS, QHb, 128.0 ** -0.5, A[13], H[0]), (KS, KHb, 1.0, A[14], H[23]))
        pss = []
        for src, dst, sc, rt, sqt in nrm:
            self.act(sqt[:, :N], src, AF.Square)
        yield
        for src, dst, sc, rt, sqt in nrm:
            ps = self.P[st["pbi"] % 2]
            st["pbi"] += 1
            pss.append(ps)
            self.mm(ps[:, :N], self.ONESB[:], sqt[:, :N])
        yield
        for (src, dst, sc, rt, sqt), ps in zip(nrm, pss):
            self.act(rt[:, :N], ps[:, :N], AF.Ln, bias=EPS)
            self.act(rt[:, :N], rt[:, :N], AF.Exp, scale=-0.5)
        yield
        for src, dst, sc, rt, sqt in nrm:
            self.stt(dst, src, sc, rt[:, :N], ALU.mult, ALU.mult)
        oh = self.OH8[:, h:h + 1]
        DI = S_["DI"][:, :N]
        DS = S_["DS"][:, :N]
        QGb = S_["QGb"][:, :N]
        PB = self.P[2]
        tmf = [A[2][:, :N], A[6][:, :N], A[7][:, :N], A[8][:, :N]]
        tm = [t[0:8, :] for t in tmf]
        self.ts("dve", tm[0], GC, oh, ALU.mult)
        self.ts("dve", tm[1], GCB, oh, ALU.mult)
        self.ts("dve", tm[2], EGC, oh, ALU.mult)
        self.ts("dve", tm[3][:, 0:nun], EGL, oh, ALU.mult)
        yield
        self.mm(PB[:, :N], self.ONES8[:, :], tmf[0])
        yield
        for b in range(nblk):
            self.tt("dve", DI[:, cbs[b]], PB[:, cbs[b]], mbI, ALU.add)
        self.mm(PB[:, :N], self.ONES8[:, :], tmf[1])
        yield
        for b in range(nblk):
            self.tt("dve", DS[:, cbs[b]], PB[:, cbs[b]], mbS, ALU.add)
        self.mm(PB[:, :N], self.ONES8[:, :], tmf[2])
        yield
        self.tt("dve", QGb, QHb, PB[:, :N], ALU.mult)
        self.mm(PB[:, 0:nun], self.ONES8[:, :], tmf[3][:, 0:nun])
        yield
        self.cp("dve", S_["EGLB"][:, 0:nun], PB[:, 0:nun])
        yield

    def solve(h):
        S_ = sets[h % 2]
        VSb = S_["VSb"][:, :N]
        QHb = S_["QHb"][:, :N]
        KHb = S_["KHb"][:, :N]
        QGb = S_["QGb"][:, :N]
        DI = S_["DI"][:, :N]
        DS = S_["DS"][:, :N]
        EGLB = S_["EGLB"]
        attnT, TT, T_, vb, kds, n0, n0t = ({} for _ in range(7))
        DTs, DTi, r0s, r1s, r2s, r3s, r4s = ({} for _ in range(7))
        for b in blks:
            ngc = self.TS[b][:, h:h + 1]
            DTs[b] = self.newM()
            DTi[b] = self.newM()
            self.act(DTs[b][:, :], DS[:, cbs[b]], AF.Exp, bias=ngc)
            self.act(DTi[b][:, :], DI[:, cbs[b]], AF.Exp, bias=ngc)
            r0s[b] = self.R()
            self.mm(r0s[b], KHb[:, cbs[b]], KHb[:, cbs[b]])
        yield
        for b in blks:
            r2s[b] = self.R()
            self.mm(r2s[b], KHb[:, cbs[b]], QHb[:, cbs[b]])
            n0t[b] = MF[b * 6 + 5]
            self.stt(n0t[b][:, :], r0s[b], -1.0, DTs[b][:, :], ALU.mult, ALU.mult)
        yield
        for b in blks:
            r3s[b] = self.Rb()
            self.tr(r3s[b], VSb[:, cbs[b]], self.IDENTB[:])
            attnT[b] = MF[b * 6 + 0]
            self.tt("dve", attnT[b][:, :], r2s[b], DTi[b][:, :], ALU.mult)
            TT[b] = self.newM()
            self.tt("pool", TT[b][:, :], n0t[b][:, :], msk(0), ALU.mult)
            self.tt("pool", TT[b][:, :], TT[b][:, :], self.IDENTB[:], ALU.add)
        yield
        for b in blks:
            r1s[b] = self.Rb()
            self.tr(r1s[b], n0t[b][:, :], self.IDENTB[:])
            vb[b] = MF[b * 6 + 1]
            self.ts("dve", vb[b][:, :], r3s[b], self.TS[b][:, 16 + h:17 + h], ALU.mult)
        yield
        for b in blks:
            r4s[b] = self.Rb()
            self.tr(r4s[b], KHb[:, cbs[b]], self.IDENTB[:])
            n0[b] = MF[b * 6 + 4]
            T_[b] = self.newM()
            self.tt("dve", T_[b][:, :], r1s[b], msk(0), ALU.mult)
            self.cp("act", n0[b][:, :], r1s[b])
            self.tt("pool", T_[b][:, :], T_[b][:, :], self.IDENTB[:], ALU.add)
        yield
        for b in blks:
            kds[b] = []
            for si, (p0, pl_, s) in enumerate(segs):
                kd = MF[b * 6 + 2] if si == 0 else MF[23 + si]
                if samp:
                    self.ts("dve", kd[:, :], r4s[b], self.DKS[:, s * 8 + h:s * 8 + h + 1], ALU.mult)
                else:
                    self.ts("dve", kd[:, :], r4s[b], self.TS[b][:, 24 + h:25 + h], ALU.mult)
                kds[b].append(kd)
        yield
        for lev in range(c.L):
            last = (lev == c.L - 1)
            X = {}
            for b in blks:
                rx = self.R()
                self.mm(rx, n0t[b][:, :], T_[b][:, :])
                X[b] = self.newM()
                self.tt("dve", X[b][:, :], rx, msk(1 + 2 * lev), ALU.mult)
            yield
            for b in blks:
                rp = self.R()
                self.mm(rp, X[b][:, :], TT[b][:, :])
                TT2 = MF[b * 6 + 3] if last else self.newM()
                self.tt("dve", TT2[:, :], rp, TT[b][:, :], ALU.add)
                if not last:
                    rq = self.R()
                    self.mm(rq, TT[b][:, :], X[b][:, :])
                    T2 = self.newM()
                    self.tt("dve", T2[:, :], rq, T_[b][:, :], ALU.add)
                    T_[b] = T2
                TT[b] = TT2
                if b % 2 == 1:
                    yield
        if not samp:
            Sbh = self.SSB[h % 2]
            self.cp("act", Sbh[:, :], self.SST[j][h][:, :])
        for b in blks:
            cb = cbs[b]
            ns1 = self.TS[b][:, 8 + h:9 + h]
            Ss = []
            r5 = self.R()
            for si, (p0, pl_, s) in enumerate(segs):
                if samp:
                    Sf = self.S0[s]
                    self.dma("sp", "lds0", Sf[:, :], (self.srec[j, s, h], []))
                    Sb_ = self.SB[s]
                    self.cp("act", Sb_[:, :], Sf[:, :])
                else:
                    Sf = self.SST[j][h]
                    Sb_ = Sbh
                Ss.append((Sf, Sb_))
                kw = {} if pl_ == 128 else {"tile_position": (0, p0)}
                self.mm(r5[p0:p0 + pl_, :], KHb[:, b * 128 + p0:b * 128 + p0 + pl_], Sb_[:, :], **kw)
            rr_ = MF[27 + b % 2]
            self.stt(rr_[:, :], r5, ns1, vb[b][:, :], ALU.mult, ALU.add)
            yield
            r6 = self.R()
            self.mm(r6, TT[b][:, :], rr_[:, :])
            vn = MF[29 + b % 2]
            self.cp("act", vn[:, :], r6)
            yield
            for si, (p0, pl_, s) in enumerate(segs):
                cs = slice(b * 128 + p0, b * 128 + p0 + pl_)
                self.mm(PO[:, cs], Ss[si][1][:, :], QGb[:, cs], start=(si == 0), stop=False)
            self.mm(PO[:, cb], vn[:, :], attnT[b][:, :], start=False, stop=True)
            for si, (p0, pl_, s) in enumerate(segs):
                Sf, Sb_ = Ss[si]
                r7 = self.R()
                self.mm(r7, kds[b][si][:, :], vn[:, :])
                u = b * (128 // U) + (si if samp else 0)
                if samp:
                    So = self.S1[self.s0i % 2]
                    self.s0i += 1
                    self.stt(So[:, :], Sf[:, :], EGLB[:, u:u + 1], r7, ALU.mult, ALU.add)
                    self.store(self.o_srec[j, s, h], So[:, :], "sts")
                else:
                    self.stt(Sf[:, :], Sf[:, :], EGLB[:, u:u + 1], r7, ALU.mult, ALU.add)
                    self.cp("act", Sb_[:, :], Sf[:, :])
            yield
        sq = H[24][:, :N]
        self.act(sq, PO[:, :N], AF.Square)
        self.mm(self.P[2][:, :N], self.ONESB[:], sq)
        rn = A[18][:, :N]
        self.act(rn, self.P[2][:, :N], AF.Ln, scale=1.0 / 128.0, bias=EPS)
        self.act(rn, rn, AF.Exp, scale=-0.5)
        self.stt(rn, PO[:, :N], self.ON[:, j:j + 1], rn, ALU.mult, ALU.mult)
        self.tt("dve", H[8 + h][:, :N], rn, S_["ZSb"][:, :N], ALU.mult)
        if (not samp) and c.last:
            self.store(self.o_prec[j, h], self.SST[j][h][:, :], "sts")
        yield

    def drive(gens):
        gens = [g for g in gens if g is not None]
        while gens:
            for g in list(gens):
                try:
                    next(g)
                except StopIteration:
                    gens.remove(g)

    drive([prep(0)])
    for h in range(8):
        drive([solve(h), prep(h + 1) if h < 7 else None])
    self.out_proj(c, ("dn_o", j))


def out_proj(self, c, key):
    N = c.N
    pb = 0
    for half in range(2):
        wo = self.wnext(key + (half,))
        for m4 in range(4):
            m = half * 4 + m4
            ps = self.P[pb % 2]
            pb += 1
            for hh in range(8):
                self.mm(ps[:, :N], wo[:, hh, m4 * 128:(m4 + 1) * 128], self.H[8 + hh][:, :N], start=(hh == 0), stop=(hh == 7))
            self.residual(c, m, ps[:, :N])


MK.dn_layer = dn_layer
MK.out_proj = out_proj


def rope(self, c, src, dst, tmp):
    N = c.N
    ps = self.P[6]
    self.mm(ps[0:64, :N], self.PROT[:, :], src)
    self.tt("dve", tmp, src, self.COS[:, :N], ALU.mult)
    self.tt("dve", dst, ps[0:64, :N], self.SIN[:, :N], ALU.mult)
    self.tt("pool", dst, dst, tmp, ALU.add)


def make_k(self, wuk, CKB, N):
    for h in range(8):
        ps = self.P[h % 2]
        for kc in range(2):
            self.mm(ps[:, :N], wuk[:, kc, h * 128:(h + 1) * 128], CKB[kc][:, :N], start=(kc == 0), stop=(kc == 1))
        self.evac(self.H[19 + h][:, :N], ps[:, :N])


def make_v(self, wuv, CKB, nblk):
    for kb in range(nblk):
        for half in range(2):
            ps = self.P[half]
            for kc in range(2):
                self.mm(ps[:, :], CKB[kc][:, kb * 128:(kb + 1) * 128], wuv[:, kc, half * 512:(half + 1) * 512], start=(kc == 0), stop=(kc == 1))
            self.evac(self.VT[kb][:, half * 512:(half + 1) * 512], ps[:, :])


def spill_k(self, j, key0, N):
    b = Buf()
    dst = self.Kscr[j].rearrange("h p n -> p h n")[:, :, key0:key0 + N]
    self.S.dma("pool", "spk", dst, self.Ht[:, 19:27, 0:N], reads=[self.H[19 + h].b for h in range(8)], writes=[b])
    return b


def spill_v(self, j, key0, nblk):
    b = Buf()
    dst = self.Vscr[j][:, key0 // 128:key0 // 128 + nblk, :]
    src = self.BXt[:, 0:2 * nblk, :].rearrange("p (b a) n -> p b (a n)", a=2)
    self.S.dma("pool", "spv", dst, src, reads=[self.VT[kb].b for kb in range(nblk)], writes=[b])
    return b


def sample_cache_kv(self):
    A = self.A
    H = self.H
    for j in range(2):
        if 2 * j + 1 >= self.NLAY:
            continue
        bl = []
        grp = [(s, g) for s in range(4) for g in range(2)]
        for gi, (s, g) in enumerate(grp):
            src = self.ckvT[j, s].rearrange("(c p) n -> p c n", p=128)[:, :, g * 512:(g + 1) * 512]
            self.S.dma("sp", "ldck", self.Awide[:, 12:14, :], src, reads=[], writes=[A[12].b, A[13].b])
            for kc in range(2):
                self.cp("pool", H[2 * gi + kc][:, :], A[12 + kc][:, :])
        for s in range(4):
            b = Buf()
            self.S.dma("pool", "spr", self.KRscr[j][:, self.SEQ + s * PAST:self.SEQ + (s + 1) * PAST], self.krT[j, s], reads=[], writes=[b])
            bl.append(b)
        wuk = self.wnext(("mla_uk", j))
        for gi, (s, g) in enumerate(grp):
            self.make_k(wuk, [H[2 * gi], H[2 * gi + 1]], TN)
            bl.append(self.spill_k(j, self.SEQ + s * PAST + g * 512, TN))
        wuv = self.wnext(("mla_uv", j))
        for gi, (s, g) in enumerate(grp):
            self.make_v(wuv, [H[2 * gi], H[2 * gi + 1]], 4)
            bl.append(self.spill_v(j, self.SEQ + s * PAST + g * 512, 4))
        self.kvbuf[(j, "cache")] = bl


def mla_layer(self, c, l):
    j = l // 2
    N = c.N
    A = self.A
    H = self.H
    samp = c.sample
    nblk = c.nblk
    pb = 0
    wi0 = self.wnext(("mla_in", j, 0))
    CQ = [A[8 + m] for m in range(4)]
    for m in range(4):
        ps = self.P[pb % 2]
        pb += 1
        for kc in range(8):
            self.mm(ps[:, :N], wi0[:, kc, m * 128:(m + 1) * 128], self.YB[kc][:, :N], start=(kc == 0), stop=(kc == 7))
        self.evac(CQ[m][:, :N], ps[:, :N])
    rstd = self.bstats(c, [CQ[m][:, :N] for m in range(4)], False, 512.0, self.P[2], self.P[3])
    for m in range(4):
        self.stt(H[4 + m][:, :N], CQ[m][:, :N], self.QN[:, j * 4 + m:j * 4 + m + 1], rstd[:, :N], ALU.mult, ALU.mult)
    wi1 = self.wnext(("mla_in", j, 1))
    CKV = [A[12], A[13]]
    CKVN = [A[14], A[15]]
    for m in range(2):
        ps = self.P[pb % 2]
        pb += 1
        for kc in range(8):
            self.mm(ps[:, :N], wi1[:, kc, m * 128:(m + 1) * 128], self.YB[kc][:, :N], start=(kc == 0), stop=(kc == 7))
        self.evac(CKV[m][:, :N], ps[:, :N])
    ps = self.P[pb % 2]
    pb += 1
    for kc in range(8):
        self.mm(ps[0:64, :N], wi1[:, kc, 256:320], self.YB[kc][:, :N], start=(kc == 0), stop=(kc == 7))
    KR = A[16][0:64, :N]
    self.evac(KR, ps[0:64, :N])
    rstd = self.bstats(c, [CKV[m][:, :N] for m in range(2)], False, 256.0, self.P[2], self.P[3])
    CKB = [H[16], H[17]]
    for m in range(2):
        self.stt(CKVN[m][:, :N], CKV[m][:, :N], self.KVN[:, j * 2 + m:j * 2 + m + 1], rstd[:, :N], ALU.mult, ALU.mult)
        if samp:
            self.store(self.o_sckv[j, m * 128:(m + 1) * 128, 0:N], CKVN[m][:, :N], "stk")
        else:
            self.store(self.o_pckv[j, m * 128:(m + 1) * 128, c.pos0:c.pos0 + N], CKVN[m][:, :N], "stk")
        self.cp("pool", CKB[m][:, :N], CKVN[m][:, :N])
    KRR = A[17][0:64, :N]
    self.rope(c, KR, KRR, A[18][0:64, :N])
    if samp:
        self.store(self.o_skr[j, :, 0:N], KRR, "stk")
    else:
        self.store(self.o_pkr[j, :, c.pos0:c.pos0 + N], KRR, "stk")
    KRB = H[18][0:64, :N]
    self.memset("pool", H[18][64:128, :N], 0.0)
    self.memset("pool", H[28][64:128, :N], 0.0)
    self.cp("pool", KRB, KRR)
    wuk = self.wnext(("mla_uk", j))
    self.make_k(wuk, CKB, N)
    wuv = self.wnext(("mla_uv", j))
    self.make_v(wuv, CKB, nblk)
    if not samp:
        bl = [self.spill_k(j, c.pos0, N), self.spill_v(j, c.pos0, nblk)]
        b = Buf()
        self.dma("pool", "spr", (self.KRscr[j][:, c.pos0:c.pos0 + N], [b]), KRB)
        bl.append(b)
        self.kvbuf[(j, c.t)] = bl
    srcs = []
    if samp:
        for s in range(4):
            msk = self.AM[:, 4 * TN + s * 128:4 * TN + (s + 1) * 128]
            srcs.append((self.SEQ + s * PAST, PAST, self.kvbuf[(j, "cache")], msk))
    else:
        t0 = 0
        while t0 < c.t:
            nt = min(2, c.t - t0)
            bl = []
            for tt_ in range(t0, t0 + nt):
                bl += self.kvbuf[(j, tt_)]
            srcs.append((t0 * TN, nt * TN, bl, None))
            t0 += nt
    PO = self.P[5]
    PD = self.P[4]
    for h in range(8):
        wq = self.wnext(("mla_uq", j, h))
        ps = self.P[pb % 2]
        pb += 1
        for kc in range(4):
            self.mm(ps[:, :N], wq[:, kc, 0:128], H[4 + kc][:, :N], start=(kc == 0), stop=(kc == 3))
        QT = H[27][:, :N]
        self.evac(QT, ps[:, :N])
        ps = self.P[pb % 2]
        pb += 1
        for kc in range(4):
            self.mm(ps[0:64, :N], wq[:, kc, 128:192], H[4 + kc][:, :N], start=(kc == 0), stop=(kc == 3))
        QR = A[10][0:64, :N]
        self.evac(QR, ps[0:64, :N])
        QRR = A[11][0:64, :N]
        self.rope(c, QR, QRR, A[18][0:64, :N])
        QRB = H[28][0:64, :N]
        self.cp("pool", QRB, QRR)
        blist = []
        for (key0, nk, bl, msk) in srcs:
            for kb in range(nk // 128):
                blist.append(("hist", key0, nk, bl, msk, kb))
        for kb in range(nblk):
            blist.append(("own", kb))
        nb_tot = len(blist)
        DEN = A[9][:, :N]
        cur = {}

        def emit_st(i):
            d = blist[i]
            if d[0] == "hist":
                _, key0, nk, bl, msk, kb = d
                if kb == 0:
                    sl = self.hslot % 2
                    self.hslot += 1
                    nb = nk // 128
                    self.dma("sp", f"hk{sl}", self.HK[sl][:, 0:nk], (self.Kscr[j, h][:, key0:key0 + nk], bl))
                    self.dma("sp", f"hv{sl}", self.HV[sl][:, 0:nb, :], (self.Vscr[j][:, key0 // 128:key0 // 128 + nb, h * 128:(h + 1) * 128], bl))
                    self.dma("sp", f"hr{sl}", self.HKR[sl][0:64, 0:nk], (self.KRscr[j][:, key0:key0 + nk], bl))
                    cur["sl"] = sl
                sl = cur["sl"]
                Kv = self.HK[sl][:, kb * 128:(kb + 1) * 128]
                KRv = self.HKR[sl][:, kb * 128:(kb + 1) * 128]
                Vv = self.HV[sl][:, kb, :]
                c0 = 0
            else:
                kb = d[1]
                Kv = H[19 + h][:, kb * 128:(kb + 1) * 128]
                KRv = H[18][:, kb * 128:(kb + 1) * 128]
                Vv = self.VT[kb][:, h * 128:(h + 1) * 128]
                if samp:
                    msk = self.AM[:, 4 * TN + 4 * 128:4 * TN + 5 * 128]
                    c0 = 0
                else:
                    c0 = kb * 128
                    msk = self.AM[:, kb * TN + c0:(kb + 1) * TN]
            ps_ = self.P[(2, 3, 6, 7)[i % 4]]
            self.mm(ps_[:, c0:N], Kv, QT[:, c0:N], start=True, stop=False)
            self.mm(ps_[:, c0:N], KRv, H[28][:, c0:N], start=False, stop=True)
            PT = (H[29], H[30], H[31], H[0], H[1], H[2])[i % 6][:, c0:N]
            self.act(PT, ps_[:, c0:N], AF.Exp, scale=MLA_SCALE)
            if msk is not None:
                self.tt("dve", PT, PT, msk, ALU.mult)
            return (PT, Vv, c0)

        GB = 2
        groups = [list(range(g0, min(g0 + GB, nb_tot))) for g0 in range(0, nb_tot, GB)]
        pend = [emit_st(i) for i in groups[0]]
        for gi, grp in enumerate(groups):
            cur_items = pend
            if gi + 1 < len(groups):
                pend = [emit_st(i) for i in groups[gi + 1]]
            for i, (PT, Vv, c0) in zip(grp, cur_items):
                self.mm(PO[:, c0:N], Vv, PT, start=(i == 0), stop=(i == nb_tot - 1))
            for i, (PT, Vv, c0) in zip(grp, cur_items):
                if i == 0:
                    self.cp("dve", DEN[:, c0:N], PT)
                else:
                    self.tt("dve", DEN[:, c0:N], DEN[:, c0:N], PT, ALU.add)
        self.mm(PD[:, :N], self.ONESF[:], DEN)
        rden = A[8][:, :N]
        self.recip(rden, PD[:, :N])
        self.tt("dve", H[8 + h][:, :N], PO[:, :N], rden, ALU.mult)
    self.out_proj(c, ("mla_o", j))


MK.mla_layer = mla_layer
MK.sample_cache_kv = sample_cache_kv
MK.rope = rope
MK.make_k = make_k
MK.make_v = make_v
MK.spill_k = spill_k
MK.spill_v = spill_v


PROMPT_CORES = [0, 1, 4, 5]
_NC_CACHE = {}


def kernel(**inputs):
    NT = SEQ_FULL // TN
    if "nc" not in _NC_CACHE:
        mk = MK(NT=NT, NLAY=4, sample=True)
        _NC_CACHE["nc"] = mk.build(mixers=True)
    nc = _NC_CACHE["nc"]
    core_seq = [None] * 8
    for b, cidx in enumerate(PROMPT_CORES):
        core_seq[cidx] = b
    core_samp = [[4 * c + i for i in range(4)] for c in range(8)]
    maps = host_inputs(inputs, NT, core_seq, core_samp)
    res = run_bass_kernel_spmd(nc, maps, core_ids=list(range(8)))
    R = res.results
    f32 = np.float32
    y_prompt = np.empty((4, SEQ_FULL, D), f32)
    y_sample = np.empty((32, 16, D), f32)
    p_conv = np.empty((2, 4, 3, 3072), f32)
    p_rec = np.empty((2, 4, 8, 128, 128), f32)
    p_ckv = np.empty((2, 4, SEQ_FULL, 256), f32)
    p_kr = np.empty((2, 4, SEQ_FULL, 64), f32)
    s_conv = np.empty((2, 32, 3, 3072), f32)
    s_rec = np.empty((2, 32, 8, 128, 128), f32)
    s_ckv = np.empty((2, 32, 16, 256), f32)
    s_kr = np.empty((2, 32, 16, 64), f32)
    for b, cidx in enumerate(PROMPT_CORES):
        r = R[cidx]
        y_prompt[b] = r["o_yT"].T
        p_conv[:, b] = np.transpose(r["o_pconv"], (0, 2, 1))
        p_rec[:, b] = r["o_prec"]
        p_ckv[:, b] = np.transpose(r["o_pckv"], (0, 2, 1))
        p_kr[:, b] = np.transpose(r["o_pkr"], (0, 2, 1))
    for c in range(8):
        r = R[c]
        sl = slice(4 * c, 4 * c + 4)
        y_sample[sl] = np.transpose(r["o_ysT"].reshape(D, 4, 32)[:, :, 16:], (1, 2, 0))
        s_conv[:, sl] = np.transpose(r["o_sconv"], (0, 1, 3, 2))
        s_rec[:, sl] = r["o_srec"]
        s_ckv[:, sl] = np.transpose(r["o_sckv"].reshape(2, 256, 4, 32)[..., 16:], (0, 2, 3, 1))
        s_kr[:, sl] = np.transpose(r["o_skr"].reshape(2, 64, 4, 32)[..., 16:], (0, 2, 3, 1))
    return (y_prompt, y_sample, p_conv, p_rec, p_ckv, p_kr, s_conv, s_rec, s_ckv, s_kr)
```
